# Optimizing a Trainium2 kernel written in Bass

```python
import jax, jax.numpy as jnp
from jax import lax
import numpy as np

D_MODEL = 1024
BATCH = 2
SEQ = 16384
DEPTH = 2
DEC_BATCH = 16
DEC_SEQ = 64
PAST_LEN = 2048

CHUNK = 64
HEAD_DIM = 64
N_HEADS_A = 8
N_HEADS_B = 8
N_KV_B = 2
GROUP_B = N_HEADS_B // N_KV_B
WIDTH_A = N_HEADS_A * HEAD_DIM
WIDTH_B = N_HEADS_B * HEAD_DIM
KV_WIDTH_B = N_KV_B * HEAD_DIM
MIX_WIDTH = WIDTH_A + WIDTH_B
IN_WIDTH = 3 * WIDTH_A + WIDTH_B + 2 * KV_WIDTH_B
PREV_CHUNKS_A = 8
A_ROWS = PREV_CHUNKS_A * CHUNK
REL_CLIP = 256
WINDOW_B = 128
PREV_CHUNKS_B = WINDOW_B // CHUNK
ROT_DIM = HEAD_DIM // 4
ROPE_THETA = 500000.0
D_FF = -(-8 * D_MODEL // (3 * 256)) * 256
D_PLE = 256
RMS_EPS = 1e-6
SCALE = HEAD_DIM ** -0.5

kernel_name = "hybrid_chunk_stream_encoder_step"


def rms_norm(x, g):
    xf = x.astype(jnp.float32)
    y = xf * lax.rsqrt(jnp.mean(xf * xf, axis=-1, keepdims=True) + RMS_EPS)
    return (y * g.astype(jnp.float32)).astype(x.dtype)


def partial_rope(x, pos):
    inv = ROPE_THETA ** (-jnp.arange(0, ROT_DIM, 2, dtype=jnp.float32) / ROT_DIM)
    ang = pos.astype(jnp.float32)[:, None] * inv[None, :]
    cos = jnp.cos(ang)[:, None, :]
    sin = jnp.sin(ang)[:, None, :]
    xr = x[..., :ROT_DIM].astype(jnp.float32)
    x1, x2 = xr[..., :ROT_DIM // 2], xr[..., ROT_DIM // 2:]
    rot = jnp.concatenate([x1 * cos - x2 * sin, x2 * cos + x1 * sin], axis=-1).astype(x.dtype)
    return jnp.concatenate([rot, x[..., ROT_DIM:]], axis=-1)


def chunk_band(x, n_prev):
    b, s = x.shape[0], x.shape[1]
    nc = s // CHUNK
    xc = x.reshape(b, nc, CHUNK, *x.shape[2:])
    xp = jnp.pad(xc, [(0, 0), (n_prev, 0)] + [(0, 0)] * (xc.ndim - 2))
    return jnp.concatenate([xp[:, j:j + nc] for j in range(n_prev + 1)], axis=2)


def band_valid(nc, n_prev):
    c = jnp.arange(nc)[:, None]
    j = jnp.arange((n_prev + 1) * CHUNK)[None, :]
    return (c - n_prev + j // CHUNK) >= 0


def rel_attention(q, k, v, dist, rel_bias, valid):
    s = jnp.einsum('...qhd,...khd->...hqk', q, k, preferred_element_type=jnp.float32) * SCALE
    s = s + rel_bias.astype(jnp.float32)[:, jnp.clip(dist, -REL_CLIP, REL_CLIP) + REL_CLIP]
    if valid is not None:
        s = jnp.where(valid, s, -jnp.inf)
    p = jax.nn.softmax(s, axis=-1).astype(v.dtype)
    return jnp.einsum('...hqk,...khd->...qhd', p, v)


def sink_gqa_attention(q, k, v, sinks, valid):
    qg = q.reshape(*q.shape[:-2], N_KV_B, GROUP_B, HEAD_DIM)
    s = jnp.einsum('...qhgd,...khd->...hgqk', qg, k, preferred_element_type=jnp.float32) * SCALE
    if valid is not None:
        s = jnp.where(valid, s, -jnp.inf)
    sk = sinks.astype(jnp.float32).reshape(N_KV_B, GROUP_B)[:, :, None, None]
    m = jnp.maximum(jnp.max(s, axis=-1, keepdims=True), sk)
    e = jnp.exp(s - m)
    p = (e / (jnp.sum(e, axis=-1, keepdims=True) + jnp.exp(sk - m))).astype(v.dtype)
    o = jnp.einsum('...hgqk,...khd->...qhgd', p, v)
    return o.reshape(*o.shape[:-3], N_HEADS_B, HEAD_DIM)


def mixer_inputs(h, g_norm, w_in):
    n = rms_norm(h, g_norm)
    z = n @ w_in
    b, s = h.shape[0], h.shape[1]
    o1 = WIDTH_A; o2 = 2 * WIDTH_A; o3 = 3 * WIDTH_A; o4 = o3 + WIDTH_B; o5 = o4 + KV_WIDTH_B
    qa = z[..., :o1].reshape(b, s, N_HEADS_A, HEAD_DIM)
    ka = z[..., o1:o2].reshape(b, s, N_HEADS_A, HEAD_DIM)
    va = z[..., o2:o3].reshape(b, s, N_HEADS_A, HEAD_DIM)
    qb = z[..., o3:o4].reshape(b, s, N_HEADS_B, HEAD_DIM)
    kb = z[..., o4:o5].reshape(b, s, N_KV_B, HEAD_DIM)
    vb = z[..., o5:].reshape(b, s, N_KV_B, HEAD_DIM)
    return qa, ka, va, qb, kb, vb


def layer_tail(h, oa, ob, p_i, g_out_a, g_out_b, w_out, g_ffn, w_gate_up, w_down, w_ple_proj, w_ple_gate):
    o = jnp.concatenate([rms_norm(oa, g_out_a), rms_norm(ob, g_out_b)], axis=-1)
    h = h + o @ w_out
    gu = rms_norm(h, g_ffn) @ w_gate_up
    h = h + (jax.nn.silu(gu[..., :D_FF]) * gu[..., D_FF:]) @ w_down
    return h + jax.nn.sigmoid(h @ w_ple_gate) * (p_i @ w_ple_proj)


def setup_inputs(seed: int = 0) -> dict:
    key = jax.random.key(seed)
    ks = jax.random.split(key, 24)
    f32 = jnp.float32
    la = min(A_ROWS, PAST_LEN)
    lb = min(WINDOW_B, PAST_LEN)
    nrm = lambda k, shape, sc: jax.random.normal(k, shape, f32) * sc
    return {
        "x_prompt": nrm(ks[0], (BATCH, SEQ, D_MODEL), 1.0),
        "x_sample": nrm(ks[1], (DEC_BATCH, DEC_SEQ, D_MODEL), 1.0),
        "p_prompt": nrm(ks[2], (DEPTH, BATCH, SEQ, D_PLE), 1.0),
        "p_sample": nrm(ks[3], (DEPTH, DEC_BATCH, DEC_SEQ, D_PLE), 1.0),
        "cache_a_k": nrm(ks[4], (DEPTH, DEC_BATCH, la, N_HEADS_A, HEAD_DIM), 1.0),
        "cache_a_v": nrm(ks[5], (DEPTH, DEC_BATCH, la, N_HEADS_A, HEAD_DIM), 1.0),
        "cache_b_k": nrm(ks[6], (DEPTH, DEC_BATCH, lb, N_KV_B, HEAD_DIM), 1.0),
        "cache_b_v": nrm(ks[7], (DEPTH, DEC_BATCH, lb, N_KV_B, HEAD_DIM), 1.0),
        "g_mix_norm": 1.0 + nrm(ks[8], (DEPTH, D_MODEL), 0.05),
        "w_in": nrm(ks[9], (DEPTH, D_MODEL, IN_WIDTH), D_MODEL ** -0.5),
        "rel_bias_a": nrm(ks[10], (DEPTH, N_HEADS_A, 2 * REL_CLIP + 1), 0.2),
        "sinks_b": nrm(ks[11], (DEPTH, N_HEADS_B), 0.5),
        "g_out_a": 1.0 + nrm(ks[12], (DEPTH, WIDTH_A), 0.05),
        "g_out_b": 1.0 + nrm(ks[13], (DEPTH, WIDTH_B), 0.05),
        "w_out": nrm(ks[14], (DEPTH, MIX_WIDTH, D_MODEL), MIX_WIDTH ** -0.5),
        "g_ffn_norm": 1.0 + nrm(ks[15], (DEPTH, D_MODEL), 0.05),
        "w_gate_up": nrm(ks[16], (DEPTH, D_MODEL, 2 * D_FF), D_MODEL ** -0.5),
        "w_down": nrm(ks[17], (DEPTH, D_FF, D_MODEL), D_FF ** -0.5),
        "w_ple_proj": nrm(ks[18], (DEPTH, D_PLE, D_MODEL), D_PLE ** -0.5),
        "w_ple_gate": nrm(ks[19], (DEPTH, D_MODEL, D_MODEL), D_MODEL ** -0.5),
        "g_final": 1.0 + nrm(ks[20], (D_MODEL,), 0.05),
    }


def reference(x_prompt, x_sample, p_prompt, p_sample, cache_a_k, cache_a_v, cache_b_k, cache_b_v,
              g_mix_norm, w_in, rel_bias_a, sinks_b, g_out_a, g_out_b, w_out, g_ffn_norm,
              w_gate_up, w_down, w_ple_proj, w_ple_gate, g_final):
    b_p, s_p = x_prompt.shape[0], x_prompt.shape[1]
    b_s, t_s = x_sample.shape[0], x_sample.shape[1]
    nc = s_p // CHUNK
    la_c = cache_a_k.shape[2]
    lb_c = cache_b_k.shape[2]
    keep_a_p = min(A_ROWS, s_p)
    keep_b_p = min(WINDOW_B, s_p)
    keep_a_s = min(A_ROWS, la_c + t_s)
    keep_b_s = min(WINDOW_B, lb_c + t_s)

    pos_p = jnp.arange(s_p)
    la_band = (PREV_CHUNKS_A + 1) * CHUNK
    dist_p = PREV_CHUNKS_A * CHUNK + jnp.arange(CHUNK)[:, None] - jnp.arange(la_band)[None, :]
    valid_a = band_valid(nc, PREV_CHUNKS_A)[:, None, None, :]
    valid_b = band_valid(nc, PREV_CHUNKS_B)[:, None, None, None, :]
    qpos_s = PAST_LEN + jnp.arange(t_s)
    kpos_s = PAST_LEN - la_c + jnp.arange(la_c + t_s)
    dist_s = qpos_s[:, None] - kpos_s[None, :]

    hp, hs = x_prompt, x_sample
    nak_p, nav_p, nbk_p, nbv_p = [], [], [], []
    nak_s, nav_s, nbk_s, nbv_s = [], [], [], []
    for i in range(DEPTH):
        tail = (g_out_a[i], g_out_b[i], w_out[i], g_ffn_norm[i], w_gate_up[i], w_down[i],
                w_ple_proj[i], w_ple_gate[i])
        qa, ka, va, qb, kb, vb = mixer_inputs(hp, g_mix_norm[i], w_in[i])
        qb = partial_rope(qb, pos_p)
        kb = partial_rope(kb, pos_p)
        oa = rel_attention(qa.reshape(b_p, nc, CHUNK, N_HEADS_A, HEAD_DIM),
                           chunk_band(ka, PREV_CHUNKS_A), chunk_band(va, PREV_CHUNKS_A),
                           dist_p, rel_bias_a[i], valid_a).reshape(b_p, s_p, WIDTH_A)
        ob = sink_gqa_attention(qb.reshape(b_p, nc, CHUNK, N_HEADS_B, HEAD_DIM),
                                chunk_band(kb, PREV_CHUNKS_B), chunk_band(vb, PREV_CHUNKS_B),
                                sinks_b[i], valid_b).reshape(b_p, s_p, WIDTH_B)
        hp = layer_tail(hp, oa, ob, p_prompt[i], *tail)
        nak_p.append(ka[:, s_p - keep_a_p:])
        nav_p.append(va[:, s_p - keep_a_p:])
        nbk_p.append(kb[:, s_p - keep_b_p:])
        nbv_p.append(vb[:, s_p - keep_b_p:])

        qa, ka, va, qb, kb, vb = mixer_inputs(hs, g_mix_norm[i], w_in[i])
        ka_all = jnp.concatenate([cache_a_k[i], ka], axis=1)
        va_all = jnp.concatenate([cache_a_v[i], va], axis=1)
        oa = rel_attention(qa, ka_all, va_all, dist_s, rel_bias_a[i], None).reshape(b_s, t_s, WIDTH_A)
        qb = partial_rope(qb, qpos_s)
        kb = partial_rope(kb, qpos_s)
        kb_all = jnp.concatenate([cache_b_k[i], kb], axis=1)
        vb_all = jnp.concatenate([cache_b_v[i], vb], axis=1)
        ob = sink_gqa_attention(qb, kb_all, vb_all, sinks_b[i], None).reshape(b_s, t_s, WIDTH_B)
        hs = layer_tail(hs, oa, ob, p_sample[i], *tail)
        nak_s.append(ka_all[:, la_c + t_s - keep_a_s:])
        nav_s.append(va_all[:, la_c + t_s - keep_a_s:])
        nbk_s.append(kb_all[:, lb_c + t_s - keep_b_s:])
        nbv_s.append(vb_all[:, lb_c + t_s - keep_b_s:])

    y_prompt = rms_norm(hp, g_final)
    y_sample = rms_norm(hs, g_final)
    return (y_prompt, y_sample,
            jnp.stack(nak_p), jnp.stack(nav_p), jnp.stack(nbk_p), jnp.stack(nbv_p),
            jnp.stack(nak_s), jnp.stack(nav_s), jnp.stack(nbk_s), jnp.stack(nbv_s))
```

```python
import numpy as np
import concourse.bass as bass
import concourse.mybir as mybir
from concourse.bass_utils import run_bass_kernel_spmd

F32 = mybir.dt.float32
BF16 = mybir.dt.bfloat16
AF = mybir.ActivationFunctionType
ALU = mybir.AluOpType

NCORES = 8
D = 1024
DFF = 2816
NL = 2
TT = 512
OWN = 4096
HALO = 1024
NROW = HALO + OWN + 128
SROW = HALO + OWN
PAST = 2048
EPS = 1e-6
SCALE = 0.125
MASKV = -60.0
NWB = 3
SLOT = 4096


def _slab(W, cols):
    K = W.shape[0]
    kc = K // 128
    sub = W[:, cols]
    return np.ascontiguousarray(sub.reshape(kc, 128, len(cols)).transpose(1, 0, 2).reshape(128, kc * len(cols)))


def slab_table():
    tab = {}
    off = 0
    for l in range(NL):
        for nm, sz in ([("qa", 4096), ("ka", 4096), ("va", 4096), ("qb", 4096), ("kv", 2048), ("o0", 4096), ("o1", 4096)]
                       + [(f"gu{i}", 4096) for i in range(11)] + [(f"dn{m}", 2816) for m in range(8)]
                       + [("pg0", 4096), ("pg1", 4096), ("pp", 2048)]):
            tab[(l, nm)] = (off, sz)
            off += sz
    return tab, off


SLABS, WTOT = slab_table()


def build_weights(w_in, w_out, w_gate_up, w_down, w_ple_proj, w_ple_gate):
    out = np.empty((128, WTOT), np.float32)
    ar = np.arange
    for l in range(NL):
        parts = {
            "qa": _slab(w_in[l], ar(0, 512)), "ka": _slab(w_in[l], ar(512, 1024)), "va": _slab(w_in[l], ar(1024, 1536)),
            "qb": _slab(w_in[l], ar(1536, 2048)), "kv": _slab(w_in[l], ar(2048, 2304)),
            "o0": _slab(w_out[l], ar(0, 512)), "o1": _slab(w_out[l], ar(512, 1024)),
            "pg0": _slab(w_ple_gate[l], ar(0, 512)), "pg1": _slab(w_ple_gate[l], ar(512, 1024)),
            "pp": _slab(w_ple_proj[l], ar(0, 1024)),
        }
        for i in range(11):
            parts[f"gu{i}"] = _slab(w_gate_up[l], np.concatenate([ar(256 * i, 256 * i + 256), ar(DFF + 256 * i, DFF + 256 * i + 256)]))
        for m in range(8):
            parts[f"dn{m}"] = _slab(w_down[l], ar(128 * m, 128 * m + 128))
        for nm, a in parts.items():
            o, sz = SLABS[(l, nm)]
            assert a.shape[1] == sz
            out[:, o:o + sz] = a
    return out


def gidx(kind, l, c):
    base = {"mix": 0, "ffn": 16, "oa": 32, "ob": 40, "fin": 48}[kind]
    n = {"mix": 8, "ffn": 8, "oa": 4, "ob": 4, "fin": 8}[kind]
    return base + l * n + c


NG = 56


class Prog:
    ENG = ("pe", "act", "dve", "pool", "sp")

    def __init__(self, ndma=None):
        self.st = {e: [] for e in self.ENG}
        self.lastw = {}
        self.readers = {}
        self.seen = {e: {} for e in self.ENG}
        self.ndma = ndma or {"sp": 10, "act": 4, "pool": 4}
        self.dma_cnt = {}
        self.dma_rr = {q: 0 for q in self.ndma}
        self.milestones = {e: set() for e in self.ENG}
        self.label = "setup"
        self.labels = {e: [] for e in self.ENG}

    def _deps(self, eng, R, W):
        deps = []
        for r in R:
            t = self.lastw.get(r)
            if t is not None:
                deps.append(t)
        for w in W:
            t = self.lastw.get(w)
            if t is not None:
                deps.append(t)
            for k, v in self.readers.get(w, {}).items():
                deps.append((k, v))
        waits = []
        seen = self.seen[eng]
        best = {}
        for k, v in deps:
            if k == eng and eng == "pe":
                continue
            if seen.get(k, -1) >= v:
                continue
            if best.get(k, -1) < v:
                best[k] = v
        for k, v in best.items():
            seen[k] = v
            waits.append((k, v))
            if not isinstance(k, tuple):
                self.milestones[k].add(v)
        return waits

    def _mark(self, tok, R, W):
        k, v = tok
        for r in R:
            d = self.readers.setdefault(r, {})
            if d.get(k, -1) < v:
                d[k] = v
        for w in W:
            self.lastw[w] = tok
            self.readers[w] = {}

    def op(self, eng, fn, R=(), W=()):
        W = list(W) + [r for r in R if r[0] == "ps" and r not in W]
        waits = self._deps(eng, R, W)
        idx = len(self.st[eng])
        self.labels[eng].append(self.label)
        self.st[eng].append(("op", fn, waits, None))
        self._mark((eng, idx), R, W)

    def dma(self, q, out_ap, in_ap, R=(), W=()):
        s = self.dma_rr[q]
        self.dma_rr[q] = (s + 1) % self.ndma[q]
        key = ("dma", q, s)
        cnt = self.dma_cnt.get(key, 0)
        waits = self._deps(q, R, W)
        if cnt > 0 and self.seen[q].get(key, -1) < 16 * cnt:
            self.seen[q][key] = 16 * cnt
            waits.append((key, 16 * cnt))
        self.dma_cnt[key] = cnt + 1
        self.labels[q].append(self.label)
        self.st[q].append(("dma", (out_ap, in_ap), waits, key))
        self._mark((key, 16 * (cnt + 1)), R, W)

    def emit(self, nc, block, sems):
        rank = {}
        for e in self.ENG:
            ms = sorted(self.milestones[e])
            rank[e] = {v: i + 1 for i, v in enumerate(ms)}
        engobj = {"pe": "tensor", "act": "scalar", "dve": "vector", "pool": "gpsimd", "sp": "sync"}

        def run(e, eng):
            for idx, (kind, payload, waits, key) in enumerate(self.st[e]):
                for k, v in waits:
                    if isinstance(k, tuple):
                        eng.wait_ge(sems[k], v)
                    else:
                        eng.wait_ge(sems[k], rank[k][v])
                if kind == "op":
                    ins = payload(eng)
                    if idx in rank[e]:
                        ins.then_inc(sems[e], 1)
                else:
                    o, i = payload
                    eng.dma_start(out=o, in_=i).then_inc(sems[key], 16)
            for key, cnt in self.dma_cnt.items():
                if key[1] == e:
                    eng.wait_ge(sems[key], 16 * cnt)

        for e in self.ENG:
            deco = getattr(block, engobj[e])

            def body(eng, e=e):
                run(e, eng)
            deco(body)


def build_program():
    nc = bass.Bass("TRN2", target_bir_lowering=False)
    dt_in = lambda n, s: nc.dram_tensor(n, s, F32, kind="ExternalInput")
    dt_out = lambda n, s: nc.dram_tensor(n, s, F32, kind="ExternalOutput")
    xin = dt_in("xin", [NROW, D])
    pin = dt_in("pin", [NL, NROW, 256])
    cak = dt_in("cak", [NL, 2, 512, 512])
    cav = dt_in("cav", [NL, 2, 512, 512])
    cbk = dt_in("cbk", [NL, 2, 128, 128])
    cbv = dt_in("cbv", [NL, 2, 128, 128])
    wf = dt_in("wf", [128, WTOT])
    gcol_d = dt_in("gcol", [128, NG])
    sink_d = dt_in("sinkrow", [1, 16])
    relc = dt_in("relc", [NL * 8, 768])
    NBLK = NROW // 128
    rope_d = dt_in("rope", [128, NBLK * 32])
    kmask_d = dt_in("kmask", [128, 4])

    wbf = nc.dram_tensor("wbf", [128, WTOT], BF16, kind="Internal")
    srel = nc.dram_tensor("srel", [NL * 8, 128 * 768], F32, kind="Internal")
    ebf = nc.dram_tensor("ebf", [NL, 128, 5120], BF16, kind="Internal")

    y_d = dt_out("y", [OWN + 128, D])
    nakp = dt_out("nakp", [NL, 512, 512])
    navp = dt_out("navp", [NL, 512, 512])
    nbkp = dt_out("nbkp", [NL, 128, 128])
    nbvp = dt_out("nbvp", [NL, 128, 128])
    naks = dt_out("naks", [NL, 2, 512, 512])
    navs = dt_out("navs", [NL, 2, 512, 512])
    nbks = dt_out("nbks", [NL, 2, 128, 128])
    nbvs = dt_out("nbvs", [NL, 2, 128, 128])

    from contextlib import ExitStack
    es = ExitStack()

    def sb(name, F, dt):
        return es.enter_context(nc.sbuf_tensor("sb_" + name, [128, F], dt))

    hT = sb("hT", 8 * 512, F32)
    nT = sb("nT", 8 * 512, BF16)
    sq = sb("sq", 2 * 512, BF16)
    xs = sb("xs", 2 * 1024, F32)
    ost = sb("ost", 2 * 512, F32)
    scr = sb("scr", 12288, BF16)
    scr32 = scr.bitcast(F32)
    kTA = [sb(f"kTA{l}", 2 * 4 * 512, BF16) for l in range(NL)]
    vA = [sb(f"vA{l}", 2 * 4 * 768, BF16) for l in range(NL)]
    kTB = [sb(f"kTB{l}", 2 * 2 * 512, BF16) for l in range(NL)]
    vB = [sb(f"vB{l}", 2 * 4 * 384, BF16) for l in range(NL)]
    qk32 = sb("qk32", 16, F32)
    rtmp = sb("rtmp", 2 * 4 * 80, F32)
    qkbf = sb("qkbf", 2 * 768, BF16)
    ex = sb("ex", 6 * 512, BF16)
    pTt = sb("pTt", 6 * 512, BF16)
    Et = sb("Et", 8 * 640, BF16)
    EB = sb("EB", 256, BF16)
    wb = sb("wb", NWB * SLOT, BF16)
    pst = sb("pst", 2 * 1024, F32)
    ppT = sb("ppT", 2 * 512, BF16)
    rs = sb("rs", 2 * 512, F32)
    rc = sb("rc", 2 * 512, F32)
    sgb = sb("sgb", 2 * 512, F32)
    id32 = sb("id32", 128, F32)
    idb = sb("idb", 128, BF16)
    onesb = sb("onesb", 128, BF16)
    gcol = sb("gcol", NG, F32)
    sinkf = sb("sinkf", 16, F32)
    esrow = sb("esrow", 16, BF16)
    sel = sb("sel", 256, BF16)
    rope = sb("rope", NBLK * 32, F32)
    kmask = sb("kmask", 4, F32)
    epsb = sb("epsb", 1, F32)
    psT = [es.enter_context(nc.psum_tensor(f"ps{i}", [128, 1024], F32)) for i in range(4)]
    ps = [psT[i // 2][:, (i % 2) * 512:(i % 2 + 1) * 512] for i in range(8)]
    ps7b = psT[3].bitcast(BF16)[:, 1024:2048]

    P = Prog()

    def V(t, off, dims, p0=0, npart=128):
        if isinstance(t, bass.AP):
            base = t.offset
            t = t.tensor
        else:
            base = 0
        Fd = t.shape[1]
        return bass.AP(t, base + p0 * Fd + off, [[Fd, npart]] + [[s, n] for s, n in dims])

    def mm(out, lhsT, rhs, start, stop, R, W, tp=None, skip=False):
        if skip:
            P.op("pe", lambda e: e.matmul(out, lhsT=lhsT, rhs=rhs, start=start, stop=stop, skip_group_check=True), R, W)
        elif tp is None:
            P.op("pe", lambda e: e.matmul(out, lhsT=lhsT, rhs=rhs, start=start, stop=stop), R, W)
        else:
            P.op("pe", lambda e: e.matmul(out, lhsT=lhsT, rhs=rhs, start=start, stop=stop, tile_position=tp), R, W)

    def tr(out, in_, ident, R, W):
        P.op("pe", lambda e: e.transpose(out, in_, ident), R, W)

    def act(out, in_, func, R, W, bias=None, scale=None):
        kw = {}
        if bias is not None:
            kw["bias"] = bias
        if scale is not None:
            kw["scale"] = scale
        P.op("act", lambda e: e.activation(out=out, in_=in_, func=func, **kw), R, W)

    def cp(eng, out, in_, R, W):
        if eng == "act":
            P.op("act", lambda e: e.activation(out=out, in_=in_, func=AF.Copy), R, W)
        else:
            P.op(eng, lambda e: e.tensor_copy(out=out, in_=in_), R, W)

    def tt(eng, out, in0, in1, op, R, W):
        P.op(eng, lambda e: e.tensor_tensor(out=out, in0=in0, in1=in1, op=op), R, W)

    def stt(eng, out, in0, scalar, in1, op0, op1, R, W):
        P.op(eng, lambda e: e.scalar_tensor_tensor(out=out, in0=in0, scalar=scalar, in1=in1, op0=op0, op1=op1), R, W)

    def recip(out, in_, R, W):
        P.op("dve", lambda e: e.reciprocal(out=out, in_=in_), R, W)

    def memset(eng, ap, val, W):
        P.op(eng, lambda e: e.memset(ap, val), (), W)

    def hTc(c, n0, n):
        return hT[:, c * 512 + n0: c * 512 + n0 + n]

    def nTc(c, n0, n):
        return nT[:, c * 512 + n0: c * 512 + n0 + n]

    def qaT(p, rows, n0, n):
        return scr[rows, p * 512 + n0: p * 512 + n0 + n]

    def qbT(p, rows, n0, n):
        return scr[rows, (4 + p) * 512 + n0: (4 + p) * 512 + n0 + n]

    def oT(s, rows, n0, n):
        return scr32[rows, 2048 + s * 512 + n0: 2048 + s * 512 + n0 + n]

    def oTres(s):
        return [("scr", 8 + 2 * s), ("scr", 9 + 2 * s)]

    def actT(f, n):
        return scr[:, f * 512: f * 512 + n]

    ALLR = slice(0, 128)
    psctr = [0]

    MMB = (0, 1, 2, 4, 5, 6)

    def mmbank():
        b = MMB[psctr[0] % len(MMB)]
        psctr[0] += 1
        return b

    evctr = [0]

    def evac_eng():
        evctr[0] += 1
        return "act" if evctr[0] % 2 else "dve"

    P.op("pool", lambda e: e.iota(id32[:], [[1, 128]], base=0, channel_multiplier=-1, allow_small_or_imprecise_dtypes=True), (), [("id32",)])
    P.op("pool", lambda e: e.tensor_single_scalar(out=id32[:], in_=id32[:], scalar=0.0, op=ALU.is_equal), [("id32",)], [("id32",)])
    cp("pool", idb[:], id32[:], [("id32",)], [("idb",)])
    memset("pool", onesb[:], 1.0, [("onesb",)])
    memset("pool", epsb[:], EPS, [("epsb",)])
    memset("pool", sel[:], 0.0, [("sel",)])
    memset("pool", sel[0:1, 64:128], 1.0, [("sel",)])
    memset("pool", sel[0:1, 128:192], 1.0, [("sel",)])
    memset("pool", EB[:], 1.0, [("EB",)])
    memset("pool", EB[64:128, 0:64], 0.0, [("EB",)])
    memset("pool", EB[0:64, 192:256], 0.0, [("EB",)])
    for l in range(NL):
        for hf in range(2):
            for blk in range(4):
                for p in range(4):
                    o = (hf * 4 + blk) * 768 + p * 192 + 64
                    memset("pool", vA[l][:, o:o + 64], 1.0, [("vA", l, hf, blk)])
                for g in range(2):
                    o = (hf * 4 + blk) * 384 + g * 192 + 64
                    memset("pool", vB[l][:, o:o + 64], 1.0, [("vB", l, hf, blk)])
    P.dma("sp", gcol[:], gcol_d.ap(), (), [("gcol",)])
    P.dma("sp", rope[:], rope_d.ap(), (), [("rope",)])
    P.dma("sp", kmask[:], kmask_d.ap(), (), [("kmask",)])
    P.dma("sp", sinkf[0:1, :], sink_d.ap(), (), [("sinkf",)])
    act(esrow[0:1, :], sinkf[0:1, :], AF.Exp, [("sinkf",)], [("esrow",)])
    for l in range(NL):
        for s in range(2):
            P.dma("sp", naks.ap()[l, s, 0:448, :], cak.ap()[l, s, 64:512, :], (), [("o_naks", l, s)])
            P.dma("sp", navs.ap()[l, s, 0:448, :], cav.ap()[l, s, 64:512, :], (), [("o_navs", l, s)])
            P.dma("sp", nbks.ap()[l, s, 0:64, :], cbk.ap()[l, s, 64:128, :], (), [("o_nbks", l, s)])
            P.dma("sp", nbvs.ap()[l, s, 0:64, :], cbv.ap()[l, s, 64:128, :], (), [("o_nbvs", l, s)])
    def build_E():
        for i in range(NL * 8):
            P.dma("sp", srel.ap()[i:i + 1, :].rearrange("a (r c) -> (a r) c", c=768),
                  bass.AP(relc, i * 768, [[0, 128], [1, 768]]), (), [("srel", i)])
        for l in range(NL):
            for h in range(8):
                i = l * 8 + h
                sl = i % 2
                P.dma("sp", xs[:, sl * 1024: sl * 1024 + 640], bass.AP(srel, i * 128 * 768, [[767, 128], [1, 640]]),
                      [("srel", i)], [("xs", sl)])
                act(Et[:, h * 640:(h + 1) * 640], xs[:, sl * 1024: sl * 1024 + 640], AF.Exp, [("xs", sl)], [("Et",)])
                memset("pool", Et[64:128, h * 640: h * 640 + 64], 0.0, [("Et",)])
                memset("pool", Et[0:64, h * 640 + 576: h * 640 + 640], 0.0, [("Et",)])
            P.dma("sp", ebf.ap()[l], Et[:], [("Et",)], [("ebf", l)])

    first_order = ["ka", "va", "kv", "qa", "qb", "o0", "o1"] + [f"gu{i}" for i in range(11)] + [f"dn{m}" for m in range(8)] \
        + ["pp", "pg0", "pg1"]
    cast_queue = [(l, nm) for l in range(NL) for nm in first_order]
    cast_done = set()
    CAST_AHEAD = 6

    def issue_casts(n):
        for _ in range(n):
            if not cast_queue:
                return
            l, nm = cast_queue.pop(0)
            if (l, nm) in cast_done:
                continue
            o, sz = SLABS[(l, nm)]
            P.dma("pool", wbf.ap()[:, o:o + sz], wf.ap()[:, o:o + sz], (), [("wbf", l, nm)])
            cast_done.add((l, nm))

    def ensure_cast(l, nm):
        if (l, nm) not in cast_done:
            cast_queue.remove((l, nm))
            o, sz = SLABS[(l, nm)]
            P.dma("pool", wbf.ap()[:, o:o + sz], wf.ap()[:, o:o + sz], (), [("wbf", l, nm)])
            cast_done.add((l, nm))

    issue_casts(CAST_AHEAD)

    wslot = [0]

    def load_slab(l, nm):
        ensure_cast(l, nm)
        issue_casts(1)
        o, sz = SLABS[(l, nm)]
        s = wslot[0] % NWB
        wslot[0] += 1
        P.dma("sp", wb[:, s * SLOT: s * SLOT + sz], wbf.ap()[:, o:o + sz], [("wbf", l, nm)], [("wb", s)])
        return s

    def wslab(s, kc, ncols, c0, n):
        o = s * SLOT + kc * ncols + c0
        return wb[:, o:o + n]

    rsctr = [0]

    class Stats:
        def __init__(self, nsrc, N):
            self.n = nsrc
            self.N = N
            self.i = 0
            self.pend = None

        def add(self, ap, rl):
            i = self.i
            self.i += 1
            sl = i % 2
            N = self.N
            tt("pool", sq[:, sl * 512: sl * 512 + N], ap, ap, ALU.mult, rl, [("sq", sl)])
            self.flush()
            self.pend = (i, sl)

        def flush(self):
            if self.pend is not None:
                i, sl = self.pend
                N = self.N
                mm(ps[3][:, 0:N], onesb[:], sq[:, sl * 512: sl * 512 + N], i == 0, i == self.n - 1,
                   [("sq", sl), ("onesb",)], [("ps", 3)])
                self.pend = None

        def finish(self, Dn):
            self.flush()
            assert self.i == self.n
            N = self.N
            k = rsctr[0] % 2
            rsctr[0] += 1
            act(rs[:, k * 512: k * 512 + N], ps[3][:, 0:N], AF.Ln, [("ps", 3)], [("rs", k)], bias=epsb[:, 0:1], scale=1.0 / Dn)
            act(rs[:, k * 512: k * 512 + N], rs[:, k * 512: k * 512 + N], AF.Exp, [("rs", k)], [("rs", k)], scale=-0.5)
            return k

    def rms_stats(srcs, N, Dn):
        st = Stats(len(srcs), N)
        for ap, rl in srcs:
            st.add(ap, rl)
        return st.finish(Dn)

    def norm_to_nT(l, kind, N, st=None):
        if st is None:
            k = rms_stats([(hTc(c, 0, N), [("hT", c)]) for c in range(8)], N, D)
        else:
            k = st.finish(D)
        for c in range(8):
            gi = gidx(kind, l, c)
            stt("dve", nTc(c, 0, N), hTc(c, 0, N), gcol[:, gi:gi + 1], rs[:, k * 512: k * 512 + N], ALU.mult, ALU.mult,
                [("hT", c), ("rs", k), ("gcol",)], [("nT", c)])

    exctr = [0]
    sbctr = [0]

    def run_units(units):
        SK = 2
        n = len(units)
        for idx in range(n + SK):
            if idx < n:
                u = units[idx]
                if "call" in u:
                    u["call"]()
                else:
                    u["S"]()
            j = idx - SK
            if j >= 0:
                uj = units[j]
                if "call" not in uj:
                    uj["pv"]()
                    if uj.get("fin"):
                        uj["fin"]()
            if idx < n:
                u = units[idx]
                if "call" not in u:
                    u["post"]()

    def make_unit(kT_ap_fn, kres, q_ap_fn, qres, Nj, mcol, e_ap_fn, eres, pv_fn, fin=None):
        sset = sbctr[0] % 3
        sbctr[0] += 1
        banks = (ps[2 + 2 * sset], ps[3 + 2 * sset])
        bres = [("ps", 2 + 2 * sset), ("ps", 3 + 2 * sset)]
        s2 = exctr[0] % 3
        exctr[0] += 1

        def S():
            for hh in range(2):
                rows = slice(64 * hh, 64 * hh + 64)
                mm(banks[hh][:, 0:Nj], kT_ap_fn(rows), q_ap_fn(rows), True, True, [kres, qres], [bres[hh]],
                   tp=(64 * hh, 0))

        def post():
            exv = V(ex, s2 * 1024, [(512, 2), (1, Nj)])
            pv_ = V(pTt, s2 * 1024, [(512, 2), (1, Nj)])
            pres = [("pT", 2 * s2), ("pT", 2 * s2 + 1)]
            if isinstance(eres, tuple) and eres[0] == "EBmask":
                eoff = eres[1]
                act(pv_, V(psT[1 + sset], 0, [(512, 2), (1, Nj)]), AF.Exp, bres + [("kmask",)], pres,
                    bias=kmask[:, mcol:mcol + 1], scale=SCALE)
                if eoff == 0:
                    memset("pool", V(pTt, s2 * 1024, [(512, 2), (1, 64)], p0=64, npart=64), 0.0, pres)
                if eoff + Nj == 256:
                    memset("pool", V(pTt, s2 * 1024 + Nj - 64, [(512, 2), (1, 64)], p0=0, npart=64), 0.0, pres)
                return
            act(exv, V(psT[1 + sset], 0, [(512, 2), (1, Nj)]), AF.Exp, bres + [("kmask",)],
                [("ex", 2 * s2), ("ex", 2 * s2 + 1)], bias=kmask[:, mcol:mcol + 1], scale=SCALE)
            tt("dve", pv_, exv, e_ap_fn(), ALU.mult, [("ex", 2 * s2), ("ex", 2 * s2 + 1), eres], pres)

        def pv():
            for hh in range(2):
                sl = 2 * s2 + hh
                pv_fn(hh, pTt[:, sl * 512: sl * 512 + Nj], ("pT", sl))

        return dict(S=S, post=post, pv=pv, fin=fin)

    fctr = [0]

    def finalize_pair(obanks, obres, slot, n0, n, Rextra=()):
        k = fctr[0] % 2
        fctr[0] += 1
        for hh in range(2):
            num = slice(64 * hh, 64 * hh + 64)
            den = slice(64 * (1 - hh), 64 * (1 - hh) + 64)
            act(rc[num, k * 512 + n0: k * 512 + n0 + n], obanks[hh][den, n0:n0 + n], AF.Ln, [obres[hh]], [("rc", k)])
        rcall = rc[:, k * 512 + n0: k * 512 + n0 + n]
        act(rcall, rcall, AF.Exp, [("rc", k)], [("rc", k)], scale=-1.0)
        for hh in range(2):
            num = slice(64 * hh, 64 * hh + 64)
            tt("dve", oT(slot, num, n0, n), obanks[hh][num, n0:n0 + n], rc[num, k * 512 + n0: k * 512 + n0 + n], ALU.mult,
               [obres[hh], ("rc", k)], oTres(slot))

    def vA_lhs(l, hf, blk, p, hh):
        o = (hf * 4 + blk) * 768 + p * 192 + 64 * hh
        return vA[l][:, o:o + 128]

    def vB_lhs(l, hf, blk, g, hh):
        o = (hf * 4 + blk) * 384 + g * 192 + 64 * hh
        return vB[l][:, o:o + 128]

    def sink_mm(l, h, hh, bank, bres, n0, n):
        mm(bank[:, n0:n0 + n], sel[0:1, hh * 128: hh * 128 + 128], V(esrow, l * 8 + h, [(0, n)], 0, 1), False, True,
           [("sel",), ("esrow",)], [bres], skip=True)

    def attn_prompt(l, hc, halo_prev, halo_cur):
        hp = 1 - hc
        units = []
        for p in range(4):
            ob = (ps[0], ps[1])
            obres = [("ps", 0), ("ps", 1)]
            steps = [-4, 3, -3, 2, -2, 1, -1, 0]
            for si, j in enumerate(steps):
                hf = hp if j < 0 else hc
                blk = j + 4 if j < 0 else j
                qb0 = max(j, 0)
                qb1 = min(j + 4, 3)
                Nj = (qb1 - qb0 + 1) * 128
                qs = qb0 * 128
                eoff = (qb0 - j) * 128
                mcol = 1 if (halo_prev if j < 0 else halo_cur) else 0

                def kf(rows, l=l, hf=hf, p=p, blk=blk):
                    return kTA[l][rows, (hf * 4 + p) * 512 + blk * 128: (hf * 4 + p) * 512 + blk * 128 + 128]

                def qf(rows, p=p, qs=qs, Nj=Nj):
                    return qaT(p, rows, qs, Nj)

                def ef(p=p, eoff=eoff, Nj=Nj):
                    return V(Et, 2 * p * 640 + eoff, [(640, 2), (1, Nj)])

                def pvf(hh, pap, pres, l=l, hf=hf, blk=blk, p=p, qs=qs, Nj=Nj, si=si, ob=ob, obres=obres):
                    mm(ob[hh][:, qs:qs + Nj], vA_lhs(l, hf, blk, p, hh), pap, si == 0, True,
                       [pres, ("vA", l, hf, blk)], [obres[hh]], skip=(si != 0))

                fin = None
                if si == 7:
                    def fin(ob=ob, obres=obres, p=p):
                        finalize_pair(ob, obres, p, 0, 512)
                units.append(make_unit(kf, ("kTA", l, hf, p), qf, ("scr", p), Nj, mcol, ef, ("Et",), pvf, fin))
        for p in range(4):
            g = p // 2
            ob = (ps[0], ps[1])
            obres = [("ps", 0), ("ps", 1)]
            for j in range(-1, 4):
                hf = hp if j < 0 else hc
                blk = j + 4 if j < 0 else j
                qb0 = max(j, 0)
                qb1 = min(j + 1, 3)
                Nj = (qb1 - qb0 + 1) * 128
                qs = qb0 * 128
                eoff = (qb0 - j) * 128
                mcol = 1 if (halo_prev if j < 0 else halo_cur) else 0

                def kf(rows, l=l, hf=hf, g=g, blk=blk):
                    o = (hf * 2 + g) * 512 + blk * 128
                    return kTB[l][rows, o:o + 128]

                def qf(rows, p=p, qs=qs, Nj=Nj):
                    return qbT(p, rows, qs, Nj)

                def ef(eoff=eoff, Nj=Nj):
                    return V(EB, eoff, [(0, 2), (1, Nj)])

                def pvf(hh, pap, pres, l=l, hf=hf, blk=blk, g=g, j=j, qb0=qb0, qb1=qb1, ob=ob, obres=obres, p=p):
                    for qb in range(qb0, qb1 + 1):
                        sub = pap[:, (qb - qb0) * 128:(qb - qb0) * 128 + 128]
                        first = (j == -1)
                        mm(ob[hh][:, qb * 128: qb * 128 + 128], vB_lhs(l, hf, blk, g, hh), sub, first, True,
                           [pres, ("vB", l, hf, blk)], [obres[hh]], skip=not first)

                fin = None
                if j == 3:
                    def fin(ob=ob, obres=obres, p=p, l=l):
                        for hh in range(2):
                            sink_mm(l, 2 * p + hh, hh, ob[hh], obres[hh], 0, 512)
                        finalize_pair(ob, obres, 4 + p, 0, 512)
                units.append(make_unit(kf, ("kTB", l, hf), qf, ("scr", 4 + p), Nj, mcol, ef, ("EBmask", eoff), pvf, fin))
                if p == 0 and j == 2:
                    units.append(dict(call=lambda l=l: out_norm(l, 512, 0)))
        run_units(units)

    def attn_sample(l, s):
        units = []
        n0 = s * 64
        for p in range(4):
            ob = (ps[0], ps[1])
            obres = [("ps", 0), ("ps", 1)]
            for m in range(5):
                hf = 0 if m < 4 else 1
                blk = m if m < 4 else 0
                eoff = (512 - 128 * m) if m < 4 else 64 * s
                mcol = 2 if (m == 4 and s == 1) else 0

                def kf(rows, l=l, hf=hf, p=p, blk=blk):
                    o = (hf * 4 + p) * 512 + blk * 128
                    return kTA[l][rows, o:o + 128]

                def qf(rows, p=p, n0=n0):
                    return qaT(p, rows, n0, 64)

                def ef(p=p, eoff=eoff):
                    return V(Et, 2 * p * 640 + eoff, [(640, 2), (1, 64)])

                def pvf(hh, pap, pres, l=l, hf=hf, blk=blk, p=p, m=m, ob=ob, obres=obres, n0=n0):
                    mm(ob[hh][:, n0:n0 + 64], vA_lhs(l, hf, blk, p, hh), pap, m == 0, True,
                       [pres, ("vA", l, hf, blk)], [obres[hh]], skip=(m != 0))

                fin = None
                if m == 4:
                    def fin(ob=ob, obres=obres, p=p, n0=n0):
                        finalize_pair(ob, obres, p, n0, 64)
                units.append(make_unit(kf, ("kTA", l, hf, p), qf, ("scr", p), 64, mcol, ef, ("Et",), pvf, fin))
        for p in range(4):
            g = p // 2
            ob = (ps[0], ps[1])
            obres = [("ps", 0), ("ps", 1)]
            for m in range(2):
                hf = m
                eoff = 64 if m == 0 else (0 if s == 0 else 192)

                def kf(rows, l=l, hf=hf, g=g):
                    o = (hf * 2 + g) * 512
                    return kTB[l][rows, o:o + 128]

                def qf(rows, p=p, n0=n0):
                    return qbT(p, rows, n0, 64)

                def ef(eoff=eoff):
                    return V(EB, eoff, [(0, 2), (1, 64)])

                def pvf(hh, pap, pres, l=l, hf=hf, g=g, m=m, ob=ob, obres=obres, n0=n0):
                    mm(ob[hh][:, n0:n0 + 64], vB_lhs(l, hf, 0, g, hh), pap, m == 0, True,
                       [pres, ("vB", l, hf, 0)], [obres[hh]], skip=(m != 0))

                fin = None
                if m == 1:
                    def fin(ob=ob, obres=obres, p=p, l=l, n0=n0):
                        for hh in range(2):
                            sink_mm(l, 2 * p + hh, hh, ob[hh], obres[hh], n0, 64)
                        finalize_pair(ob, obres, 4 + p, n0, 64)
                units.append(make_unit(kf, ("kTB", l, hf), qf, ("scr", 4 + p), 64, 0, ef, ("EBmask", eoff), pvf, fin))
        run_units(units)

    def evac_v(l, hf, blk, bank, bres, nvalid=128):
        o = (hf * 4 + blk) * 768
        outv = V(vA[l], o, [(192, 4), (128, 2), (1, 64)])
        inv = V(bank, 0, [(128, 4), (64, 2), (1, 64)])
        cp(evac_eng(), outv, inv, [bres], [("vA", l, hf, blk)])

    def evac_vb(l, hf, blk, src_ap_t, src_off, src_res):
        o = (hf * 4 + blk) * 384
        for dup in range(2):
            outv = V(vB[l], o + dup * 128, [(192, 2), (1, 64)])
            inv = V(src_ap_t, src_off, [(64, 2), (1, 64)])
            cp(evac_eng(), outv, inv, [src_res], [("vB", l, hf, blk)])

    def rope_block(ridx, nh, q=0):
        h0 = 10 - nh
        o = q * 640
        x1 = V(qk32, o + h0 * 64, [(64, nh), (1, 8)])
        x2 = V(qk32, o + h0 * 64 + 8, [(64, nh), (1, 8)])
        cs = V(rope, ridx * 16, [(0, nh), (1, 8)])
        sn = V(rope, ridx * 16 + 8, [(0, nh), (1, 8)])
        t = [V(rtmp, q * 320 + i * 80, [(8, nh), (1, 8)]) for i in range(4)]
        R = [("qk32", q), ("rope",)]
        tt("dve", t[0], x1, cs, ALU.mult, R, [("rtmp", q, 0)])
        tt("dve", t[1], x2, sn, ALU.mult, R, [("rtmp", q, 1)])
        tt("dve", t[2], x2, cs, ALU.mult, R, [("rtmp", q, 2)])
        tt("dve", t[3], x1, sn, ALU.mult, R, [("rtmp", q, 3)])
        tt("dve", x1, t[0], t[1], ALU.subtract, [("rtmp", q, 0), ("rtmp", q, 1)], [("qk32", q)])
        tt("dve", x2, t[2], t[3], ALU.add, [("rtmp", q, 2), ("rtmp", q, 3)], [("qk32", q)])

    def in_proj(tile, l, hc, do_q, phase=lambda n: None):
        N = tile["N"]
        NB = N // 128
        kv_out = tile["kvout"]
        issample = tile["kind"] == "S"
        if do_q:
            s = load_slab(l, "qa")
            bs = [mmbank() for _ in range(4)]
            for kc in range(8):
                for p in range(4):
                    mm(ps[bs[p]][:, 0:N], wslab(s, kc, 512, p * 128, 128), nTc(kc, 0, N), kc == 0, kc == 7,
                       [("wb", s), ("nT", kc)], [("ps", bs[p])])
            for p in range(4):
                cp(evac_eng(), qaT(p, ALLR, 0, N), ps[bs[p]][:, 0:N], [("ps", bs[p])], [("scr", p)])
        phase("ip_ka")
        s = load_slab(l, "ka")
        bs = [mmbank() for _ in range(4)]
        for kc in range(8):
            for p in range(4):
                mm(ps[bs[p]][:, 0:N], wslab(s, kc, 512, p * 128, 128), nTc(kc, 0, N), kc == 0, kc == 7,
                   [("wb", s), ("nT", kc)], [("ps", bs[p])])
        for p in range(4):
            o = (hc * 4 + p) * 512
            cp(evac_eng(), kTA[l][:, o:o + N], ps[bs[p]][:, 0:N], [("ps", bs[p])], [("kTA", l, hc, p)])
        phase("ip_kaout")
        if kv_out:
            for tb in range(NB):
                b = mmbank()
                for kc in range(8):
                    mm(ps[b][:, 0:512], nTc(kc, tb * 128, 128), wslab(s, kc, 512, 0, 512), kc == 0, kc == 7,
                       [("wb", s), ("nT", kc)], [("ps", b)])
                k = tb % 2
                cp("act", ost[:, k * 512:(k + 1) * 512], ps[b][:, 0:512], [("ps", b)], [("ost", k)])
                if issample:
                    for sm in range(2):
                        P.dma("act", naks.ap()[l, sm, 448:512, :], ost[64 * sm:64 * sm + 64, k * 512:(k + 1) * 512],
                              [("ost", k)], [("o_naks", l, sm)])
                else:
                    P.dma("act", nakp.ap()[l, tb * 128:(tb + 1) * 128, :], ost[:, k * 512:(k + 1) * 512], [("ost", k)], [("o_nakp", l, tb)])
        phase("ip_va")
        s = load_slab(l, "va")
        for tb in range(NB):
            b = mmbank()
            for kc in range(8):
                mm(ps[b][:, 0:512], nTc(kc, tb * 128, 128), wslab(s, kc, 512, 0, 512), kc == 0, kc == 7,
                   [("wb", s), ("nT", kc)], [("ps", b)])
            evac_v(l, hc, tb, ps[b], ("ps", b))
            if kv_out:
                k = tb % 2
                cp("act", ost[:, k * 512:(k + 1) * 512], ps[b][:, 0:512], [("ps", b)], [("ost", k)])
                if issample:
                    for sm in range(2):
                        P.dma("act", navs.ap()[l, sm, 448:512, :], ost[64 * sm:64 * sm + 64, k * 512:(k + 1) * 512],
                              [("ost", k)], [("o_navs", l, sm)])
                else:
                    P.dma("act", navp.ap()[l, tb * 128:(tb + 1) * 128, :], ost[:, k * 512:(k + 1) * 512], [("ost", k)], [("o_navp", l, tb)])
        phase("ip_qbkv")
        if do_q:
            sq_ = load_slab(l, "qb")
        sk = load_slab(l, "kv")
        def part_a(tb):
            q = tb % 2
            ridx = tile["row0"] // 128 + tb
            qb0 = q * 768
            rt0 = q * 320
            last_b = (tb == NB - 1)
            flagged = kv_out and (issample or last_b)
            k = tb % 2
            Cq = V(rope, ridx * 32, [(0, 8), (1, 16)])
            Sq = V(rope, ridx * 32 + 16, [(0, 8), (8, 2), (1, 8)])
            Ck = V(rope, ridx * 32, [(0, 2), (1, 16)])
            Sk = V(rope, ridx * 32 + 16, [(0, 2), (8, 2), (1, 8)])
            if do_q:
                b = mmbank()
                for kc in range(8):
                    mm(ps[b][:, 0:512], nTc(kc, tb * 128, 128), wslab(sq_, kc, 512, 0, 512), kc == 0, kc == 7,
                       [("wb", sq_), ("nT", kc)], [("ps", b)])
                tt("dve", V(rtmp, rt0, [(16, 8), (1, 16)]), V(ps[b], 0, [(64, 8), (1, 16)]), Cq, ALU.mult,
                   [("ps", b), ("rope",)], [("rtmp", q, 0)])
                tt("dve", V(rtmp, rt0 + 128, [(16, 8), (8, 2), (1, 8)]), V(ps[b], 8, [(64, 8), (-8, 2), (1, 8)]), Sq, ALU.mult,
                   [("ps", b), ("rope",)], [("rtmp", q, 1)])
                cp("act", V(qkbf, qb0 + 16, [(64, 8), (1, 48)]), V(ps[b], 16, [(64, 8), (1, 48)]), [("ps", b)], [("qkbf", q)])
                tt("dve", V(qkbf, qb0, [(64, 8), (1, 16)]), V(rtmp, rt0, [(16, 8), (1, 16)]), V(rtmp, rt0 + 128, [(16, 8), (1, 16)]),
                   ALU.add, [("rtmp", q, 0), ("rtmp", q, 1)], [("qkbf", q)])
            b2 = mmbank()
            for kc in range(8):
                mm(ps[b2][:, 0:256], nTc(kc, tb * 128, 128), wslab(sk, kc, 256, 0, 256), kc == 0, kc == 7,
                   [("wb", sk), ("nT", kc)], [("ps", b2)])
            tt("dve", V(rtmp, rt0 + 256, [(16, 2), (1, 16)]), V(ps[b2], 0, [(64, 2), (1, 16)]), Ck, ALU.mult,
               [("ps", b2), ("rope",)], [("rtmp", q, 2)])
            tt("dve", V(rtmp, rt0 + 288, [(16, 2), (8, 2), (1, 8)]), V(ps[b2], 8, [(64, 2), (-8, 2), (1, 8)]), Sk, ALU.mult,
               [("ps", b2), ("rope",)], [("rtmp", q, 3)])
            cp("act", V(qkbf, qb0 + 512 + 16, [(128, 2), (64, 2), (1, 48)]), V(ps[b2], 16, [(64, 2), (0, 2), (1, 48)]),
               [("ps", b2)], [("qkbf", q)])
            evac_vb(l, hc, tb, ps[b2], 128, ("ps", b2))
            if flagged:
                cp("act", ost[:, k * 512: k * 512 + 256], ps[b2][:, 0:256], [("ps", b2)], [("ost", k)])
            tt("dve", V(qkbf, qb0 + 512, [(128, 2), (64, 2), (1, 16)]), V(rtmp, rt0 + 256, [(16, 2), (0, 2), (1, 16)]),
               V(rtmp, rt0 + 288, [(16, 2), (0, 2), (1, 16)]), ALU.add, [("rtmp", q, 2), ("rtmp", q, 3)], [("qkbf", q)])
            if flagged:
                tt("dve", V(ost, k * 512, [(64, 2), (1, 16)]), V(rtmp, rt0 + 256, [(16, 2), (1, 16)]),
                   V(rtmp, rt0 + 288, [(16, 2), (1, 16)]), ALU.add, [("rtmp", q, 2), ("rtmp", q, 3), ("ost", k)], [("ost", k)])
                if issample:
                    for sm in range(2):
                        P.dma("act", nbks.ap()[l, sm, 64:128, :], ost[64 * sm:64 * sm + 64, k * 512: k * 512 + 128],
                              [("ost", k)], [("o_nbks", l, sm)])
                        P.dma("act", nbvs.ap()[l, sm, 64:128, :], ost[64 * sm:64 * sm + 64, k * 512 + 128: k * 512 + 256],
                              [("ost", k)], [("o_nbvs", l, sm)])
                else:
                    P.dma("act", nbkp.ap()[l], ost[:, k * 512: k * 512 + 128], [("ost", k)], [("o_nbkp", l)])
                    P.dma("act", nbvp.ap()[l], ost[:, k * 512 + 128: k * 512 + 256], [("ost", k)], [("o_nbvp", l)])

        def part_b(tb):
            q = tb % 2
            qb0 = q * 768
            j0 = 0 if do_q else 4
            for j in range(j0, 6):
                tr(ps7b[:, j * 128:(j + 1) * 128], qkbf[:, qb0 + j * 128: qb0 + (j + 1) * 128], idb[:], [("qkbf", q), ("idb",)], [("ps", 7)])
            if do_q:
                cp(evac_eng(), V(scr, 4 * 512 + tb * 128, [(512, 4), (1, 128)]), V(ps7b, 0, [(128, 4), (1, 128)]),
                   [("ps", 7)], [("scr", 4), ("scr", 5), ("scr", 6), ("scr", 7)])
            o = hc * 2 * 512 + tb * 128
            cp(evac_eng(), V(kTB[l], o, [(512, 2), (1, 128)]), V(ps7b, 512, [(128, 2), (1, 128)]), [("ps", 7)], [("kTB", l, hc)])

        for tb in range(NB):
            part_a(tb)
            if tb > 0:
                phase("ip_tr")
                part_b(tb - 1)
                phase("ip_qbkv")
        phase("ip_tr")
        part_b(NB - 1)

    def out_norm(l, N, grp):
        kind = "oa" if grp == 0 else "ob"
        k = rms_stats([(oT(grp * 4 + p, ALLR, 0, N), oTres(grp * 4 + p)) for p in range(4)], N, 512)
        for p in range(4):
            gi = gidx(kind, l, p)
            c = grp * 4 + p
            stt("dve", nTc(c, 0, N), oT(c, ALLR, 0, N), gcol[:, gi:gi + 1], rs[:, k * 512: k * 512 + N], ALU.mult, ALU.mult,
                oTres(c) + [("rs", k), ("gcol",)], [("nT", c)])

    def out_proj(l, N, do_norm=(0, 1)):
        for grp in do_norm:
            out_norm(l, N, grp)
        st = Stats(8, N)
        for half in range(2):
            s = load_slab(l, f"o{half}")
            if half == 0:
                bs = [mmbank() for _ in range(4)]
                for kc in range(8):
                    for mm_ in range(4):
                        mm(ps[bs[mm_]][:, 0:N], wslab(s, kc, 512, mm_ * 128, 128), nTc(kc, 0, N), kc == 0, kc == 7,
                           [("wb", s), ("nT", kc)], [("ps", bs[mm_])])
                for mm_ in range(4):
                    m = mm_
                    tt("dve", hTc(m, 0, N), ps[bs[mm_]][:, 0:N], hTc(m, 0, N), ALU.add, [("ps", bs[mm_]), ("hT", m)], [("hT", m)])
                    st.add(hTc(m, 0, N), [("hT", m)])
                continue
            for mm_ in range(4):
                m = half * 4 + mm_
                b = mmbank()
                for kc in range(8):
                    mm(ps[b][:, 0:N], wslab(s, kc, 512, mm_ * 128, 128), nTc(kc, 0, N), kc == 0, kc == 7,
                       [("wb", s), ("nT", kc)], [("ps", b)])
                tt("dve", hTc(m, 0, N), ps[b][:, 0:N], hTc(m, 0, N), ALU.add, [("ps", b), ("hT", m)], [("hT", m)])
                st.add(hTc(m, 0, N), [("hT", m)])
        return st

    def ffn(l, N, st=None):
        norm_to_nT(l, "ffn", N, st)
        for i in range(11):
            s = load_slab(l, f"gu{i}")
            if i == 0:
                bs = [mmbank() for _ in range(4)]
                coff = [0, 256, 128, 384]
                for kc in range(8):
                    for gi_ in range(4):
                        mm(ps[bs[gi_]][:, 0:N], wslab(s, kc, 512, coff[gi_], 128), nTc(kc, 0, N), kc == 0, kc == 7,
                           [("wb", s), ("nT", kc)], [("ps", bs[gi_])])
                for ff in range(2):
                    f = ff
                    bg, bu = bs[2 * ff], bs[2 * ff + 1]
                    k = f % 2
                    act(sgb[:, k * 512: k * 512 + N], ps[bg][:, 0:N], AF.Silu, [("ps", bg)], [("sgb", k)])
                    tt("dve", actT(f, N), sgb[:, k * 512: k * 512 + N], ps[bu][:, 0:N], ALU.mult, [("sgb", k), ("ps", bu)], [("scr", f)])
                continue
            for ff in range(2):
                f = 2 * i + ff
                bg = mmbank()
                for kc in range(8):
                    mm(ps[bg][:, 0:N], wslab(s, kc, 512, ff * 128, 128), nTc(kc, 0, N), kc == 0, kc == 7,
                       [("wb", s), ("nT", kc)], [("ps", bg)])
                bu = mmbank()
                for kc in range(8):
                    mm(ps[bu][:, 0:N], wslab(s, kc, 512, 256 + ff * 128, 128), nTc(kc, 0, N), kc == 0, kc == 7,
                       [("wb", s), ("nT", kc)], [("ps", bu)])
                k = f % 2
                act(sgb[:, k * 512: k * 512 + N], ps[bg][:, 0:N], AF.Silu, [("ps", bg)], [("sgb", k)])
                tt("dve", actT(f, N), sgb[:, k * 512: k * 512 + N], ps[bu][:, 0:N], ALU.mult, [("sgb", k), ("ps", bu)], [("scr", f)])
        for m in range(8):
            s = load_slab(l, f"dn{m}")
            b = mmbank()
            for kc in range(22):
                mm(ps[b][:, 0:N], wslab(s, kc, 128, 0, 128), actT(kc, N), kc == 0, kc == 21,
                   [("wb", s), ("scr", kc)], [("ps", b)])
            tt("dve", hTc(m, 0, N), ps[b][:, 0:N], hTc(m, 0, N), ALU.add, [("ps", b), ("hT", m)], [("hT", m)])
            cp("pool" if m % 2 else "act", nTc(m, 0, N), hTc(m, 0, N), [("hT", m)], [("nT", m)])

    plctr = [0]

    def load_p(tile, l):
        k = plctr[0] % 2
        plctr[0] += 1
        N = tile["N"]
        NB = N // 128
        r0 = tile["row0"]
        P.dma("sp", V(pst, k * 1024, [(256, NB), (1, 256)]),
              pin.ap()[l, r0:r0 + N, :].rearrange("(b p) c -> p b c", p=128), (), [("pst", k)])
        return k

    def ple(tile, l, pk):
        N = tile["N"]
        NB = N // 128
        for tb in range(NB):
            b = mmbank()
            for kc in range(2):
                o = pk * 1024 + tb * 256 + kc * 128
                tr(ps[b][:, kc * 128:(kc + 1) * 128], pst[:, o:o + 128], id32[:], [("pst", pk), ("id32",)], [("ps", b)])
            cp(evac_eng(), V(ppT, tb * 128, [(512, 2), (1, 128)]), V(ps[b], 0, [(128, 2), (1, 128)]), [("ps", b)], [("ppT",)])
        sp_ = load_slab(l, "pp")
        st = Stats(8, N)
        for half in range(2):
            s = load_slab(l, f"pg{half}")
            for mm_ in range(4):
                m = half * 4 + mm_
                bg = mmbank()
                for kc in range(8):
                    mm(ps[bg][:, 0:N], wslab(s, kc, 512, mm_ * 128, 128), nTc(kc, 0, N), kc == 0, kc == 7,
                       [("wb", s), ("nT", kc)], [("ps", bg)])
                bp = mmbank()
                for kc in range(2):
                    mm(ps[bp][:, 0:N], wslab(sp_, kc, 1024, m * 128, 128), ppT[:, kc * 512: kc * 512 + N], kc == 0, kc == 1,
                       [("wb", sp_), ("ppT",)], [("ps", bp)])
                k = m % 2
                act(sgb[:, k * 512: k * 512 + N], ps[bg][:, 0:N], AF.Sigmoid, [("ps", bg)], [("sgb", k)])
                tt("dve", rc[:, k * 512: k * 512 + N], sgb[:, k * 512: k * 512 + N], ps[bp][:, 0:N], ALU.mult,
                   [("sgb", k), ("ps", bp)], [("rc", k)])
                tt("dve", hTc(m, 0, N), hTc(m, 0, N), rc[:, k * 512: k * 512 + N], ALU.add, [("hT", m), ("rc", k)], [("hT", m)])
                st.add(hTc(m, 0, N), [("hT", m)])
        return st

    def load_x_block(tile, tb, pslot=None):
        r0 = tile["row0"] + tb * 128
        if tb < 2:
            buf, off, res = xs, tb * 1024, ("xs", tb)
        else:
            buf, off, res = pst, pslot * 1024, ("pst", pslot)
        tile.setdefault("xloc", {})[tb] = (buf, off, res)
        P.dma("sp", buf[:, off:off + 1024], xin.ap()[r0:r0 + 128, :], (), [res])

    def x_to_hT(tile, preloaded):
        N = tile["N"]
        NB = N // 128
        for tb in range(NB):
            if tb >= preloaded:
                ps_ = None
                if tb == 2:
                    ps_ = plctr[0] % 2
                elif tb == 3:
                    ps_ = (plctr[0] + 1) % 2
                load_x_block(tile, tb, ps_)
            buf, off, res = tile["xloc"][tb]
            for half in range(2):
                b = 4 + half
                for cc in range(4):
                    c = half * 4 + cc
                    tr(ps[b][:, cc * 128:(cc + 1) * 128], buf[:, off + c * 128: off + (c + 1) * 128], id32[:],
                       [res, ("id32",)], [("ps", b)])
                cp(evac_eng(), V(hT, half * 4 * 512 + tb * 128, [(512, 4), (1, 128)]), V(ps[b], 0, [(128, 4), (1, 128)]),
                   [("ps", b)], [("hT", half * 4 + cc) for cc in range(4)])

    def write_y(tile, st=None):
        N = tile["N"]
        NB = N // 128
        if st is None:
            k = rms_stats([(hTc(c, 0, N), [("hT", c)]) for c in range(8)], N, D)
        else:
            k = st.finish(D)
        for c in range(8):
            gi = gidx("fin", 0, c)
            stt("dve", hTc(c, 0, N), hTc(c, 0, N), gcol[:, gi:gi + 1], rs[:, k * 512: k * 512 + N], ALU.mult, ALU.mult,
                [("hT", c), ("rs", k), ("gcol",)], [("hT", c)])
        for tb in range(NB):
            for half in range(2):
                b = mmbank()
                for cc in range(4):
                    c = half * 4 + cc
                    tr(ps[b][:, cc * 128:(cc + 1) * 128], hTc(c, tb * 128, 128), id32[:], [("hT", c), ("id32",)], [("ps", b)])
                k2 = (tb * 2 + half) % 2
                cp("act", ost[:, k2 * 512:(k2 + 1) * 512], ps[b][:, 0:512], [("ps", b)], [("ost", k2)])
                r0 = tile["yrow0"] + tb * 128
                P.dma("act", y_d.ap()[r0:r0 + 128, half * 512:(half + 1) * 512], ost[:, k2 * 512:(k2 + 1) * 512],
                      [("ost", k2)], [("o_y", r0, half)])

    def load_E(l):
        P.dma("sp", Et[:], ebf.ap()[l], [("ebf", l)], [("Et",)])

    def load_sample_cache(l, s):
        for m in range(4):
            sl = m % 2
            P.dma("sp", xs[:, sl * 1024: sl * 1024 + 512], cak.ap()[l, s, m * 128:(m + 1) * 128, :], (), [("xs", sl)])
            P.dma("sp", xs[:, sl * 1024 + 512: sl * 1024 + 1024], cav.ap()[l, s, m * 128:(m + 1) * 128, :], (), [("xs", sl)])
            b = mmbank()
            for p in range(4):
                tr(ps[b][:, p * 128:(p + 1) * 128], xs[:, sl * 1024 + p * 128: sl * 1024 + (p + 1) * 128], id32[:],
                   [("xs", sl), ("id32",)], [("ps", b)])
            cp(evac_eng(), V(kTA[l], m * 128, [(512, 4), (1, 128)]), V(ps[b], 0, [(128, 4), (1, 128)]), [("ps", b)],
               [("kTA", l, 0, p) for p in range(4)])
            o = m * 768
            cp("pool", V(vA[l], o, [(192, 4), (128, 2), (1, 64)]), V(xs, sl * 1024 + 512, [(128, 4), (64, 2), (1, 64)]),
               [("xs", sl)], [("vA", l, 0, m)])
        P.dma("sp", xs[:, 0:128], cbk.ap()[l, s], (), [("xs", 0)])
        P.dma("sp", xs[:, 128:256], cbv.ap()[l, s], (), [("xs", 0)])
        cp("pool", V(qkbf, 512, [(128, 2), (64, 2), (1, 64)]), V(xs, 0, [(64, 2), (0, 2), (1, 64)]), [("xs", 0)], [("qkbf", 0)])
        for j in range(4, 6):
            tr(ps7b[:, j * 128:(j + 1) * 128], qkbf[:, j * 128:(j + 1) * 128], idb[:], [("qkbf", 0), ("idb",)], [("ps", 7)])
        cp(evac_eng(), V(kTB[l], 0, [(512, 2), (1, 128)]), V(ps7b, 512, [(128, 2), (1, 128)]), [("ps", 7)], [("kTB", l, 0)])
        evac_vb(l, 0, 0, xs, 128, ("xs", 0))

    tiles = []
    tiles.append(dict(kind="H0", row0=0, N=512, kvout=False))
    tiles.append(dict(kind="H1", row0=512, N=512, kvout=False))
    for i in range(8):
        tiles.append(dict(kind="own", row0=HALO + 512 * i, N=512, yrow0=512 * i, kvout=(i == 7)))
    tiles.append(dict(kind="S", row0=SROW, N=128, yrow0=OWN, kvout=True))

    import os
    DBG = int(os.environ.get("KDBG", "9999"))
    phctr = [0]

    class _Stop(Exception):
        pass

    def phase(name):
        P.label = name
        phctr[0] += 1
        if phctr[0] > DBG:
            print("KDBG stop before phase", phctr[0], name)
            raise _Stop()

    half_ctr = [0, 0]

    def main_schedule():
        for ti, tile in enumerate(tiles):
            kind = tile["kind"]
            N = tile["N"]
            pre = tile.get("preloaded", 0)
            phase("x_to_hT")
            x_to_hT(tile, pre)
            layers = [0] if kind == "H0" else [0, 1]
            nst = None
            for l in layers:
                kvonly = (kind == "H0") or (kind == "H1" and l == 1)
                if kind == "S":
                    hc = 1
                else:
                    hc = half_ctr[l] % 2
                    half_ctr[l] += 1
                if not kvonly:
                    pk = load_p(tile, l)
                    load_E(l)
                if l == 1 and kind != "S" and ti + 1 < len(tiles):
                    nt = tiles[ti + 1]
                    nb = min(3, nt["N"] // 128)
                    for tb in range(nb):
                        load_x_block(nt, tb, plctr[0] % 2)
                    nt["preloaded"] = nb
                phase("norm")
                norm_to_nT(l, "mix", N, nst)
                nst = None
                phase("in_proj")
                in_proj(tile, l, hc, not kvonly, phase)
                if kvonly:
                    continue
                phase("attn")
                if kind == "S":
                    for s in range(2):
                        load_sample_cache(l, s)
                        attn_sample(l, s)
                else:
                    halo_cur = kind in ("H0", "H1")
                    halo_prev = kind in ("H0", "H1") or (kind == "own" and tile["row0"] == HALO)
                    attn_prompt(l, hc, halo_prev, halo_cur)
                phase("out_proj")
                fst = out_proj(l, N, (0, 1) if kind == "S" else (1,))
                phase("ffn")
                ffn(l, N, fst)
                phase("ple")
                nst = ple(tile, l, pk)
            if kind in ("S", "own"):
                phase("write_y")
                write_y(tile, nst)
            if kind == "H0":
                P.label = "setup"
                build_E()

    try:
        main_schedule()
    except _Stop:
        pass

    sems = {}
    for e in Prog.ENG:
        sems[e] = es.enter_context(nc.semaphore(f"sem_{e}"))
    for q, n in P.ndma.items():
        for s in range(n):
            sems[("dma", q, s)] = es.enter_context(nc.semaphore(f"dma_{q}_{s}"))
    block = es.enter_context(nc.Block())
    P.emit(nc, block, sems)
    es.close()
    nc._prog = P
    return nc


_CACHE = {}


def _rope_tables():
    NBLK = NROW // 128
    inv = (500000.0 ** (-np.arange(0, 16, 2, dtype=np.float32) / np.float32(16))).astype(np.float32)
    tabs = []
    for c in range(NCORES):
        s0 = (c % 4) * OWN
        pos = np.concatenate([np.arange(s0 - HALO, s0 + OWN), PAST + np.arange(64), PAST + np.arange(64)]).astype(np.float32)
        ang = (pos[:, None] * inv[None, :]).astype(np.float32)
        co = np.cos(ang.astype(np.float64)).astype(np.float32)
        si = np.sin(ang.astype(np.float64)).astype(np.float32)
        t = np.concatenate([co, co, -si, si], axis=1).astype(np.float32)
        tabs.append(np.ascontiguousarray(t.reshape(NBLK, 128, 32).transpose(1, 0, 2).reshape(128, NBLK * 32)))
    return tabs


def kernel(x_prompt, x_sample, p_prompt, p_sample, cache_a_k, cache_a_v, cache_b_k, cache_b_v,
           g_mix_norm, w_in, rel_bias_a, sinks_b, g_out_a, g_out_b, w_out, g_ffn_norm,
           w_gate_up, w_down, w_ple_proj, w_ple_gate, g_final):
    if "nc" not in _CACHE:
        _CACHE["nc"] = build_program()
    nc = _CACHE["nc"]
    in_maps = prep_inputs(x_prompt, x_sample, p_prompt, p_sample, cache_a_k, cache_a_v, cache_b_k, cache_b_v,
                          g_mix_norm, w_in, rel_bias_a, sinks_b, g_out_a, g_out_b, w_out, g_ffn_norm,
                          w_gate_up, w_down, w_ple_proj, w_ple_gate, g_final)
    res = run_bass_kernel_spmd(nc, in_maps, core_ids=list(range(NCORES)))
    return gather_outputs(res.results)


def prep_inputs(x_prompt, x_sample, p_prompt, p_sample, cache_a_k, cache_a_v, cache_b_k, cache_b_v,
                g_mix_norm, w_in, rel_bias_a, sinks_b, g_out_a, g_out_b, w_out, g_ffn_norm,
                w_gate_up, w_down, w_ple_proj, w_ple_gate, g_final):
    f = lambda a: np.asarray(a, dtype=np.float32)
    x_prompt, x_sample, p_prompt, p_sample = f(x_prompt), f(x_sample), f(p_prompt), f(p_sample)
    cache_a_k, cache_a_v, cache_b_k, cache_b_v = f(cache_a_k), f(cache_a_v), f(cache_b_k), f(cache_b_v)
    wf = build_weights(f(w_in), f(w_out), f(w_gate_up), f(w_down), f(w_ple_proj), f(w_ple_gate))
    gcol = np.zeros((128, NG), np.float32)
    for l in range(NL):
        for c in range(8):
            gcol[:, gidx("mix", l, c)] = f(g_mix_norm)[l, c * 128:(c + 1) * 128]
            gcol[:, gidx("ffn", l, c)] = f(g_ffn_norm)[l, c * 128:(c + 1) * 128]
        for c in range(4):
            gcol[:, gidx("oa", l, c)] = f(g_out_a)[l, c * 128:(c + 1) * 128]
            gcol[:, gidx("ob", l, c)] = f(g_out_b)[l, c * 128:(c + 1) * 128]
    for c in range(8):
        gcol[:, gidx("fin", 0, c)] = f(g_final)[c * 128:(c + 1) * 128]
    sinkrow = np.ascontiguousarray(f(sinks_b).reshape(1, 16))
    j = np.arange(768)
    dist = np.where(j <= 640, j, j - 768)
    idx = np.clip(dist, -256, 256) + 256
    relc = np.ascontiguousarray(f(rel_bias_a)[:, :, idx].reshape(NL * 8, 768))
    ropes = _rope_tables()
    in_maps = []
    for c in range(NCORES):
        b = c // 4
        s0 = (c % 4) * OWN
        xin = np.zeros((NROW, D), np.float32)
        pin = np.zeros((NL, NROW, 256), np.float32)
        lo = s0 - HALO
        if lo >= 0:
            xin[0:HALO + OWN] = x_prompt[b, lo:s0 + OWN]
            pin[:, 0:HALO + OWN] = p_prompt[:, b, lo:s0 + OWN]
        else:
            xin[HALO:HALO + OWN] = x_prompt[b, s0:s0 + OWN]
            pin[:, HALO:HALO + OWN] = p_prompt[:, b, s0:s0 + OWN]
        xin[SROW:SROW + 64] = x_sample[2 * c]
        xin[SROW + 64:SROW + 128] = x_sample[2 * c + 1]
        pin[:, SROW:SROW + 64] = p_sample[:, 2 * c]
        pin[:, SROW + 64:SROW + 128] = p_sample[:, 2 * c + 1]
        km = np.zeros((128, 4), np.float32)
        if c % 4 == 0:
            km[:, 1] = MASKV
        km[0:64, 2] = MASKV
        in_maps.append(dict(
            xin=xin, pin=pin,
            cak=np.ascontiguousarray(cache_a_k[:, 2 * c:2 * c + 2].reshape(NL, 2, 512, 512)),
            cav=np.ascontiguousarray(cache_a_v[:, 2 * c:2 * c + 2].reshape(NL, 2, 512, 512)),
            cbk=np.ascontiguousarray(cache_b_k[:, 2 * c:2 * c + 2].reshape(NL, 2, 128, 128)),
            cbv=np.ascontiguousarray(cache_b_v[:, 2 * c:2 * c + 2].reshape(NL, 2, 128, 128)),
            wf=wf, gcol=gcol, sinkrow=sinkrow, relc=relc, rope=ropes[c], kmask=km))
    return in_maps


def gather_outputs(R):
    B, S = 2, 16384
    y_prompt = np.empty((B, S, D), np.float32)
    y_sample = np.empty((16, 64, D), np.float32)
    for c in range(NCORES):
        b = c // 4
        s0 = (c % 4) * OWN
        y_prompt[b, s0:s0 + OWN] = R[c]["y"][0:OWN]
        y_sample[2 * c] = R[c]["y"][OWN:OWN + 64]
        y_sample[2 * c + 1] = R[c]["y"][OWN + 64:OWN + 128]
    last = [3, 7]
    nak_p = np.stack([R[c]["nakp"] for c in last], axis=1).reshape(NL, B, 512, 8, 64)
    nav_p = np.stack([R[c]["navp"] for c in last], axis=1).reshape(NL, B, 512, 8, 64)
    nbk_p = np.stack([R[c]["nbkp"] for c in last], axis=1).reshape(NL, B, 128, 2, 64)
    nbv_p = np.stack([R[c]["nbvp"] for c in last], axis=1).reshape(NL, B, 128, 2, 64)
    nak_s = np.concatenate([R[c]["naks"] for c in range(NCORES)], axis=1).reshape(NL, 16, 512, 8, 64)
    nav_s = np.concatenate([R[c]["navs"] for c in range(NCORES)], axis=1).reshape(NL, 16, 512, 8, 64)
    nbk_s = np.concatenate([R[c]["nbks"] for c in range(NCORES)], axis=1).reshape(NL, 16, 128, 2, 64)
    nbv_s = np.concatenate([R[c]["nbvs"] for c in range(NCORES)], axis=1).reshape(NL, 16, 128, 2, 64)
    return (y_prompt, y_sample, nak_p, nav_p, nbk_p, nbv_p, nak_s, nav_s, nbk_s, nbv_s)
```

```python
import numpy as np
import concourse.bass as bass
import concourse.mybir as mybir
from concourse.bass_utils import run_bass_kernel_spmd

F32 = mybir.dt.float32
BF16 = mybir.dt.bfloat16
AF = mybir.ActivationFunctionType
ALU = mybir.AluOpType

NCORES = 8
D = 1024
DFF = 2816
NL = 2
TT = 512
OWN = 4096
HALO = 1024
NROW = HALO + OWN + 128
SROW = HALO + OWN
PAST = 2048
EPS = 1e-6
SCALE = 0.125
MASKV = -60.0
NWB = 3
SLOT = 4096


def _slab(W, cols):
    K = W.shape[0]
    kc = K // 128
    sub = W[:, cols]
    return np.ascontiguousarray(sub.reshape(kc, 128, len(cols)).transpose(1, 0, 2).reshape(128, kc * len(cols)))


def slab_table():
    tab = {}
    off = 0
    for l in range(NL):
        for nm, sz in ([("qa", 4096), ("ka", 4096), ("va", 4096), ("qb", 4096), ("kv", 2048), ("o0", 4096), ("o1", 4096)]
                       + [(f"gu{i}", 4096) for i in range(11)] + [(f"dn{m}", 2816) for m in range(8)]
                       + [("pg0", 4096), ("pg1", 4096), ("pp", 2048)]):
            tab[(l, nm)] = (off, sz)
            off += sz
    return tab, off


SLABS, WTOT = slab_table()


def build_weights(w_in, w_out, w_gate_up, w_down, w_ple_proj, w_ple_gate):
    out = np.empty((128, WTOT), np.float32)
    ar = np.arange
    for l in range(NL):
        parts = {
            "qa": _slab(w_in[l], ar(0, 512)), "ka": _slab(w_in[l], ar(512, 1024)), "va": _slab(w_in[l], ar(1024, 1536)),
            "qb": _slab(w_in[l], ar(1536, 2048)), "kv": _slab(w_in[l], ar(2048, 2304)),
            "o0": _slab(w_out[l], ar(0, 512)), "o1": _slab(w_out[l], ar(512, 1024)),
            "pg0": _slab(w_ple_gate[l], ar(0, 512)), "pg1": _slab(w_ple_gate[l], ar(512, 1024)),
            "pp": _slab(w_ple_proj[l], ar(0, 1024)),
        }
        for i in range(11):
            parts[f"gu{i}"] = _slab(w_gate_up[l], np.concatenate([ar(256 * i, 256 * i + 256), ar(DFF + 256 * i, DFF + 256 * i + 256)]))
        for m in range(8):
            parts[f"dn{m}"] = _slab(w_down[l], ar(128 * m, 128 * m + 128))
        for nm, a in parts.items():
            o, sz = SLABS[(l, nm)]
            assert a.shape[1] == sz
            out[:, o:o + sz] = a
    return out


def gidx(kind, l, c):
    base = {"mix": 0, "ffn": 16, "oa": 32, "ob": 40, "fin": 48}[kind]
    n = {"mix": 8, "ffn": 8, "oa": 4, "ob": 4, "fin": 8}[kind]
    return base + l * n + c


NG = 56


class Prog:
    ENG = ("pe", "act", "dve", "pool", "sp")

    def __init__(self, ndma=None):
        self.st = {e: [] for e in self.ENG}
        self.lastw = {}
        self.readers = {}
        self.seen = {e: {} for e in self.ENG}
        self.ndma = ndma or {"sp": 10, "act": 4, "pool": 4}
        self.dma_cnt = {}
        self.dma_rr = {q: 0 for q in self.ndma}
        self.milestones = {e: set() for e in self.ENG}
        self.label = "setup"
        self.labels = {e: [] for e in self.ENG}

    def _deps(self, eng, R, W):
        deps = []
        for r in R:
            t = self.lastw.get(r)
            if t is not None:
                deps.append(t)
        for w in W:
            t = self.lastw.get(w)
            if t is not None:
                deps.append(t)
            for k, v in self.readers.get(w, {}).items():
                deps.append((k, v))
        waits = []
        seen = self.seen[eng]
        best = {}
        for k, v in deps:
            if k == eng and eng == "pe":
                continue
            if seen.get(k, -1) >= v:
                continue
            if best.get(k, -1) < v:
                best[k] = v
        for k, v in best.items():
            seen[k] = v
            waits.append((k, v))
            if not isinstance(k, tuple):
                self.milestones[k].add(v)
        return waits

    def _mark(self, tok, R, W):
        k, v = tok
        for r in R:
            d = self.readers.setdefault(r, {})
            if d.get(k, -1) < v:
                d[k] = v
        for w in W:
            self.lastw[w] = tok
            self.readers[w] = {}

    def op(self, eng, fn, R=(), W=()):
        W = list(W) + [r for r in R if r[0] == "ps" and r not in W]
        waits = self._deps(eng, R, W)
        idx = len(self.st[eng])
        self.labels[eng].append(self.label)
        self.st[eng].append(("op", fn, waits, None))
        self._mark((eng, idx), R, W)

    def dma(self, q, out_ap, in_ap, R=(), W=()):
        s = self.dma_rr[q]
        self.dma_rr[q] = (s + 1) % self.ndma[q]
        key = ("dma", q, s)
        cnt = self.dma_cnt.get(key, 0)
        waits = self._deps(q, R, W)
        if cnt > 0 and self.seen[q].get(key, -1) < 16 * cnt:
            self.seen[q][key] = 16 * cnt
            waits.append((key, 16 * cnt))
        self.dma_cnt[key] = cnt + 1
        self.labels[q].append(self.label)
        self.st[q].append(("dma", (out_ap, in_ap), waits, key))
        self._mark((key, 16 * (cnt + 1)), R, W)

    def emit(self, nc, block, sems):
        rank = {}
        for e in self.ENG:
            ms = sorted(self.milestones[e])
            rank[e] = {v: i + 1 for i, v in enumerate(ms)}
        engobj = {"pe": "tensor", "act": "scalar", "dve": "vector", "pool": "gpsimd", "sp": "sync"}

        def run(e, eng):
            for idx, (kind, payload, waits, key) in enumerate(self.st[e]):
                for k, v in waits:
                    if isinstance(k, tuple):
                        eng.wait_ge(sems[k], v)
                    else:
                        eng.wait_ge(sems[k], rank[k][v])
                if kind == "op":
                    ins = payload(eng)
                    if idx in rank[e]:
                        ins.then_inc(sems[e], 1)
                else:
                    o, i = payload
                    eng.dma_start(out=o, in_=i).then_inc(sems[key], 16)
            for key, cnt in self.dma_cnt.items():
                if key[1] == e:
                    eng.wait_ge(sems[key], 16 * cnt)

        for e in self.ENG:
            deco = getattr(block, engobj[e])

            def body(eng, e=e):
                run(e, eng)
            deco(body)


def build_program():
    nc = bass.Bass("TRN2", target_bir_lowering=False)
    dt_in = lambda n, s: nc.dram_tensor(n, s, F32, kind="ExternalInput")
    dt_out = lambda n, s: nc.dram_tensor(n, s, F32, kind="ExternalOutput")
    xin = dt_in("xin", [NROW, D])
    pin = dt_in("pin", [NL, NROW, 256])
    cak = dt_in("cak", [NL, 2, 512, 512])
    cav = dt_in("cav", [NL, 2, 512, 512])
    cbk = dt_in("cbk", [NL, 2, 128, 128])
    cbv = dt_in("cbv", [NL, 2, 128, 128])
    wf = dt_in("wf", [128, WTOT])
    gcol_d = dt_in("gcol", [128, NG])
    sink_d = dt_in("sinkrow", [1, 16])
    relc = dt_in("relc", [NL * 8, 768])
    NBLK = NROW // 128
    rope_d = dt_in("rope", [128, NBLK * 32])
    kmask_d = dt_in("kmask", [128, 4])

    wbf = nc.dram_tensor("wbf", [128, WTOT], BF16, kind="Internal")
    srel = nc.dram_tensor("srel", [NL * 8, 128 * 768], F32, kind="Internal")
    ebf = nc.dram_tensor("ebf", [NL, 128, 5120], BF16, kind="Internal")

    y_d = dt_out("y", [OWN + 128, D])
    nakp = dt_out("nakp", [NL, 512, 512])
    navp = dt_out("navp", [NL, 512, 512])
    nbkp = dt_out("nbkp", [NL, 128, 128])
    nbvp = dt_out("nbvp", [NL, 128, 128])
    naks = dt_out("naks", [NL, 2, 512, 512])
    navs = dt_out("navs", [NL, 2, 512, 512])
    nbks = dt_out("nbks", [NL, 2, 128, 128])
    nbvs = dt_out("nbvs", [NL, 2, 128, 128])

    from contextlib import ExitStack
    es = ExitStack()

    def sb(name, F, dt):
        return es.enter_context(nc.sbuf_tensor("sb_" + name, [128, F], dt))

    hT = sb("hT", 8 * 512, F32)
    nT = sb("nT", 8 * 512, BF16)
    sq = sb("sq", 2 * 512, BF16)
    xs = sb("xs", 2 * 1024, F32)
    ost = sb("ost", 2 * 512, F32)
    scr = sb("scr", 12288, BF16)
    scr32 = scr.bitcast(F32)
    kTA = [sb(f"kTA{l}", 2 * 4 * 512, BF16) for l in range(NL)]
    vA = [sb(f"vA{l}", 2 * 4 * 768, BF16) for l in range(NL)]
    kTB = [sb(f"kTB{l}", 2 * 2 * 512, BF16) for l in range(NL)]
    vB = [sb(f"vB{l}", 2 * 4 * 384, BF16) for l in range(NL)]
    qk32 = sb("qk32", 16, F32)
    rtmp = sb("rtmp", 2 * 4 * 80, F32)
    qkbf = sb("qkbf", 2 * 768, BF16)
    ex = sb("ex", 6 * 512, BF16)
    pTt = sb("pTt", 6 * 512, BF16)
    Et = sb("Et", 8 * 640, BF16)
    EB = sb("EB", 256, BF16)
    wb = sb("wb", NWB * SLOT, BF16)
    pst = sb("pst", 2 * 1024, F32)
    ppT = sb("ppT", 2 * 512, BF16)
    rs = sb("rs", 2 * 512, F32)
    rc = sb("rc", 2 * 512, F32)
    sgb = sb("sgb", 2 * 512, F32)
    id32 = sb("id32", 128, F32)
    idb = sb("idb", 128, BF16)
    onesb = sb("onesb", 128, BF16)
    gcol = sb("gcol", NG, F32)
    sinkf = sb("sinkf", 16, F32)
    esrow = sb("esrow", 16, BF16)
    sel = sb("sel", 256, BF16)
    rope = sb("rope", NBLK * 32, F32)
    kmask = sb("kmask", 4, F32)
    epsb = sb("epsb", 1, F32)
    psT = [es.enter_context(nc.psum_tensor(f"ps{i}", [128, 1024], F32)) for i in range(4)]
    ps = [psT[i // 2][:, (i % 2) * 512:(i % 2 + 1) * 512] for i in range(8)]
    ps7b = psT[3].bitcast(BF16)[:, 1024:2048]

    P = Prog()

    def V(t, off, dims, p0=0, npart=128):
        if isinstance(t, bass.AP):
            base = t.offset
            t = t.tensor
        else:
            base = 0
        Fd = t.shape[1]
        return bass.AP(t, base + p0 * Fd + off, [[Fd, npart]] + [[s, n] for s, n in dims])

    def mm(out, lhsT, rhs, start, stop, R, W, tp=None, skip=False):
        if skip:
            P.op("pe", lambda e: e.matmul(out, lhsT=lhsT, rhs=rhs, start=start, stop=stop, skip_group_check=True), R, W)
        elif tp is None:
            P.op("pe", lambda e: e.matmul(out, lhsT=lhsT, rhs=rhs, start=start, stop=stop), R, W)
        else:
            P.op("pe", lambda e: e.matmul(out, lhsT=lhsT, rhs=rhs, start=start, stop=stop, tile_position=tp), R, W)

    def tr(out, in_, ident, R, W):
        P.op("pe", lambda e: e.transpose(out, in_, ident), R, W)

    def act(out, in_, func, R, W, bias=None, scale=None):
        kw = {}
        if bias is not None:
            kw["bias"] = bias
        if scale is not None:
            kw["scale"] = scale
        P.op("act", lambda e: e.activation(out=out, in_=in_, func=func, **kw), R, W)

    def cp(eng, out, in_, R, W):
        if eng == "act":
            P.op("act", lambda e: e.activation(out=out, in_=in_, func=AF.Copy), R, W)
        else:
            P.op(eng, lambda e: e.tensor_copy(out=out, in_=in_), R, W)

    def tt(eng, out, in0, in1, op, R, W):
        P.op(eng, lambda e: e.tensor_tensor(out=out, in0=in0, in1=in1, op=op), R, W)

    def stt(eng, out, in0, scalar, in1, op0, op1, R, W):
        P.op(eng, lambda e: e.scalar_tensor_tensor(out=out, in0=in0, scalar=scalar, in1=in1, op0=op0, op1=op1), R, W)

    def recip(out, in_, R, W):
        P.op("dve", lambda e: e.reciprocal(out=out, in_=in_), R, W)

    def memset(eng, ap, val, W):
        P.op(eng, lambda e: e.memset(ap, val), (), W)

    def hTc(c, n0, n):
        return hT[:, c * 512 + n0: c * 512 + n0 + n]

    def nTc(c, n0, n):
        return nT[:, c * 512 + n0: c * 512 + n0 + n]

    def qaT(p, rows, n0, n):
        return scr[rows, p * 512 + n0: p * 512 + n0 + n]

    def qbT(p, rows, n0, n):
        return scr[rows, (4 + p) * 512 + n0: (4 + p) * 512 + n0 + n]

    def oT(s, rows, n0, n):
        return scr32[rows, 2048 + s * 512 + n0: 2048 + s * 512 + n0 + n]

    def oTres(s):
        return [("scr", 8 + 2 * s), ("scr", 9 + 2 * s)]

    def actT(f, n):
        return scr[:, f * 512: f * 512 + n]

    ALLR = slice(0, 128)
    psctr = [0]

    MMB = (0, 1, 2, 4, 5, 6)

    def mmbank():
        b = MMB[psctr[0] % len(MMB)]
        psctr[0] += 1
        return b

    evctr = [0]

    def evac_eng():
        evctr[0] += 1
        return "act" if evctr[0] % 2 else "dve"

    P.op("pool", lambda e: e.iota(id32[:], [[1, 128]], base=0, channel_multiplier=-1, allow_small_or_imprecise_dtypes=True), (), [("id32",)])
    P.op("pool", lambda e: e.tensor_single_scalar(out=id32[:], in_=id32[:], scalar=0.0, op=ALU.is_equal), [("id32",)], [("id32",)])
    cp("pool", idb[:], id32[:], [("id32",)], [("idb",)])
    memset("pool", onesb[:], 1.0, [("onesb",)])
    memset("pool", epsb[:], EPS, [("epsb",)])
    memset("pool", sel[:], 0.0, [("sel",)])
    memset("pool", sel[0:1, 64:128], 1.0, [("sel",)])
    memset("pool", sel[0:1, 128:192], 1.0, [("sel",)])
    memset("pool", EB[:], 1.0, [("EB",)])
    memset("pool", EB[64:128, 0:64], 0.0, [("EB",)])
    memset("pool", EB[0:64, 192:256], 0.0, [("EB",)])
    for l in range(NL):
        for hf in range(2):
            for blk in range(4):
                for p in range(4):
                    o = (hf * 4 + blk) * 768 + p * 192 + 64
                    memset("pool", vA[l][:, o:o + 64], 1.0, [("vA", l, hf, blk)])
                for g in range(2):
                    o = (hf * 4 + blk) * 384 + g * 192 + 64
                    memset("pool", vB[l][:, o:o + 64], 1.0, [("vB", l, hf, blk)])
    P.dma("sp", gcol[:], gcol_d.ap(), (), [("gcol",)])
    P.dma("sp", rope[:], rope_d.ap(), (), [("rope",)])
    P.dma("sp", kmask[:], kmask_d.ap(), (), [("kmask",)])
    P.dma("sp", sinkf[0:1, :], sink_d.ap(), (), [("sinkf",)])
    act(esrow[0:1, :], sinkf[0:1, :], AF.Exp, [("sinkf",)], [("esrow",)])
    for l in range(NL):
        for s in range(2):
            P.dma("sp", naks.ap()[l, s, 0:448, :], cak.ap()[l, s, 64:512, :], (), [("o_naks", l, s)])
            P.dma("sp", navs.ap()[l, s, 0:448, :], cav.ap()[l, s, 64:512, :], (), [("o_navs", l, s)])
            P.dma("sp", nbks.ap()[l, s, 0:64, :], cbk.ap()[l, s, 64:128, :], (), [("o_nbks", l, s)])
            P.dma("sp", nbvs.ap()[l, s, 0:64, :], cbv.ap()[l, s, 64:128, :], (), [("o_nbvs", l, s)])
    def build_E():
        for i in range(NL * 8):
            P.dma("sp", srel.ap()[i:i + 1, :].rearrange("a (r c) -> (a r) c", c=768),
                  bass.AP(relc, i * 768, [[0, 128], [1, 768]]), (), [("srel", i)])
        for l in range(NL):
            for h in range(8):
                i = l * 8 + h
                sl = i % 2
                P.dma("sp", xs[:, sl * 1024: sl * 1024 + 640], bass.AP(srel, i * 128 * 768, [[767, 128], [1, 640]]),
                      [("srel", i)], [("xs", sl)])
                act(Et[:, h * 640:(h + 1) * 640], xs[:, sl * 1024: sl * 1024 + 640], AF.Exp, [("xs", sl)], [("Et",)])
                memset("pool", Et[64:128, h * 640: h * 640 + 64], 0.0, [("Et",)])
                memset("pool", Et[0:64, h * 640 + 576: h * 640 + 640], 0.0, [("Et",)])
            P.dma("sp", ebf.ap()[l], Et[:], [("Et",)], [("ebf", l)])

    first_order = ["ka", "va", "kv", "qa", "qb", "o0", "o1"] + [f"gu{i}" for i in range(11)] + [f"dn{m}" for m in range(8)] \
        + ["pp", "pg0", "pg1"]
    cast_queue = [(l, nm) for l in range(NL) for nm in first_order]
    cast_done = set()
    CAST_AHEAD = 6

    def issue_casts(n):
        for _ in range(n):
            if not cast_queue:
                return
            l, nm = cast_queue.pop(0)
            if (l, nm) in cast_done:
                continue
            o, sz = SLABS[(l, nm)]
            P.dma("pool", wbf.ap()[:, o:o + sz], wf.ap()[:, o:o + sz], (), [("wbf", l, nm)])
            cast_done.add((l, nm))

    def ensure_cast(l, nm):
        if (l, nm) not in cast_done:
            cast_queue.remove((l, nm))
            o, sz = SLABS[(l, nm)]
            P.dma("pool", wbf.ap()[:, o:o + sz], wf.ap()[:, o:o + sz], (), [("wbf", l, nm)])
            cast_done.add((l, nm))

    issue_casts(CAST_AHEAD)

    wslot = [0]

    def load_slab(l, nm):
        ensure_cast(l, nm)
        issue_casts(1)
        o, sz = SLABS[(l, nm)]
        s = wslot[0] % NWB
        wslot[0] += 1
        P.dma("sp", wb[:, s * SLOT: s * SLOT + sz], wbf.ap()[:, o:o + sz], [("wbf", l, nm)], [("wb", s)])
        return s

    def wslab(s, kc, ncols, c0, n):
        o = s * SLOT + kc * ncols + c0
        return wb[:, o:o + n]

    rsctr = [0]

    class Stats:
        def __init__(self, nsrc, N):
            self.n = nsrc
            self.N = N
            self.i = 0
            self.pend = None

        def add(self, ap, rl):
            i = self.i
            self.i += 1
            sl = i % 2
            N = self.N
            tt("pool", sq[:, sl * 512: sl * 512 + N], ap, ap, ALU.mult, rl, [("sq", sl)])
            self.flush()
            self.pend = (i, sl)

        def flush(self):
            if self.pend is not None:
                i, sl = self.pend
                N = self.N
                mm(ps[3][:, 0:N], onesb[:], sq[:, sl * 512: sl * 512 + N], i == 0, i == self.n - 1,
                   [("sq", sl), ("onesb",)], [("ps", 3)])
                self.pend = None

        def finish(self, Dn):
            self.flush()
            assert self.i == self.n
            N = self.N
            k = rsctr[0] % 2
            rsctr[0] += 1
            act(rs[:, k * 512: k * 512 + N], ps[3][:, 0:N], AF.Ln, [("ps", 3)], [("rs", k)], bias=epsb[:, 0:1], scale=1.0 / Dn)
            act(rs[:, k * 512: k * 512 + N], rs[:, k * 512: k * 512 + N], AF.Exp, [("rs", k)], [("rs", k)], scale=-0.5)
            return k

    def rms_stats(srcs, N, Dn):
        st = Stats(len(srcs), N)
        for ap, rl in srcs:
            st.add(ap, rl)
        return st.finish(Dn)

    def norm_to_nT(l, kind, N, st=None):
        if st is None:
            k = rms_stats([(hTc(c, 0, N), [("hT", c)]) for c in range(8)], N, D)
        else:
            k = st.finish(D)
        for c in range(8):
            gi = gidx(kind, l, c)
            stt("dve", nTc(c, 0, N), hTc(c, 0, N), gcol[:, gi:gi + 1], rs[:, k * 512: k * 512 + N], ALU.mult, ALU.mult,
                [("hT", c), ("rs", k), ("gcol",)], [("nT", c)])

    exctr = [0]
    sbctr = [0]

    def run_units(units):
        SK = 2
        n = len(units)
        for idx in range(n + SK):
            if idx < n:
                u = units[idx]
                if "call" in u:
                    u["call"]()
                else:
                    u["S"]()
            j = idx - SK
            if j >= 0:
                uj = units[j]
                if "call" not in uj:
                    uj["pv"]()
                    if uj.get("fin"):
                        uj["fin"]()
            if idx < n:
                u = units[idx]
                if "call" not in u:
                    u["post"]()

    def make_unit(kT_ap_fn, kres, q_ap_fn, qres, Nj, mcol, e_ap_fn, eres, pv_fn, fin=None):
        sset = sbctr[0] % 3
        sbctr[0] += 1
        banks = (ps[2 + 2 * sset], ps[3 + 2 * sset])
        bres = [("ps", 2 + 2 * sset), ("ps", 3 + 2 * sset)]
        s2 = exctr[0] % 3
        exctr[0] += 1

        def S():
            for hh in range(2):
                rows = slice(64 * hh, 64 * hh + 64)
                mm(banks[hh][:, 0:Nj], kT_ap_fn(rows), q_ap_fn(rows), True, True, [kres, qres], [bres[hh]],
                   tp=(64 * hh, 0))

        def post():
            exv = V(ex, s2 * 1024, [(512, 2), (1, Nj)])
            pv_ = V(pTt, s2 * 1024, [(512, 2), (1, Nj)])
            pres = [("pT", 2 * s2), ("pT", 2 * s2 + 1)]
            if isinstance(eres, tuple) and eres[0] == "EBmask":
                eoff = eres[1]
                act(pv_, V(psT[1 + sset], 0, [(512, 2), (1, Nj)]), AF.Exp, bres + [("kmask",)], pres,
                    bias=kmask[:, mcol:mcol + 1], scale=SCALE)
                if eoff == 0:
                    memset("pool", V(pTt, s2 * 1024, [(512, 2), (1, 64)], p0=64, npart=64), 0.0, pres)
                if eoff + Nj == 256:
                    memset("pool", V(pTt, s2 * 1024 + Nj - 64, [(512, 2), (1, 64)], p0=0, npart=64), 0.0, pres)
                return
            act(exv, V(psT[1 + sset], 0, [(512, 2), (1, Nj)]), AF.Exp, bres + [("kmask",)],
                [("ex", 2 * s2), ("ex", 2 * s2 + 1)], bias=kmask[:, mcol:mcol + 1], scale=SCALE)
            tt("dve", pv_, exv, e_ap_fn(), ALU.mult, [("ex", 2 * s2), ("ex", 2 * s2 + 1), eres], pres)

        def pv():
            for hh in range(2):
                sl = 2 * s2 + hh
                pv_fn(hh, pTt[:, sl * 512: sl * 512 + Nj], ("pT", sl))

        return dict(S=S, post=post, pv=pv, fin=fin)

    fctr = [0]

    def finalize_pair(obanks, obres, slot, n0, n, Rextra=()):
        k = fctr[0] % 2
        fctr[0] += 1
        for hh in range(2):
            num = slice(64 * hh, 64 * hh + 64)
            den = slice(64 * (1 - hh), 64 * (1 - hh) + 64)
            act(rc[num, k * 512 + n0: k * 512 + n0 + n], obanks[hh][den, n0:n0 + n], AF.Ln, [obres[hh]], [("rc", k)])
        rcall = rc[:, k * 512 + n0: k * 512 + n0 + n]
        act(rcall, rcall, AF.Exp, [("rc", k)], [("rc", k)], scale=-1.0)
        for hh in range(2):
            num = slice(64 * hh, 64 * hh + 64)
            tt("dve", oT(slot, num, n0, n), obanks[hh][num, n0:n0 + n], rc[num, k * 512 + n0: k * 512 + n0 + n], ALU.mult,
               [obres[hh], ("rc", k)], oTres(slot))

    def vA_lhs(l, hf, blk, p, hh):
        o = (hf * 4 + blk) * 768 + p * 192 + 64 * hh
        return vA[l][:, o:o + 128]

    def vB_lhs(l, hf, blk, g, hh):
        o = (hf * 4 + blk) * 384 + g * 192 + 64 * hh
        return vB[l][:, o:o + 128]

    def sink_mm(l, h, hh, bank, bres, n0, n):
        mm(bank[:, n0:n0 + n], sel[0:1, hh * 128: hh * 128 + 128], V(esrow, l * 8 + h, [(0, n)], 0, 1), False, True,
           [("sel",), ("esrow",)], [bres], skip=True)

    def attn_prompt(l, hc, halo_prev, halo_cur):
        hp = 1 - hc
        units = []
        for p in range(4):
            ob = (ps[0], ps[1])
            obres = [("ps", 0), ("ps", 1)]
            steps = [-1, 0, -2, 1, -3, 2, -4, 3]
            for si, j in enumerate(steps):
                hf = hp if j < 0 else hc
                blk = j + 4 if j < 0 else j
                qb0 = max(j, 0)
                qb1 = min(j + 4, 3)
                Nj = (qb1 - qb0 + 1) * 128
                qs = qb0 * 128
                eoff = (qb0 - j) * 128
                mcol = 1 if (halo_prev if j < 0 else halo_cur) else 0

                def kf(rows, l=l, hf=hf, p=p, blk=blk):
                    return kTA[l][rows, (hf * 4 + p) * 512 + blk * 128: (hf * 4 + p) * 512 + blk * 128 + 128]

                def qf(rows, p=p, qs=qs, Nj=Nj):
                    return qaT(p, rows, qs, Nj)

                def ef(p=p, eoff=eoff, Nj=Nj):
                    return V(Et, 2 * p * 640 + eoff, [(640, 2), (1, Nj)])

                def pvf(hh, pap, pres, l=l, hf=hf, blk=blk, p=p, qs=qs, Nj=Nj, si=si, ob=ob, obres=obres):
                    mm(ob[hh][:, qs:qs + Nj], vA_lhs(l, hf, blk, p, hh), pap, si == 0, True,
                       [pres, ("vA", l, hf, blk)], [obres[hh]], skip=(si != 0))

                fin = None
                if si == 7:
                    def fin(ob=ob, obres=obres, p=p):
                        finalize_pair(ob, obres, p, 0, 512)
                units.append(make_unit(kf, ("kTA", l, hf, p), qf, ("scr", p), Nj, mcol, ef, ("Et",), pvf, fin))
        for p in range(4):
            g = p // 2
            ob = (ps[0], ps[1])
            obres = [("ps", 0), ("ps", 1)]
            for bi, j in enumerate([0, 1, 2, -1, 3]):
                hf = hp if j < 0 else hc
                blk = j + 4 if j < 0 else j
                qb0 = max(j, 0)
                qb1 = min(j + 1, 3)
                Nj = (qb1 - qb0 + 1) * 128
                qs = qb0 * 128
                eoff = (qb0 - j) * 128
                mcol = 1 if (halo_prev if j < 0 else halo_cur) else 0

                def kf(rows, l=l, hf=hf, g=g, blk=blk):
                    o = (hf * 2 + g) * 512 + blk * 128
                    return kTB[l][rows, o:o + 128]

                def qf(rows, p=p, qs=qs, Nj=Nj):
                    return qbT(p, rows, qs, Nj)

                def ef(eoff=eoff, Nj=Nj):
                    return V(EB, eoff, [(0, 2), (1, Nj)])

                def pvf(hh, pap, pres, l=l, hf=hf, blk=blk, g=g, j=j, qb0=qb0, qb1=qb1, ob=ob, obres=obres, p=p, bi=bi):
                    for qb in range(qb0, qb1 + 1):
                        sub = pap[:, (qb - qb0) * 128:(qb - qb0) * 128 + 128]
                        first = (bi == 0 and qb == qb0)
                        mm(ob[hh][:, qb * 128: qb * 128 + 128], vB_lhs(l, hf, blk, g, hh), sub, first, True,
                           [pres, ("vB", l, hf, blk)], [obres[hh]], skip=not first)

                fin = None
                if bi == 4:
                    def fin(ob=ob, obres=obres, p=p, l=l):
                        for hh in range(2):
                            sink_mm(l, 2 * p + hh, hh, ob[hh], obres[hh], 0, 512)
                        finalize_pair(ob, obres, 4 + p, 0, 512)
                units.append(make_unit(kf, ("kTB", l, hf), qf, ("scr", 4 + p), Nj, mcol, ef, ("EBmask", eoff), pvf, fin))
                if p == 0 and j == 2:
                    units.append(dict(call=lambda l=l: out_norm(l, 512, 0)))
        run_units(units)

    def attn_sample(l, s):
        units = []
        n0 = s * 64
        for p in range(4):
            ob = (ps[0], ps[1])
            obres = [("ps", 0), ("ps", 1)]
            for m in range(5):
                hf = 0 if m < 4 else 1
                blk = m if m < 4 else 0
                eoff = (512 - 128 * m) if m < 4 else 64 * s
                mcol = 2 if (m == 4 and s == 1) else 0

                def kf(rows, l=l, hf=hf, p=p, blk=blk):
                    o = (hf * 4 + p) * 512 + blk * 128
                    return kTA[l][rows, o:o + 128]

                def qf(rows, p=p, n0=n0):
                    return qaT(p, rows, n0, 64)

                def ef(p=p, eoff=eoff):
                    return V(Et, 2 * p * 640 + eoff, [(640, 2), (1, 64)])

                def pvf(hh, pap, pres, l=l, hf=hf, blk=blk, p=p, m=m, ob=ob, obres=obres, n0=n0):
                    mm(ob[hh][:, n0:n0 + 64], vA_lhs(l, hf, blk, p, hh), pap, m == 0, True,
                       [pres, ("vA", l, hf, blk)], [obres[hh]], skip=(m != 0))

                fin = None
                if m == 4:
                    def fin(ob=ob, obres=obres, p=p, n0=n0):
                        finalize_pair(ob, obres, p, n0, 64)
                units.append(make_unit(kf, ("kTA", l, hf, p), qf, ("scr", p), 64, mcol, ef, ("Et",), pvf, fin))
        for p in range(4):
            g = p // 2
            ob = (ps[0], ps[1])
            obres = [("ps", 0), ("ps", 1)]
            for m in range(2):
                hf = m
                eoff = 64 if m == 0 else (0 if s == 0 else 192)

                def kf(rows, l=l, hf=hf, g=g):
                    o = (hf * 2 + g) * 512
                    return kTB[l][rows, o:o + 128]

                def qf(rows, p=p, n0=n0):
                    return qbT(p, rows, n0, 64)

                def ef(eoff=eoff):
                    return V(EB, eoff, [(0, 2), (1, 64)])

                def pvf(hh, pap, pres, l=l, hf=hf, g=g, m=m, ob=ob, obres=obres, n0=n0):
                    mm(ob[hh][:, n0:n0 + 64], vB_lhs(l, hf, 0, g, hh), pap, m == 0, True,
                       [pres, ("vB", l, hf, 0)], [obres[hh]], skip=(m != 0))

                fin = None
                if m == 1:
                    def fin(ob=ob, obres=obres, p=p, l=l, n0=n0):
                        for hh in range(2):
                            sink_mm(l, 2 * p + hh, hh, ob[hh], obres[hh], n0, 64)
                        finalize_pair(ob, obres, 4 + p, n0, 64)
                units.append(make_unit(kf, ("kTB", l, hf), qf, ("scr", 4 + p), 64, 0, ef, ("EBmask", eoff), pvf, fin))
        run_units(units)

    def evac_v(l, hf, blk, bank, bres, nvalid=128):
        o = (hf * 4 + blk) * 768
        outv = V(vA[l], o, [(192, 4), (128, 2), (1, 64)])
        inv = V(bank, 0, [(128, 4), (64, 2), (1, 64)])
        cp(evac_eng(), outv, inv, [bres], [("vA", l, hf, blk)])

    def evac_vb(l, hf, blk, src_ap_t, src_off, src_res):
        o = (hf * 4 + blk) * 384
        for dup in range(2):
            outv = V(vB[l], o + dup * 128, [(192, 2), (1, 64)])
            inv = V(src_ap_t, src_off, [(64, 2), (1, 64)])
            cp(evac_eng(), outv, inv, [src_res], [("vB", l, hf, blk)])

    def rope_block(ridx, nh, q=0):
        h0 = 10 - nh
        o = q * 640
        x1 = V(qk32, o + h0 * 64, [(64, nh), (1, 8)])
        x2 = V(qk32, o + h0 * 64 + 8, [(64, nh), (1, 8)])
        cs = V(rope, ridx * 16, [(0, nh), (1, 8)])
        sn = V(rope, ridx * 16 + 8, [(0, nh), (1, 8)])
        t = [V(rtmp, q * 320 + i * 80, [(8, nh), (1, 8)]) for i in range(4)]
        R = [("qk32", q), ("rope",)]
        tt("dve", t[0], x1, cs, ALU.mult, R, [("rtmp", q, 0)])
        tt("dve", t[1], x2, sn, ALU.mult, R, [("rtmp", q, 1)])
        tt("dve", t[2], x2, cs, ALU.mult, R, [("rtmp", q, 2)])
        tt("dve", t[3], x1, sn, ALU.mult, R, [("rtmp", q, 3)])
        tt("dve", x1, t[0], t[1], ALU.subtract, [("rtmp", q, 0), ("rtmp", q, 1)], [("qk32", q)])
        tt("dve", x2, t[2], t[3], ALU.add, [("rtmp", q, 2), ("rtmp", q, 3)], [("qk32", q)])

    def in_proj(tile, l, hc, do_q, phase=lambda n: None):
        N = tile["N"]
        NB = N // 128
        kv_out = tile["kvout"]
        issample = tile["kind"] == "S"
        if do_q:
            s = load_slab(l, "qa")
            bs = [mmbank() for _ in range(4)]
            for kc in range(8):
                for p in range(4):
                    mm(ps[bs[p]][:, 0:N], wslab(s, kc, 512, p * 128, 128), nTc(kc, 0, N), kc == 0, kc == 7,
                       [("wb", s), ("nT", kc)], [("ps", bs[p])])
            for p in range(4):
                cp(evac_eng(), qaT(p, ALLR, 0, N), ps[bs[p]][:, 0:N], [("ps", bs[p])], [("scr", p)])
        phase("ip_ka")
        s = load_slab(l, "ka")
        bs = [mmbank() for _ in range(4)]
        for kc in range(8):
            for p in range(4):
                mm(ps[bs[p]][:, 0:N], wslab(s, kc, 512, p * 128, 128), nTc(kc, 0, N), kc == 0, kc == 7,
                   [("wb", s), ("nT", kc)], [("ps", bs[p])])
        for p in range(4):
            o = (hc * 4 + p) * 512
            cp(evac_eng(), kTA[l][:, o:o + N], ps[bs[p]][:, 0:N], [("ps", bs[p])], [("kTA", l, hc, p)])
        phase("ip_kaout")
        if kv_out:
            for tb in range(NB):
                b = mmbank()
                for kc in range(8):
                    mm(ps[b][:, 0:512], nTc(kc, tb * 128, 128), wslab(s, kc, 512, 0, 512), kc == 0, kc == 7,
                       [("wb", s), ("nT", kc)], [("ps", b)])
                k = tb % 2
                cp("act", ost[:, k * 512:(k + 1) * 512], ps[b][:, 0:512], [("ps", b)], [("ost", k)])
                if issample:
                    for sm in range(2):
                        P.dma("act", naks.ap()[l, sm, 448:512, :], ost[64 * sm:64 * sm + 64, k * 512:(k + 1) * 512],
                              [("ost", k)], [("o_naks", l, sm)])
                else:
                    P.dma("act", nakp.ap()[l, tb * 128:(tb + 1) * 128, :], ost[:, k * 512:(k + 1) * 512], [("ost", k)], [("o_nakp", l, tb)])
        phase("ip_va")
        s = load_slab(l, "va")
        for tb in range(NB):
            b = mmbank()
            for kc in range(8):
                mm(ps[b][:, 0:512], nTc(kc, tb * 128, 128), wslab(s, kc, 512, 0, 512), kc == 0, kc == 7,
                   [("wb", s), ("nT", kc)], [("ps", b)])
            evac_v(l, hc, tb, ps[b], ("ps", b))
            if kv_out:
                k = tb % 2
                cp("act", ost[:, k * 512:(k + 1) * 512], ps[b][:, 0:512], [("ps", b)], [("ost", k)])
                if issample:
                    for sm in range(2):
                        P.dma("act", navs.ap()[l, sm, 448:512, :], ost[64 * sm:64 * sm + 64, k * 512:(k + 1) * 512],
                              [("ost", k)], [("o_navs", l, sm)])
                else:
                    P.dma("act", navp.ap()[l, tb * 128:(tb + 1) * 128, :], ost[:, k * 512:(k + 1) * 512], [("ost", k)], [("o_navp", l, tb)])
        phase("ip_qbkv")
        if do_q:
            sq_ = load_slab(l, "qb")
        sk = load_slab(l, "kv")
        def part_a(tb):
            q = tb % 2
            ridx = tile["row0"] // 128 + tb
            qb0 = q * 768
            rt0 = q * 320
            last_b = (tb == NB - 1)
            flagged = kv_out and (issample or last_b)
            k = tb % 2
            Cq = V(rope, ridx * 32, [(0, 8), (1, 16)])
            Sq = V(rope, ridx * 32 + 16, [(0, 8), (8, 2), (1, 8)])
            Ck = V(rope, ridx * 32, [(0, 2), (1, 16)])
            Sk = V(rope, ridx * 32 + 16, [(0, 2), (8, 2), (1, 8)])
            if do_q:
                b = mmbank()
                for kc in range(8):
                    mm(ps[b][:, 0:512], nTc(kc, tb * 128, 128), wslab(sq_, kc, 512, 0, 512), kc == 0, kc == 7,
                       [("wb", sq_), ("nT", kc)], [("ps", b)])
                tt("dve", V(rtmp, rt0, [(16, 8), (1, 16)]), V(ps[b], 0, [(64, 8), (1, 16)]), Cq, ALU.mult,
                   [("ps", b), ("rope",)], [("rtmp", q, 0)])
                tt("dve", V(rtmp, rt0 + 128, [(16, 8), (8, 2), (1, 8)]), V(ps[b], 8, [(64, 8), (-8, 2), (1, 8)]), Sq, ALU.mult,
                   [("ps", b), ("rope",)], [("rtmp", q, 1)])
                cp("act", V(qkbf, qb0 + 16, [(64, 8), (1, 48)]), V(ps[b], 16, [(64, 8), (1, 48)]), [("ps", b)], [("qkbf", q)])
                tt("dve", V(qkbf, qb0, [(64, 8), (1, 16)]), V(rtmp, rt0, [(16, 8), (1, 16)]), V(rtmp, rt0 + 128, [(16, 8), (1, 16)]),
                   ALU.add, [("rtmp", q, 0), ("rtmp", q, 1)], [("qkbf", q)])
            b2 = mmbank()
            for kc in range(8):
                mm(ps[b2][:, 0:256], nTc(kc, tb * 128, 128), wslab(sk, kc, 256, 0, 256), kc == 0, kc == 7,
                   [("wb", sk), ("nT", kc)], [("ps", b2)])
            tt("dve", V(rtmp, rt0 + 256, [(16, 2), (1, 16)]), V(ps[b2], 0, [(64, 2), (1, 16)]), Ck, ALU.mult,
               [("ps", b2), ("rope",)], [("rtmp", q, 2)])
            tt("dve", V(rtmp, rt0 + 288, [(16, 2), (8, 2), (1, 8)]), V(ps[b2], 8, [(64, 2), (-8, 2), (1, 8)]), Sk, ALU.mult,
               [("ps", b2), ("rope",)], [("rtmp", q, 3)])
            cp("act", V(qkbf, qb0 + 512 + 16, [(128, 2), (64, 2), (1, 48)]), V(ps[b2], 16, [(64, 2), (0, 2), (1, 48)]),
               [("ps", b2)], [("qkbf", q)])
            evac_vb(l, hc, tb, ps[b2], 128, ("ps", b2))
            if flagged:
                cp("act", ost[:, k * 512: k * 512 + 256], ps[b2][:, 0:256], [("ps", b2)], [("ost", k)])
            tt("dve", V(qkbf, qb0 + 512, [(128, 2), (64, 2), (1, 16)]), V(rtmp, rt0 + 256, [(16, 2), (0, 2), (1, 16)]),
               V(rtmp, rt0 + 288, [(16, 2), (0, 2), (1, 16)]), ALU.add, [("rtmp", q, 2), ("rtmp", q, 3)], [("qkbf", q)])
            if flagged:
                tt("dve", V(ost, k * 512, [(64, 2), (1, 16)]), V(rtmp, rt0 + 256, [(16, 2), (1, 16)]),
                   V(rtmp, rt0 + 288, [(16, 2), (1, 16)]), ALU.add, [("rtmp", q, 2), ("rtmp", q, 3), ("ost", k)], [("ost", k)])
                if issample:
                    for sm in range(2):
                        P.dma("act", nbks.ap()[l, sm, 64:128, :], ost[64 * sm:64 * sm + 64, k * 512: k * 512 + 128],
                              [("ost", k)], [("o_nbks", l, sm)])
                        P.dma("act", nbvs.ap()[l, sm, 64:128, :], ost[64 * sm:64 * sm + 64, k * 512 + 128: k * 512 + 256],
                              [("ost", k)], [("o_nbvs", l, sm)])
                else:
                    P.dma("act", nbkp.ap()[l], ost[:, k * 512: k * 512 + 128], [("ost", k)], [("o_nbkp", l)])
                    P.dma("act", nbvp.ap()[l], ost[:, k * 512 + 128: k * 512 + 256], [("ost", k)], [("o_nbvp", l)])

        def part_b(tb):
            q = tb % 2
            qb0 = q * 768
            j0 = 0 if do_q else 4
            for j in range(j0, 6):
                tr(ps7b[:, j * 128:(j + 1) * 128], qkbf[:, qb0 + j * 128: qb0 + (j + 1) * 128], idb[:], [("qkbf", q), ("idb",)], [("ps", 7)])
            if do_q:
                cp(evac_eng(), V(scr, 4 * 512 + tb * 128, [(512, 4), (1, 128)]), V(ps7b, 0, [(128, 4), (1, 128)]),
                   [("ps", 7)], [("scr", 4), ("scr", 5), ("scr", 6), ("scr", 7)])
            o = hc * 2 * 512 + tb * 128
            cp(evac_eng(), V(kTB[l], o, [(512, 2), (1, 128)]), V(ps7b, 512, [(128, 2), (1, 128)]), [("ps", 7)], [("kTB", l, hc)])

        for tb in range(NB):
            part_a(tb)
            if tb > 0:
                phase("ip_tr")
                part_b(tb - 1)
                phase("ip_qbkv")
        phase("ip_tr")
        part_b(NB - 1)

    def out_norm(l, N, grp):
        kind = "oa" if grp == 0 else "ob"
        k = rms_stats([(oT(grp * 4 + p, ALLR, 0, N), oTres(grp * 4 + p)) for p in range(4)], N, 512)
        for p in range(4):
            gi = gidx(kind, l, p)
            c = grp * 4 + p
            stt("dve", nTc(c, 0, N), oT(c, ALLR, 0, N), gcol[:, gi:gi + 1], rs[:, k * 512: k * 512 + N], ALU.mult, ALU.mult,
                oTres(c) + [("rs", k), ("gcol",)], [("nT", c)])

    def out_proj(l, N, do_norm=(0, 1)):
        for grp in do_norm:
            out_norm(l, N, grp)
        st = Stats(8, N)
        for half in range(2):
            s = load_slab(l, f"o{half}")
            if half == 0:
                bs = [mmbank() for _ in range(4)]
                for kc in range(8):
                    for mm_ in range(4):
                        mm(ps[bs[mm_]][:, 0:N], wslab(s, kc, 512, mm_ * 128, 128), nTc(kc, 0, N), kc == 0, kc == 7,
                           [("wb", s), ("nT", kc)], [("ps", bs[mm_])])
                for mm_ in range(4):
                    m = mm_
                    tt("dve", hTc(m, 0, N), ps[bs[mm_]][:, 0:N], hTc(m, 0, N), ALU.add, [("ps", bs[mm_]), ("hT", m)], [("hT", m)])
                    st.add(hTc(m, 0, N), [("hT", m)])
                continue
            for mm_ in range(4):
                m = half * 4 + mm_
                b = mmbank()
                for kc in range(8):
                    mm(ps[b][:, 0:N], wslab(s, kc, 512, mm_ * 128, 128), nTc(kc, 0, N), kc == 0, kc == 7,
                       [("wb", s), ("nT", kc)], [("ps", b)])
                tt("dve", hTc(m, 0, N), ps[b][:, 0:N], hTc(m, 0, N), ALU.add, [("ps", b), ("hT", m)], [("hT", m)])
                st.add(hTc(m, 0, N), [("hT", m)])
        return st

    def ffn(l, N, st=None):
        norm_to_nT(l, "ffn", N, st)
        for i in range(11):
            s = load_slab(l, f"gu{i}")
            if i == 0:
                bs = [mmbank() for _ in range(4)]
                coff = [0, 256, 128, 384]
                for kc in range(8):
                    for gi_ in range(4):
                        mm(ps[bs[gi_]][:, 0:N], wslab(s, kc, 512, coff[gi_], 128), nTc(kc, 0, N), kc == 0, kc == 7,
                           [("wb", s), ("nT", kc)], [("ps", bs[gi_])])
                for ff in range(2):
                    f = ff
                    bg, bu = bs[2 * ff], bs[2 * ff + 1]
                    k = f % 2
                    act(sgb[:, k * 512: k * 512 + N], ps[bg][:, 0:N], AF.Silu, [("ps", bg)], [("sgb", k)])
                    tt("dve", actT(f, N), sgb[:, k * 512: k * 512 + N], ps[bu][:, 0:N], ALU.mult, [("sgb", k), ("ps", bu)], [("scr", f)])
                continue
            for ff in range(2):
                f = 2 * i + ff
                bg = mmbank()
                for kc in range(8):
                    mm(ps[bg][:, 0:N], wslab(s, kc, 512, ff * 128, 128), nTc(kc, 0, N), kc == 0, kc == 7,
                       [("wb", s), ("nT", kc)], [("ps", bg)])
                bu = mmbank()
                for kc in range(8):
                    mm(ps[bu][:, 0:N], wslab(s, kc, 512, 256 + ff * 128, 128), nTc(kc, 0, N), kc == 0, kc == 7,
                       [("wb", s), ("nT", kc)], [("ps", bu)])
                k = f % 2
                act(sgb[:, k * 512: k * 512 + N], ps[bg][:, 0:N], AF.Silu, [("ps", bg)], [("sgb", k)])
                tt("dve", actT(f, N), sgb[:, k * 512: k * 512 + N], ps[bu][:, 0:N], ALU.mult, [("sgb", k), ("ps", bu)], [("scr", f)])
        for m in range(8):
            s = load_slab(l, f"dn{m}")
            b = mmbank()
            for kc in range(22):
                mm(ps[b][:, 0:N], wslab(s, kc, 128, 0, 128), actT(kc, N), kc == 0, kc == 21,
                   [("wb", s), ("scr", kc)], [("ps", b)])
            tt("dve", hTc(m, 0, N), ps[b][:, 0:N], hTc(m, 0, N), ALU.add, [("ps", b), ("hT", m)], [("hT", m)])
            cp("pool" if m % 2 else "act", nTc(m, 0, N), hTc(m, 0, N), [("hT", m)], [("nT", m)])

    plctr = [0]

    def load_p(tile, l):
        k = plctr[0] % 2
        plctr[0] += 1
        N = tile["N"]
        NB = N // 128
        r0 = tile["row0"]
        P.dma("sp", V(pst, k * 1024, [(256, NB), (1, 256)]),
              pin.ap()[l, r0:r0 + N, :].rearrange("(b p) c -> p b c", p=128), (), [("pst", k)])
        return k

    def ple(tile, l, pk):
        N = tile["N"]
        NB = N // 128
        for tb in range(NB):
            b = mmbank()
            for kc in range(2):
                o = pk * 1024 + tb * 256 + kc * 128
                tr(ps[b][:, kc * 128:(kc + 1) * 128], pst[:, o:o + 128], id32[:], [("pst", pk), ("id32",)], [("ps", b)])
            cp(evac_eng(), V(ppT, tb * 128, [(512, 2), (1, 128)]), V(ps[b], 0, [(128, 2), (1, 128)]), [("ps", b)], [("ppT",)])
        sp_ = load_slab(l, "pp")
        st = Stats(8, N)
        for half in range(2):
            s = load_slab(l, f"pg{half}")
            for mm_ in range(4):
                m = half * 4 + mm_
                bg = mmbank()
                for kc in range(8):
                    mm(ps[bg][:, 0:N], wslab(s, kc, 512, mm_ * 128, 128), nTc(kc, 0, N), kc == 0, kc == 7,
                       [("wb", s), ("nT", kc)], [("ps", bg)])
                bp = mmbank()
                for kc in range(2):
                    mm(ps[bp][:, 0:N], wslab(sp_, kc, 1024, m * 128, 128), ppT[:, kc * 512: kc * 512 + N], kc == 0, kc == 1,
                       [("wb", sp_), ("ppT",)], [("ps", bp)])
                k = m % 2
                act(sgb[:, k * 512: k * 512 + N], ps[bg][:, 0:N], AF.Sigmoid, [("ps", bg)], [("sgb", k)])
                tt("dve", rc[:, k * 512: k * 512 + N], sgb[:, k * 512: k * 512 + N], ps[bp][:, 0:N], ALU.mult,
                   [("sgb", k), ("ps", bp)], [("rc", k)])
                tt("dve", hTc(m, 0, N), hTc(m, 0, N), rc[:, k * 512: k * 512 + N], ALU.add, [("hT", m), ("rc", k)], [("hT", m)])
                st.add(hTc(m, 0, N), [("hT", m)])
        return st

    def load_x_block(tile, tb, pslot=None):
        r0 = tile["row0"] + tb * 128
        if tb < 2:
            buf, off, res = xs, tb * 1024, ("xs", tb)
        else:
            buf, off, res = pst, pslot * 1024, ("pst", pslot)
        tile.setdefault("xloc", {})[tb] = (buf, off, res)
        P.dma("sp", buf[:, off:off + 1024], xin.ap()[r0:r0 + 128, :], (), [res])

    def x_to_hT(tile, preloaded):
        N = tile["N"]
        NB = N // 128
        for tb in range(NB):
            if tb >= preloaded:
                ps_ = None
                if tb == 2:
                    ps_ = plctr[0] % 2
                elif tb == 3:
                    ps_ = (plctr[0] + 1) % 2
                load_x_block(tile, tb, ps_)
            buf, off, res = tile["xloc"][tb]
            for half in range(2):
                b = 4 + half
                for cc in range(4):
                    c = half * 4 + cc
                    tr(ps[b][:, cc * 128:(cc + 1) * 128], buf[:, off + c * 128: off + (c + 1) * 128], id32[:],
                       [res, ("id32",)], [("ps", b)])
                cp(evac_eng(), V(hT, half * 4 * 512 + tb * 128, [(512, 4), (1, 128)]), V(ps[b], 0, [(128, 4), (1, 128)]),
                   [("ps", b)], [("hT", half * 4 + cc) for cc in range(4)])

    def write_y(tile, st=None):
        N = tile["N"]
        NB = N // 128
        if st is None:
            k = rms_stats([(hTc(c, 0, N), [("hT", c)]) for c in range(8)], N, D)
        else:
            k = st.finish(D)
        for c in range(8):
            gi = gidx("fin", 0, c)
            stt("dve", hTc(c, 0, N), hTc(c, 0, N), gcol[:, gi:gi + 1], rs[:, k * 512: k * 512 + N], ALU.mult, ALU.mult,
                [("hT", c), ("rs", k), ("gcol",)], [("hT", c)])
        for tb in range(NB):
            for half in range(2):
                b = mmbank()
                for cc in range(4):
                    c = half * 4 + cc
                    tr(ps[b][:, cc * 128:(cc + 1) * 128], hTc(c, tb * 128, 128), id32[:], [("hT", c), ("id32",)], [("ps", b)])
                k2 = (tb * 2 + half) % 2
                cp("act", ost[:, k2 * 512:(k2 + 1) * 512], ps[b][:, 0:512], [("ps", b)], [("ost", k2)])
                r0 = tile["yrow0"] + tb * 128
                P.dma("act", y_d.ap()[r0:r0 + 128, half * 512:(half + 1) * 512], ost[:, k2 * 512:(k2 + 1) * 512],
                      [("ost", k2)], [("o_y", r0, half)])

    def load_E(l):
        P.dma("sp", Et[:], ebf.ap()[l], [("ebf", l)], [("Et",)])

    def load_sample_cache(l, s):
        for m in range(4):
            sl = m % 2
            P.dma("sp", xs[:, sl * 1024: sl * 1024 + 512], cak.ap()[l, s, m * 128:(m + 1) * 128, :], (), [("xs", sl)])
            P.dma("sp", xs[:, sl * 1024 + 512: sl * 1024 + 1024], cav.ap()[l, s, m * 128:(m + 1) * 128, :], (), [("xs", sl)])
            b = mmbank()
            for p in range(4):
                tr(ps[b][:, p * 128:(p + 1) * 128], xs[:, sl * 1024 + p * 128: sl * 1024 + (p + 1) * 128], id32[:],
                   [("xs", sl), ("id32",)], [("ps", b)])
            cp(evac_eng(), V(kTA[l], m * 128, [(512, 4), (1, 128)]), V(ps[b], 0, [(128, 4), (1, 128)]), [("ps", b)],
               [("kTA", l, 0, p) for p in range(4)])
            o = m * 768
            cp("pool", V(vA[l], o, [(192, 4), (128, 2), (1, 64)]), V(xs, sl * 1024 + 512, [(128, 4), (64, 2), (1, 64)]),
               [("xs", sl)], [("vA", l, 0, m)])
        P.dma("sp", xs[:, 0:128], cbk.ap()[l, s], (), [("xs", 0)])
        P.dma("sp", xs[:, 128:256], cbv.ap()[l, s], (), [("xs", 0)])
        cp("pool", V(qkbf, 512, [(128, 2), (64, 2), (1, 64)]), V(xs, 0, [(64, 2), (0, 2), (1, 64)]), [("xs", 0)], [("qkbf", 0)])
        for j in range(4, 6):
            tr(ps7b[:, j * 128:(j + 1) * 128], qkbf[:, j * 128:(j + 1) * 128], idb[:], [("qkbf", 0), ("idb",)], [("ps", 7)])
        cp(evac_eng(), V(kTB[l], 0, [(512, 2), (1, 128)]), V(ps7b, 512, [(128, 2), (1, 128)]), [("ps", 7)], [("kTB", l, 0)])
        evac_vb(l, 0, 0, xs, 128, ("xs", 0))

    tiles = []
    tiles.append(dict(kind="H0", row0=0, N=512, kvout=False))
    tiles.append(dict(kind="H1", row0=512, N=512, kvout=False))
    for i in range(8):
        tiles.append(dict(kind="own", row0=HALO + 512 * i, N=512, yrow0=512 * i, kvout=(i == 7)))
    tiles.append(dict(kind="S", row0=SROW, N=128, yrow0=OWN, kvout=True))

    import os
    DBG = int(os.environ.get("KDBG", "9999"))
    phctr = [0]

    class _Stop(Exception):
        pass

    def phase(name):
        P.label = name
        phctr[0] += 1
        if phctr[0] > DBG:
            print("KDBG stop before phase", phctr[0], name)
            raise _Stop()

    half_ctr = [0, 0]

    def main_schedule():
        for ti, tile in enumerate(tiles):
            kind = tile["kind"]
            N = tile["N"]
            pre = tile.get("preloaded", 0)
            phase("x_to_hT")
            x_to_hT(tile, pre)
            layers = [0] if kind == "H0" else [0, 1]
            nst = None
            for l in layers:
                kvonly = (kind == "H0") or (kind == "H1" and l == 1)
                if kind == "S":
                    hc = 1
                else:
                    hc = half_ctr[l] % 2
                    half_ctr[l] += 1
                if not kvonly:
                    pk = load_p(tile, l)
                    load_E(l)
                if l == 1 and kind != "S" and ti + 1 < len(tiles):
                    nt = tiles[ti + 1]
                    nb = min(3, nt["N"] // 128)
                    for tb in range(nb):
                        load_x_block(nt, tb, plctr[0] % 2)
                    nt["preloaded"] = nb
                phase("norm")
                norm_to_nT(l, "mix", N, nst)
                nst = None
                phase("in_proj")
                in_proj(tile, l, hc, not kvonly, phase)
                if kvonly:
                    continue
                phase("attn")
                if kind == "S":
                    for s in range(2):
                        load_sample_cache(l, s)
                        attn_sample(l, s)
                else:
                    halo_cur = kind in ("H0", "H1")
                    halo_prev = kind in ("H0", "H1") or (kind == "own" and tile["row0"] == HALO)
                    attn_prompt(l, hc, halo_prev, halo_cur)
                phase("out_proj")
                fst = out_proj(l, N, (0, 1) if kind == "S" else (1,))
                phase("ffn")
                ffn(l, N, fst)
                phase("ple")
                nst = ple(tile, l, pk)
            if kind in ("S", "own"):
                phase("write_y")
                write_y(tile, nst)
            if kind == "H0":
                P.label = "setup"
                build_E()

    try:
        main_schedule()
    except _Stop:
        pass

    sems = {}
    for e in Prog.ENG:
        sems[e] = es.enter_context(nc.semaphore(f"sem_{e}"))
    for q, n in P.ndma.items():
        for s in range(n):
            sems[("dma", q, s)] = es.enter_context(nc.semaphore(f"dma_{q}_{s}"))
    block = es.enter_context(nc.Block())
    P.emit(nc, block, sems)
    es.close()
    nc._prog = P
    return nc


_CACHE = {}


def _rope_tables():
    NBLK = NROW // 128
    inv = (500000.0 ** (-np.arange(0, 16, 2, dtype=np.float32) / np.float32(16))).astype(np.float32)
    tabs = []
    for c in range(NCORES):
        s0 = (c % 4) * OWN
        pos = np.concatenate([np.arange(s0 - HALO, s0 + OWN), PAST + np.arange(64), PAST + np.arange(64)]).astype(np.float32)
        ang = (pos[:, None] * inv[None, :]).astype(np.float32)
        co = np.cos(ang.astype(np.float64)).astype(np.float32)
        si = np.sin(ang.astype(np.float64)).astype(np.float32)
        t = np.concatenate([co, co, -si, si], axis=1).astype(np.float32)
        tabs.append(np.ascontiguousarray(t.reshape(NBLK, 128, 32).transpose(1, 0, 2).reshape(128, NBLK * 32)))
    return tabs


def kernel(x_prompt, x_sample, p_prompt, p_sample, cache_a_k, cache_a_v, cache_b_k, cache_b_v,
           g_mix_norm, w_in, rel_bias_a, sinks_b, g_out_a, g_out_b, w_out, g_ffn_norm,
           w_gate_up, w_down, w_ple_proj, w_ple_gate, g_final):
    if "nc" not in _CACHE:
        _CACHE["nc"] = build_program()
    nc = _CACHE["nc"]
    in_maps = prep_inputs(x_prompt, x_sample, p_prompt, p_sample, cache_a_k, cache_a_v, cache_b_k, cache_b_v,
                          g_mix_norm, w_in, rel_bias_a, sinks_b, g_out_a, g_out_b, w_out, g_ffn_norm,
                          w_gate_up, w_down, w_ple_proj, w_ple_gate, g_final)
    res = run_bass_kernel_spmd(nc, in_maps, core_ids=list(range(NCORES)))
    return gather_outputs(res.results)


def prep_inputs(x_prompt, x_sample, p_prompt, p_sample, cache_a_k, cache_a_v, cache_b_k, cache_b_v,
                g_mix_norm, w_in, rel_bias_a, sinks_b, g_out_a, g_out_b, w_out, g_ffn_norm,
                w_gate_up, w_down, w_ple_proj, w_ple_gate, g_final):
    f = lambda a: np.asarray(a, dtype=np.float32)
    x_prompt, x_sample, p_prompt, p_sample = f(x_prompt), f(x_sample), f(p_prompt), f(p_sample)
    cache_a_k, cache_a_v, cache_b_k, cache_b_v = f(cache_a_k), f(cache_a_v), f(cache_b_k), f(cache_b_v)
    wf = build_weights(f(w_in), f(w_out), f(w_gate_up), f(w_down), f(w_ple_proj), f(w_ple_gate))
    gcol = np.zeros((128, NG), np.float32)
    for l in range(NL):
        for c in range(8):
            gcol[:, gidx("mix", l, c)] = f(g_mix_norm)[l, c * 128:(c + 1) * 128]
            gcol[:, gidx("ffn", l, c)] = f(g_ffn_norm)[l, c * 128:(c + 1) * 128]
        for c in range(4):
            gcol[:, gidx("oa", l, c)] = f(g_out_a)[l, c * 128:(c + 1) * 128]
            gcol[:, gidx("ob", l, c)] = f(g_out_b)[l, c * 128:(c + 1) * 128]
    for c in range(8):
        gcol[:, gidx("fin", 0, c)] = f(g_final)[c * 128:(c + 1) * 128]
    sinkrow = np.ascontiguousarray(f(sinks_b).reshape(1, 16))
    j = np.arange(768)
    dist = np.where(j <= 640, j, j - 768)
    idx = np.clip(dist, -256, 256) + 256
    relc = np.ascontiguousarray(f(rel_bias_a)[:, :, idx].reshape(NL * 8, 768))
    ropes = _rope_tables()
    in_maps = []
    for c in range(NCORES):
        b = c // 4
        s0 = (c % 4) * OWN
        xin = np.zeros((NROW, D), np.float32)
        pin = np.zeros((NL, NROW, 256), np.float32)
        lo = s0 - HALO
        if lo >= 0:
            xin[0:HALO + OWN] = x_prompt[b, lo:s0 + OWN]
            pin[:, 0:HALO + OWN] = p_prompt[:, b, lo:s0 + OWN]
        else:
            xin[HALO:HALO + OWN] = x_prompt[b, s0:s0 + OWN]
            pin[:, HALO:HALO + OWN] = p_prompt[:, b, s0:s0 + OWN]
        xin[SROW:SROW + 64] = x_sample[2 * c]
        xin[SROW + 64:SROW + 128] = x_sample[2 * c + 1]
        pin[:, SROW:SROW + 64] = p_sample[:, 2 * c]
        pin[:, SROW + 64:SROW + 128] = p_sample[:, 2 * c + 1]
        km = np.zeros((128, 4), np.float32)
        if c % 4 == 0:
            km[:, 1] = MASKV
        km[0:64, 2] = MASKV
        in_maps.append(dict(
            xin=xin, pin=pin,
            cak=np.ascontiguousarray(cache_a_k[:, 2 * c:2 * c + 2].reshape(NL, 2, 512, 512)),
            cav=np.ascontiguousarray(cache_a_v[:, 2 * c:2 * c + 2].reshape(NL, 2, 512, 512)),
            cbk=np.ascontiguousarray(cache_b_k[:, 2 * c:2 * c + 2].reshape(NL, 2, 128, 128)),
            cbv=np.ascontiguousarray(cache_b_v[:, 2 * c:2 * c + 2].reshape(NL, 2, 128, 128)),
            wf=wf, gcol=gcol, sinkrow=sinkrow, relc=relc, rope=ropes[c], kmask=km))
    return in_maps


def gather_outputs(R):
    B, S = 2, 16384
    y_prompt = np.empty((B, S, D), np.float32)
    y_sample = np.empty((16, 64, D), np.float32)
    for c in range(NCORES):
        b = c // 4
        s0 = (c % 4) * OWN
        y_prompt[b, s0:s0 + OWN] = R[c]["y"][0:OWN]
        y_sample[2 * c] = R[c]["y"][OWN:OWN + 64]
        y_sample[2 * c + 1] = R[c]["y"][OWN + 64:OWN + 128]
    last = [3, 7]
    nak_p = np.stack([R[c]["nakp"] for c in last], axis=1).reshape(NL, B, 512, 8, 64)
    nav_p = np.stack([R[c]["navp"] for c in last], axis=1).reshape(NL, B, 512, 8, 64)
    nbk_p = np.stack([R[c]["nbkp"] for c in last], axis=1).reshape(NL, B, 128, 2, 64)
    nbv_p = np.stack([R[c]["nbvp"] for c in last], axis=1).reshape(NL, B, 128, 2, 64)
    nak_s = np.concatenate([R[c]["naks"] for c in range(NCORES)], axis=1).reshape(NL, 16, 512, 8, 64)
    nav_s = np.concatenate([R[c]["navs"] for c in range(NCORES)], axis=1).reshape(NL, 16, 512, 8, 64)
    nbk_s = np.concatenate([R[c]["nbks"] for c in range(NCORES)], axis=1).reshape(NL, 16, 128, 2, 64)
    nbv_s = np.concatenate([R[c]["nbvs"] for c in range(NCORES)], axis=1).reshape(NL, 16, 128, 2, 64)
    return (y_prompt, y_sample, nak_p, nav_p, nbk_p, nbv_p, nak_s, nav_s, nbk_s, nbv_s)
```

```python
import numpy as np
import concourse.bass as bass
import concourse.mybir as mybir
from concourse.bass_utils import run_bass_kernel_spmd

F32 = mybir.dt.float32
BF16 = mybir.dt.bfloat16
AF = mybir.ActivationFunctionType
ALU = mybir.AluOpType

NCORES = 8
D = 1024
DFF = 2816
NL = 2
TT = 512
OWN = 4096
HALO = 1024
NROW = HALO + OWN + 128
SROW = HALO + OWN
PAST = 2048
EPS = 1e-6
SCALE = 0.125
MASKV = -60.0
NWB = 3
SLOT = 4096


def _slab(W, cols):
    K = W.shape[0]
    kc = K // 128
    sub = W[:, cols]
    return np.ascontiguousarray(sub.reshape(kc, 128, len(cols)).transpose(1, 0, 2).reshape(128, kc * len(cols)))


def slab_table():
    tab = {}
    off = 0
    for l in range(NL):
        for nm, sz in ([("qa", 4096), ("ka", 4096), ("va", 4096), ("qb", 4096), ("kv", 2048), ("o0", 4096), ("o1", 4096)]
                       + [(f"gu{i}", 4096) for i in range(11)] + [(f"dn{m}", 2816) for m in range(8)]
                       + [("pg0", 4096), ("pg1", 4096), ("pp", 2048)]):
            tab[(l, nm)] = (off, sz)
            off += sz
    return tab, off


SLABS, WTOT = slab_table()


def build_weights(w_in, w_out, w_gate_up, w_down, w_ple_proj, w_ple_gate):
    out = np.empty((128, WTOT), np.float32)
    ar = np.arange
    for l in range(NL):
        parts = {
            "qa": _slab(w_in[l], ar(0, 512)), "ka": _slab(w_in[l], ar(512, 1024)), "va": _slab(w_in[l], ar(1024, 1536)),
            "qb": _slab(w_in[l], ar(1536, 2048)), "kv": _slab(w_in[l], ar(2048, 2304)),
            "o0": _slab(w_out[l], ar(0, 512)), "o1": _slab(w_out[l], ar(512, 1024)),
            "pg0": _slab(w_ple_gate[l], ar(0, 512)), "pg1": _slab(w_ple_gate[l], ar(512, 1024)),
            "pp": _slab(w_ple_proj[l], ar(0, 1024)),
        }
        for i in range(11):
            parts[f"gu{i}"] = _slab(w_gate_up[l], np.concatenate([ar(256 * i, 256 * i + 256), ar(DFF + 256 * i, DFF + 256 * i + 256)]))
        for m in range(8):
            parts[f"dn{m}"] = _slab(w_down[l], ar(128 * m, 128 * m + 128))
        for nm, a in parts.items():
            o, sz = SLABS[(l, nm)]
            assert a.shape[1] == sz
            out[:, o:o + sz] = a
    return out


def gidx(kind, l, c):
    base = {"mix": 0, "ffn": 16, "oa": 32, "ob": 40, "fin": 48}[kind]
    n = {"mix": 8, "ffn": 8, "oa": 4, "ob": 4, "fin": 8}[kind]
    return base + l * n + c


NG = 56


class Prog:
    ENG = ("pe", "act", "dve", "pool", "sp")

    def __init__(self, ndma=None):
        self.st = {e: [] for e in self.ENG}
        self.lastw = {}
        self.readers = {}
        self.seen = {e: {} for e in self.ENG}
        self.ndma = ndma or {"sp": 10, "act": 4, "pool": 4}
        self.dma_cnt = {}
        self.dma_rr = {q: 0 for q in self.ndma}
        self.milestones = {e: set() for e in self.ENG}
        self.label = "setup"
        self.labels = {e: [] for e in self.ENG}

    def _deps(self, eng, R, W):
        deps = []
        for r in R:
            t = self.lastw.get(r)
            if t is not None:
                deps.append(t)
        for w in W:
            t = self.lastw.get(w)
            if t is not None:
                deps.append(t)
            for k, v in self.readers.get(w, {}).items():
                deps.append((k, v))
        waits = []
        seen = self.seen[eng]
        best = {}
        for k, v in deps:
            if k == eng and eng == "pe":
                continue
            if seen.get(k, -1) >= v:
                continue
            if best.get(k, -1) < v:
                best[k] = v
        for k, v in best.items():
            seen[k] = v
            waits.append((k, v))
            if not isinstance(k, tuple):
                self.milestones[k].add(v)
        return waits

    def _mark(self, tok, R, W):
        k, v = tok
        for r in R:
            d = self.readers.setdefault(r, {})
            if d.get(k, -1) < v:
                d[k] = v
        for w in W:
            self.lastw[w] = tok
            self.readers[w] = {}

    def op(self, eng, fn, R=(), W=()):
        W = list(W) + [r for r in R if r[0] == "ps" and r not in W]
        waits = self._deps(eng, R, W)
        idx = len(self.st[eng])
        self.labels[eng].append(self.label)
        self.st[eng].append(("op", fn, waits, None))
        self._mark((eng, idx), R, W)

    def dma(self, q, out_ap, in_ap, R=(), W=()):
        s = self.dma_rr[q]
        self.dma_rr[q] = (s + 1) % self.ndma[q]
        key = ("dma", q, s)
        cnt = self.dma_cnt.get(key, 0)
        waits = self._deps(q, R, W)
        if cnt > 0 and self.seen[q].get(key, -1) < 16 * cnt:
            self.seen[q][key] = 16 * cnt
            waits.append((key, 16 * cnt))
        self.dma_cnt[key] = cnt + 1
        self.labels[q].append(self.label)
        self.st[q].append(("dma", (out_ap, in_ap), waits, key))
        self._mark((key, 16 * (cnt + 1)), R, W)

    def emit(self, nc, block, sems):
        rank = {}
        for e in self.ENG:
            ms = sorted(self.milestones[e])
            rank[e] = {v: i + 1 for i, v in enumerate(ms)}
        engobj = {"pe": "tensor", "act": "scalar", "dve": "vector", "pool": "gpsimd", "sp": "sync"}

        def run(e, eng):
            for idx, (kind, payload, waits, key) in enumerate(self.st[e]):
                for k, v in waits:
                    if isinstance(k, tuple):
                        eng.wait_ge(sems[k], v)
                    else:
                        eng.wait_ge(sems[k], rank[k][v])
                if kind == "op":
                    ins = payload(eng)
                    if idx in rank[e]:
                        ins.then_inc(sems[e], 1)
                else:
                    o, i = payload
                    eng.dma_start(out=o, in_=i).then_inc(sems[key], 16)
            for key, cnt in self.dma_cnt.items():
                if key[1] == e:
                    eng.wait_ge(sems[key], 16 * cnt)

        for e in self.ENG:
            deco = getattr(block, engobj[e])

            def body(eng, e=e):
                run(e, eng)
            deco(body)


def build_program():
    nc = bass.Bass("TRN2", target_bir_lowering=False)
    dt_in = lambda n, s: nc.dram_tensor(n, s, F32, kind="ExternalInput")
    dt_out = lambda n, s: nc.dram_tensor(n, s, F32, kind="ExternalOutput")
    xin = dt_in("xin", [NROW, D])
    pin = dt_in("pin", [NL, NROW, 256])
    cak = dt_in("cak", [NL, 2, 512, 512])
    cav = dt_in("cav", [NL, 2, 512, 512])
    cbk = dt_in("cbk", [NL, 2, 128, 128])
    cbv = dt_in("cbv", [NL, 2, 128, 128])
    wf = dt_in("wf", [128, WTOT])
    gcol_d = dt_in("gcol", [128, NG])
    sink_d = dt_in("sinkrow", [1, 16])
    relc = dt_in("relc", [NL * 8, 768])
    NBLK = NROW // 128
    rope_d = dt_in("rope", [128, NBLK * 32])
    kmask_d = dt_in("kmask", [128, 4])

    wbf = nc.dram_tensor("wbf", [128, WTOT], BF16, kind="Internal")
    srel = nc.dram_tensor("srel", [NL * 8, 128 * 768], F32, kind="Internal")
    ebf = nc.dram_tensor("ebf", [NL, 128, 5120], BF16, kind="Internal")

    y_d = dt_out("y", [OWN + 128, D])
    nakp = dt_out("nakp", [NL, 512, 512])
    navp = dt_out("navp", [NL, 512, 512])
    nbkp = dt_out("nbkp", [NL, 128, 128])
    nbvp = dt_out("nbvp", [NL, 128, 128])
    naks = dt_out("naks", [NL, 2, 512, 512])
    navs = dt_out("navs", [NL, 2, 512, 512])
    nbks = dt_out("nbks", [NL, 2, 128, 128])
    nbvs = dt_out("nbvs", [NL, 2, 128, 128])

    from contextlib import ExitStack
    es = ExitStack()

    def sb(name, F, dt):
        return es.enter_context(nc.sbuf_tensor("sb_" + name, [128, F], dt))

    hT = sb("hT", 8 * 512, F32)
    nT = sb("nT", 8 * 512, BF16)
    sq = sb("sq", 2 * 512, BF16)
    xs = sb("xs", 2 * 1024, F32)
    ost = sb("ost", 2 * 512, F32)
    scr = sb("scr", 12288, BF16)
    scr32 = scr.bitcast(F32)
    kTA = [sb(f"kTA{l}", 2 * 4 * 512, BF16) for l in range(NL)]
    vA = [sb(f"vA{l}", 2 * 4 * 768, BF16) for l in range(NL)]
    kTB = [sb(f"kTB{l}", 2 * 2 * 512, BF16) for l in range(NL)]
    vB = [sb(f"vB{l}", 2 * 4 * 384, BF16) for l in range(NL)]
    qk32 = sb("qk32", 16, F32)
    rtmp = sb("rtmp", 2 * 4 * 80, F32)
    qkbf = sb("qkbf", 2 * 768, BF16)
    ex = sb("ex", 6 * 512, BF16)
    pTt = sb("pTt", 6 * 512, BF16)
    Et = sb("Et", 8 * 640, BF16)
    EB = sb("EB", 256, BF16)
    wb = sb("wb", NWB * SLOT, BF16)
    pst = sb("pst", 2 * 1024, F32)
    ppT = sb("ppT", 2 * 512, BF16)
    rs = sb("rs", 2 * 512, F32)
    rc = sb("rc", 2 * 512, F32)
    sgb = sb("sgb", 2 * 512, F32)
    id32 = sb("id32", 128, F32)
    idb = sb("idb", 128, BF16)
    onesb = sb("onesb", 128, BF16)
    gcol = sb("gcol", NG, F32)
    sinkf = sb("sinkf", 16, F32)
    esrow = sb("esrow", 16, BF16)
    sel = sb("sel", 256, BF16)
    rope = sb("rope", NBLK * 32, F32)
    kmask = sb("kmask", 4, F32)
    epsb = sb("epsb", 1, F32)
    psT = [es.enter_context(nc.psum_tensor(f"ps{i}", [128, 1024], F32)) for i in range(4)]
    ps = [psT[i // 2][:, (i % 2) * 512:(i % 2 + 1) * 512] for i in range(8)]
    ps7b = psT[3].bitcast(BF16)[:, 1024:2048]

    P = Prog()

    def V(t, off, dims, p0=0, npart=128):
        if isinstance(t, bass.AP):
            base = t.offset
            t = t.tensor
        else:
            base = 0
        Fd = t.shape[1]
        return bass.AP(t, base + p0 * Fd + off, [[Fd, npart]] + [[s, n] for s, n in dims])

    def mm(out, lhsT, rhs, start, stop, R, W, tp=None, skip=False):
        if skip:
            P.op("pe", lambda e: e.matmul(out, lhsT=lhsT, rhs=rhs, start=start, stop=stop, skip_group_check=True), R, W)
        elif tp is None:
            P.op("pe", lambda e: e.matmul(out, lhsT=lhsT, rhs=rhs, start=start, stop=stop), R, W)
        else:
            P.op("pe", lambda e: e.matmul(out, lhsT=lhsT, rhs=rhs, start=start, stop=stop, tile_position=tp), R, W)

    def tr(out, in_, ident, R, W):
        P.op("pe", lambda e: e.transpose(out, in_, ident), R, W)

    def act(out, in_, func, R, W, bias=None, scale=None):
        kw = {}
        if bias is not None:
            kw["bias"] = bias
        if scale is not None:
            kw["scale"] = scale
        P.op("act", lambda e: e.activation(out=out, in_=in_, func=func, **kw), R, W)

    def cp(eng, out, in_, R, W):
        if eng == "act":
            P.op("act", lambda e: e.activation(out=out, in_=in_, func=AF.Copy), R, W)
        else:
            P.op(eng, lambda e: e.tensor_copy(out=out, in_=in_), R, W)

    def tt(eng, out, in0, in1, op, R, W):
        P.op(eng, lambda e: e.tensor_tensor(out=out, in0=in0, in1=in1, op=op), R, W)

    def stt(eng, out, in0, scalar, in1, op0, op1, R, W):
        P.op(eng, lambda e: e.scalar_tensor_tensor(out=out, in0=in0, scalar=scalar, in1=in1, op0=op0, op1=op1), R, W)

    def recip(out, in_, R, W):
        P.op("dve", lambda e: e.reciprocal(out=out, in_=in_), R, W)

    def memset(eng, ap, val, W):
        P.op(eng, lambda e: e.memset(ap, val), (), W)

    def hTc(c, n0, n):
        return hT[:, c * 512 + n0: c * 512 + n0 + n]

    def nTc(c, n0, n):
        return nT[:, c * 512 + n0: c * 512 + n0 + n]

    def qaT(p, rows, n0, n):
        return scr[rows, p * 512 + n0: p * 512 + n0 + n]

    def qbT(p, rows, n0, n):
        return scr[rows, (4 + p) * 512 + n0: (4 + p) * 512 + n0 + n]

    def oT(s, rows, n0, n):
        return scr32[rows, 2048 + s * 512 + n0: 2048 + s * 512 + n0 + n]

    def oTres(s):
        return [("scr", 8 + 2 * s), ("scr", 9 + 2 * s)]

    def actT(f, n):
        return scr[:, f * 512: f * 512 + n]

    ALLR = slice(0, 128)
    psctr = [0]

    MMB = (0, 1, 2, 4, 5, 6)

    def mmbank():
        b = MMB[psctr[0] % len(MMB)]
        psctr[0] += 1
        return b

    evctr = [0]

    def evac_eng():
        evctr[0] += 1
        return "act" if evctr[0] % 2 else "dve"

    P.op("pool", lambda e: e.iota(id32[:], [[1, 128]], base=0, channel_multiplier=-1, allow_small_or_imprecise_dtypes=True), (), [("id32",)])
    P.op("pool", lambda e: e.tensor_single_scalar(out=id32[:], in_=id32[:], scalar=0.0, op=ALU.is_equal), [("id32",)], [("id32",)])
    cp("pool", idb[:], id32[:], [("id32",)], [("idb",)])
    memset("pool", onesb[:], 1.0, [("onesb",)])
    memset("pool", epsb[:], EPS, [("epsb",)])
    memset("pool", sel[:], 0.0, [("sel",)])
    memset("pool", sel[0:1, 64:128], 1.0, [("sel",)])
    memset("pool", sel[0:1, 128:192], 1.0, [("sel",)])
    memset("pool", EB[:], 1.0, [("EB",)])
    memset("pool", EB[64:128, 0:64], 0.0, [("EB",)])
    memset("pool", EB[0:64, 192:256], 0.0, [("EB",)])
    for l in range(NL):
        for hf in range(2):
            for blk in range(4):
                for p in range(4):
                    o = (hf * 4 + blk) * 768 + p * 192 + 64
                    memset("pool", vA[l][:, o:o + 64], 1.0, [("vA", l, hf, blk)])
                for g in range(2):
                    o = (hf * 4 + blk) * 384 + g * 192 + 64
                    memset("pool", vB[l][:, o:o + 64], 1.0, [("vB", l, hf, blk)])
    P.dma("sp", gcol[:], gcol_d.ap(), (), [("gcol",)])
    P.dma("sp", rope[:], rope_d.ap(), (), [("rope",)])
    P.dma("sp", kmask[:], kmask_d.ap(), (), [("kmask",)])
    P.dma("sp", sinkf[0:1, :], sink_d.ap(), (), [("sinkf",)])
    act(esrow[0:1, :], sinkf[0:1, :], AF.Exp, [("sinkf",)], [("esrow",)])
    for l in range(NL):
        for s in range(2):
            P.dma("sp", naks.ap()[l, s, 0:448, :], cak.ap()[l, s, 64:512, :], (), [("o_naks", l, s)])
            P.dma("sp", navs.ap()[l, s, 0:448, :], cav.ap()[l, s, 64:512, :], (), [("o_navs", l, s)])
            P.dma("sp", nbks.ap()[l, s, 0:64, :], cbk.ap()[l, s, 64:128, :], (), [("o_nbks", l, s)])
            P.dma("sp", nbvs.ap()[l, s, 0:64, :], cbv.ap()[l, s, 64:128, :], (), [("o_nbvs", l, s)])
    def build_E():
        for i in range(NL * 8):
            P.dma("sp", srel.ap()[i:i + 1, :].rearrange("a (r c) -> (a r) c", c=768),
                  bass.AP(relc, i * 768, [[0, 128], [1, 768]]), (), [("srel", i)])
        for l in range(NL):
            for h in range(8):
                i = l * 8 + h
                sl = i % 2
                P.dma("sp", xs[:, sl * 1024: sl * 1024 + 640], bass.AP(srel, i * 128 * 768, [[767, 128], [1, 640]]),
                      [("srel", i)], [("xs", sl)])
                act(Et[:, h * 640:(h + 1) * 640], xs[:, sl * 1024: sl * 1024 + 640], AF.Exp, [("xs", sl)], [("Et",)])
                memset("pool", Et[64:128, h * 640: h * 640 + 64], 0.0, [("Et",)])
                memset("pool", Et[0:64, h * 640 + 576: h * 640 + 640], 0.0, [("Et",)])
            P.dma("sp", ebf.ap()[l], Et[:], [("Et",)], [("ebf", l)])

    first_order = ["ka", "va", "kv", "qa", "qb", "o0", "o1"] + [f"gu{i}" for i in range(11)] + [f"dn{m}" for m in range(8)] \
        + ["pp", "pg0", "pg1"]
    cast_queue = [(l, nm) for l in range(NL) for nm in first_order]
    cast_done = set()
    CAST_AHEAD = 6

    def issue_casts(n):
        for _ in range(n):
            if not cast_queue:
                return
            l, nm = cast_queue.pop(0)
            if (l, nm) in cast_done:
                continue
            o, sz = SLABS[(l, nm)]
            P.dma("pool", wbf.ap()[:, o:o + sz], wf.ap()[:, o:o + sz], (), [("wbf", l, nm)])
            cast_done.add((l, nm))

    def ensure_cast(l, nm):
        if (l, nm) not in cast_done:
            cast_queue.remove((l, nm))
            o, sz = SLABS[(l, nm)]
            P.dma("pool", wbf.ap()[:, o:o + sz], wf.ap()[:, o:o + sz], (), [("wbf", l, nm)])
            cast_done.add((l, nm))

    issue_casts(CAST_AHEAD)

    wslot = [0]

    def load_slab(l, nm):
        ensure_cast(l, nm)
        issue_casts(1)
        o, sz = SLABS[(l, nm)]
        s = wslot[0] % NWB
        wslot[0] += 1
        P.dma("sp", wb[:, s * SLOT: s * SLOT + sz], wbf.ap()[:, o:o + sz], [("wbf", l, nm)], [("wb", s)])
        return s

    def wslab(s, kc, ncols, c0, n):
        o = s * SLOT + kc * ncols + c0
        return wb[:, o:o + n]

    rsctr = [0]

    class Stats:
        def __init__(self, nsrc, N):
            self.n = nsrc
            self.N = N
            self.i = 0
            self.pend = None

        def add(self, ap, rl):
            i = self.i
            self.i += 1
            sl = i % 2
            N = self.N
            tt("pool", sq[:, sl * 512: sl * 512 + N], ap, ap, ALU.mult, rl, [("sq", sl)])
            self.flush()
            self.pend = (i, sl)

        def flush(self):
            if self.pend is not None:
                i, sl = self.pend
                N = self.N
                mm(ps[3][:, 0:N], onesb[:], sq[:, sl * 512: sl * 512 + N], i == 0, i == self.n - 1,
                   [("sq", sl), ("onesb",)], [("ps", 3)])
                self.pend = None

        def finish(self, Dn):
            self.flush()
            assert self.i == self.n
            N = self.N
            k = rsctr[0] % 2
            rsctr[0] += 1
            act(rs[:, k * 512: k * 512 + N], ps[3][:, 0:N], AF.Ln, [("ps", 3)], [("rs", k)], bias=epsb[:, 0:1], scale=1.0 / Dn)
            act(rs[:, k * 512: k * 512 + N], rs[:, k * 512: k * 512 + N], AF.Exp, [("rs", k)], [("rs", k)], scale=-0.5)
            return k

    def rms_stats(srcs, N, Dn):
        st = Stats(len(srcs), N)
        for ap, rl in srcs:
            st.add(ap, rl)
        return st.finish(Dn)

    def norm_to_nT(l, kind, N, st=None):
        if st is None:
            k = rms_stats([(hTc(c, 0, N), [("hT", c)]) for c in range(8)], N, D)
        else:
            k = st.finish(D)
        for c in range(8):
            gi = gidx(kind, l, c)
            stt("dve", nTc(c, 0, N), hTc(c, 0, N), gcol[:, gi:gi + 1], rs[:, k * 512: k * 512 + N], ALU.mult, ALU.mult,
                [("hT", c), ("rs", k), ("gcol",)], [("nT", c)])

    exctr = [0]
    sbctr = [0]

    def run_units(units):
        SK = 2
        n = len(units)
        for idx in range(n + SK):
            if idx < n:
                u = units[idx]
                if "call" in u:
                    u["call"]()
                else:
                    u["S"]()
            j = idx - SK
            if j >= 0:
                uj = units[j]
                if "call" not in uj:
                    uj["pv"]()
                    if uj.get("fin"):
                        uj["fin"]()
            if idx < n:
                u = units[idx]
                if "call" not in u:
                    u["post"]()

    def make_unit(kT_ap_fn, kres, q_ap_fn, qres, Nj, mcol, e_ap_fn, eres, pv_fn, fin=None):
        sset = sbctr[0] % 3
        sbctr[0] += 1
        banks = (ps[2 + 2 * sset], ps[3 + 2 * sset])
        bres = [("ps", 2 + 2 * sset), ("ps", 3 + 2 * sset)]
        s2 = exctr[0] % 3
        exctr[0] += 1

        def S():
            for hh in range(2):
                rows = slice(64 * hh, 64 * hh + 64)
                mm(banks[hh][:, 0:Nj], kT_ap_fn(rows), q_ap_fn(rows), True, True, [kres, qres], [bres[hh]],
                   tp=(64 * hh, 0))

        def post():
            exv = V(ex, s2 * 1024, [(512, 2), (1, Nj)])
            pv_ = V(pTt, s2 * 1024, [(512, 2), (1, Nj)])
            pres = [("pT", 2 * s2), ("pT", 2 * s2 + 1)]
            if isinstance(eres, tuple) and eres[0] == "EBmask":
                eoff = eres[1]
                act(pv_, V(psT[1 + sset], 0, [(512, 2), (1, Nj)]), AF.Exp, bres + [("kmask",)], pres,
                    bias=kmask[:, mcol:mcol + 1], scale=SCALE)
                if eoff == 0:
                    memset("pool", V(pTt, s2 * 1024, [(512, 2), (1, 64)], p0=64, npart=64), 0.0, pres)
                if eoff + Nj == 256:
                    memset("pool", V(pTt, s2 * 1024 + Nj - 64, [(512, 2), (1, 64)], p0=0, npart=64), 0.0, pres)
                return
            act(exv, V(psT[1 + sset], 0, [(512, 2), (1, Nj)]), AF.Exp, bres + [("kmask",)],
                [("ex", 2 * s2), ("ex", 2 * s2 + 1)], bias=kmask[:, mcol:mcol + 1], scale=SCALE)
            tt("dve", pv_, exv, e_ap_fn(), ALU.mult, [("ex", 2 * s2), ("ex", 2 * s2 + 1), eres], pres)

        def pv():
            for hh in range(2):
                sl = 2 * s2 + hh
                pv_fn(hh, pTt[:, sl * 512: sl * 512 + Nj], ("pT", sl))

        return dict(S=S, post=post, pv=pv, fin=fin)

    fctr = [0]

    def finalize_pair(obanks, obres, slot, n0, n, Rextra=()):
        k = fctr[0] % 2
        fctr[0] += 1
        for hh in range(2):
            num = slice(64 * hh, 64 * hh + 64)
            den = slice(64 * (1 - hh), 64 * (1 - hh) + 64)
            act(rc[num, k * 512 + n0: k * 512 + n0 + n], obanks[hh][den, n0:n0 + n], AF.Ln, [obres[hh]], [("rc", k)])
        rcall = rc[:, k * 512 + n0: k * 512 + n0 + n]
        act(rcall, rcall, AF.Exp, [("rc", k)], [("rc", k)], scale=-1.0)
        for hh in range(2):
            num = slice(64 * hh, 64 * hh + 64)
            tt("dve", oT(slot, num, n0, n), obanks[hh][num, n0:n0 + n], rc[num, k * 512 + n0: k * 512 + n0 + n], ALU.mult,
               [obres[hh], ("rc", k)], oTres(slot))

    def vA_lhs(l, hf, blk, p, hh):
        o = (hf * 4 + blk) * 768 + p * 192 + 64 * hh
        return vA[l][:, o:o + 128]

    def vB_lhs(l, hf, blk, g, hh):
        o = (hf * 4 + blk) * 384 + g * 192 + 64 * hh
        return vB[l][:, o:o + 128]

    def sink_mm(l, h, hh, bank, bres, n0, n):
        mm(bank[:, n0:n0 + n], sel[0:1, hh * 128: hh * 128 + 128], V(esrow, l * 8 + h, [(0, n)], 0, 1), False, True,
           [("sel",), ("esrow",)], [bres], skip=True)

    def attn_prompt(l, hc, halo_prev, halo_cur):
        hp = 1 - hc
        units = []
        for p in range(4):
            ob = (ps[0], ps[1])
            obres = [("ps", 0), ("ps", 1)]
            steps = [-1, 0, -2, 1, -3, 2, -4, 3]
            for si, j in enumerate(steps):
                hf = hp if j < 0 else hc
                blk = j + 4 if j < 0 else j
                qb0 = max(j, 0)
                qb1 = min(j + 4, 3)
                Nj = (qb1 - qb0 + 1) * 128
                qs = qb0 * 128
                eoff = (qb0 - j) * 128
                mcol = 1 if (halo_prev if j < 0 else halo_cur) else 0

                def kf(rows, l=l, hf=hf, p=p, blk=blk):
                    return kTA[l][rows, (hf * 4 + p) * 512 + blk * 128: (hf * 4 + p) * 512 + blk * 128 + 128]

                def qf(rows, p=p, qs=qs, Nj=Nj):
                    return qaT(p, rows, qs, Nj)

                def ef(p=p, eoff=eoff, Nj=Nj):
                    return V(Et, 2 * p * 640 + eoff, [(640, 2), (1, Nj)])

                def pvf(hh, pap, pres, l=l, hf=hf, blk=blk, p=p, qs=qs, Nj=Nj, si=si, ob=ob, obres=obres):
                    mm(ob[hh][:, qs:qs + Nj], vA_lhs(l, hf, blk, p, hh), pap, si == 0, True,
                       [pres, ("vA", l, hf, blk)], [obres[hh]], skip=(si != 0))

                fin = None
                if si == 7:
                    def fin(ob=ob, obres=obres, p=p):
                        finalize_pair(ob, obres, p, 0, 512)
                units.append(make_unit(kf, ("kTA", l, hf, p), qf, ("scr", p), Nj, mcol, ef, ("Et",), pvf, fin))
        for p in range(4):
            g = p // 2
            ob = (ps[0], ps[1])
            obres = [("ps", 0), ("ps", 1)]
            for bi, j in enumerate([0, 1, 2, -1, 3]):
                hf = hp if j < 0 else hc
                blk = j + 4 if j < 0 else j
                qb0 = max(j, 0)
                qb1 = min(j + 1, 3)
                Nj = (qb1 - qb0 + 1) * 128
                qs = qb0 * 128
                eoff = (qb0 - j) * 128
                mcol = 1 if (halo_prev if j < 0 else halo_cur) else 0

                def kf(rows, l=l, hf=hf, g=g, blk=blk):
                    o = (hf * 2 + g) * 512 + blk * 128
                    return kTB[l][rows, o:o + 128]

                def qf(rows, p=p, qs=qs, Nj=Nj):
                    return qbT(p, rows, qs, Nj)

                def ef(eoff=eoff, Nj=Nj):
                    return V(EB, eoff, [(0, 2), (1, Nj)])

                def pvf(hh, pap, pres, l=l, hf=hf, blk=blk, g=g, j=j, qb0=qb0, qb1=qb1, ob=ob, obres=obres, p=p, bi=bi):
                    first = (bi == 0)
                    nq = (qb1 - qb0 + 1) * 128
                    mm(ob[hh][:, qb0 * 128: qb0 * 128 + nq], vB_lhs(l, hf, blk, g, hh), pap[:, 0:nq], first, True,
                       [pres, ("vB", l, hf, blk)], [obres[hh]], skip=not first)

                fin = None
                if bi == 4:
                    def fin(ob=ob, obres=obres, p=p, l=l):
                        for hh in range(2):
                            sink_mm(l, 2 * p + hh, hh, ob[hh], obres[hh], 0, 512)
                        finalize_pair(ob, obres, 4 + p, 0, 512)
                units.append(make_unit(kf, ("kTB", l, hf), qf, ("scr", 4 + p), Nj, mcol, ef, ("EBmask", eoff), pvf, fin))
                if p == 0 and j == 2:
                    units.append(dict(call=lambda l=l: out_norm(l, 512, 0)))
        run_units(units)

    def attn_sample(l, s):
        units = []
        n0 = s * 64
        for p in range(4):
            ob = (ps[0], ps[1])
            obres = [("ps", 0), ("ps", 1)]
            for m in range(5):
                hf = 0 if m < 4 else 1
                blk = m if m < 4 else 0
                eoff = (512 - 128 * m) if m < 4 else 64 * s
                mcol = 2 if (m == 4 and s == 1) else 0

                def kf(rows, l=l, hf=hf, p=p, blk=blk):
                    o = (hf * 4 + p) * 512 + blk * 128
                    return kTA[l][rows, o:o + 128]

                def qf(rows, p=p, n0=n0):
                    return qaT(p, rows, n0, 64)

                def ef(p=p, eoff=eoff):
                    return V(Et, 2 * p * 640 + eoff, [(640, 2), (1, 64)])

                def pvf(hh, pap, pres, l=l, hf=hf, blk=blk, p=p, m=m, ob=ob, obres=obres, n0=n0):
                    mm(ob[hh][:, n0:n0 + 64], vA_lhs(l, hf, blk, p, hh), pap, m == 0, True,
                       [pres, ("vA", l, hf, blk)], [obres[hh]], skip=(m != 0))

                fin = None
                if m == 4:
                    def fin(ob=ob, obres=obres, p=p, n0=n0):
                        finalize_pair(ob, obres, p, n0, 64)
                units.append(make_unit(kf, ("kTA", l, hf, p), qf, ("scr", p), 64, mcol, ef, ("Et",), pvf, fin))
        for p in range(4):
            g = p // 2
            ob = (ps[0], ps[1])
            obres = [("ps", 0), ("ps", 1)]
            for m in range(2):
                hf = m
                eoff = 64 if m == 0 else (0 if s == 0 else 192)

                def kf(rows, l=l, hf=hf, g=g):
                    o = (hf * 2 + g) * 512
                    return kTB[l][rows, o:o + 128]

                def qf(rows, p=p, n0=n0):
                    return qbT(p, rows, n0, 64)

                def ef(eoff=eoff):
                    return V(EB, eoff, [(0, 2), (1, 64)])

                def pvf(hh, pap, pres, l=l, hf=hf, g=g, m=m, ob=ob, obres=obres, n0=n0):
                    mm(ob[hh][:, n0:n0 + 64], vB_lhs(l, hf, 0, g, hh), pap, m == 0, True,
                       [pres, ("vB", l, hf, 0)], [obres[hh]], skip=(m != 0))

                fin = None
                if m == 1:
                    def fin(ob=ob, obres=obres, p=p, l=l, n0=n0):
                        for hh in range(2):
                            sink_mm(l, 2 * p + hh, hh, ob[hh], obres[hh], n0, 64)
                        finalize_pair(ob, obres, 4 + p, n0, 64)
                units.append(make_unit(kf, ("kTB", l, hf), qf, ("scr", 4 + p), 64, 0, ef, ("EBmask", eoff), pvf, fin))
        run_units(units)

    def evac_v(l, hf, blk, bank, bres, nvalid=128):
        o = (hf * 4 + blk) * 768
        outv = V(vA[l], o, [(192, 4), (128, 2), (1, 64)])
        inv = V(bank, 0, [(128, 4), (64, 2), (1, 64)])
        cp(evac_eng(), outv, inv, [bres], [("vA", l, hf, blk)])

    def evac_vb(l, hf, blk, src_ap_t, src_off, src_res):
        o = (hf * 4 + blk) * 384
        for dup in range(2):
            outv = V(vB[l], o + dup * 128, [(192, 2), (1, 64)])
            inv = V(src_ap_t, src_off, [(64, 2), (1, 64)])
            cp(evac_eng(), outv, inv, [src_res], [("vB", l, hf, blk)])

    def rope_block(ridx, nh, q=0):
        h0 = 10 - nh
        o = q * 640
        x1 = V(qk32, o + h0 * 64, [(64, nh), (1, 8)])
        x2 = V(qk32, o + h0 * 64 + 8, [(64, nh), (1, 8)])
        cs = V(rope, ridx * 16, [(0, nh), (1, 8)])
        sn = V(rope, ridx * 16 + 8, [(0, nh), (1, 8)])
        t = [V(rtmp, q * 320 + i * 80, [(8, nh), (1, 8)]) for i in range(4)]
        R = [("qk32", q), ("rope",)]
        tt("dve", t[0], x1, cs, ALU.mult, R, [("rtmp", q, 0)])
        tt("dve", t[1], x2, sn, ALU.mult, R, [("rtmp", q, 1)])
        tt("dve", t[2], x2, cs, ALU.mult, R, [("rtmp", q, 2)])
        tt("dve", t[3], x1, sn, ALU.mult, R, [("rtmp", q, 3)])
        tt("dve", x1, t[0], t[1], ALU.subtract, [("rtmp", q, 0), ("rtmp", q, 1)], [("qk32", q)])
        tt("dve", x2, t[2], t[3], ALU.add, [("rtmp", q, 2), ("rtmp", q, 3)], [("qk32", q)])

    def in_proj(tile, l, hc, do_q, phase=lambda n: None):
        N = tile["N"]
        NB = N // 128
        kv_out = tile["kvout"]
        issample = tile["kind"] == "S"
        if do_q:
            s = load_slab(l, "qa")
            bs = [mmbank() for _ in range(4)]
            for kc in range(8):
                for p in range(4):
                    mm(ps[bs[p]][:, 0:N], wslab(s, kc, 512, p * 128, 128), nTc(kc, 0, N), kc == 0, kc == 7,
                       [("wb", s), ("nT", kc)], [("ps", bs[p])])
            for p in range(4):
                cp(evac_eng(), qaT(p, ALLR, 0, N), ps[bs[p]][:, 0:N], [("ps", bs[p])], [("scr", p)])
        phase("ip_ka")
        s = load_slab(l, "ka")
        bs = [mmbank() for _ in range(4)]
        for kc in range(8):
            for p in range(4):
                mm(ps[bs[p]][:, 0:N], wslab(s, kc, 512, p * 128, 128), nTc(kc, 0, N), kc == 0, kc == 7,
                   [("wb", s), ("nT", kc)], [("ps", bs[p])])
        for p in range(4):
            o = (hc * 4 + p) * 512
            cp(evac_eng(), kTA[l][:, o:o + N], ps[bs[p]][:, 0:N], [("ps", bs[p])], [("kTA", l, hc, p)])
        phase("ip_kaout")
        if kv_out:
            for tb in range(NB):
                b = mmbank()
                for kc in range(8):
                    mm(ps[b][:, 0:512], nTc(kc, tb * 128, 128), wslab(s, kc, 512, 0, 512), kc == 0, kc == 7,
                       [("wb", s), ("nT", kc)], [("ps", b)])
                k = tb % 2
                cp("act", ost[:, k * 512:(k + 1) * 512], ps[b][:, 0:512], [("ps", b)], [("ost", k)])
                if issample:
                    for sm in range(2):
                        P.dma("act", naks.ap()[l, sm, 448:512, :], ost[64 * sm:64 * sm + 64, k * 512:(k + 1) * 512],
                              [("ost", k)], [("o_naks", l, sm)])
                else:
                    P.dma("act", nakp.ap()[l, tb * 128:(tb + 1) * 128, :], ost[:, k * 512:(k + 1) * 512], [("ost", k)], [("o_nakp", l, tb)])
        phase("ip_va")
        s = load_slab(l, "va")
        for tb in range(NB):
            b = mmbank()
            for kc in range(8):
                mm(ps[b][:, 0:512], nTc(kc, tb * 128, 128), wslab(s, kc, 512, 0, 512), kc == 0, kc == 7,
                   [("wb", s), ("nT", kc)], [("ps", b)])
            evac_v(l, hc, tb, ps[b], ("ps", b))
            if kv_out:
                k = tb % 2
                cp("act", ost[:, k * 512:(k + 1) * 512], ps[b][:, 0:512], [("ps", b)], [("ost", k)])
                if issample:
                    for sm in range(2):
                        P.dma("act", navs.ap()[l, sm, 448:512, :], ost[64 * sm:64 * sm + 64, k * 512:(k + 1) * 512],
                              [("ost", k)], [("o_navs", l, sm)])
                else:
                    P.dma("act", navp.ap()[l, tb * 128:(tb + 1) * 128, :], ost[:, k * 512:(k + 1) * 512], [("ost", k)], [("o_navp", l, tb)])
        phase("ip_qbkv")
        if do_q:
            sq_ = load_slab(l, "qb")
        sk = load_slab(l, "kv")
        def part_a(tb):
            q = tb % 2
            ridx = tile["row0"] // 128 + tb
            qb0 = q * 768
            rt0 = q * 320
            last_b = (tb == NB - 1)
            flagged = kv_out and (issample or last_b)
            k = tb % 2
            Cq = V(rope, ridx * 32, [(0, 8), (1, 16)])
            Sq = V(rope, ridx * 32 + 16, [(0, 8), (8, 2), (1, 8)])
            Ck = V(rope, ridx * 32, [(0, 2), (1, 16)])
            Sk = V(rope, ridx * 32 + 16, [(0, 2), (8, 2), (1, 8)])
            if do_q:
                b = mmbank()
                for kc in range(8):
                    mm(ps[b][:, 0:512], nTc(kc, tb * 128, 128), wslab(sq_, kc, 512, 0, 512), kc == 0, kc == 7,
                       [("wb", sq_), ("nT", kc)], [("ps", b)])
                tt("dve", V(rtmp, rt0, [(16, 8), (1, 16)]), V(ps[b], 0, [(64, 8), (1, 16)]), Cq, ALU.mult,
                   [("ps", b), ("rope",)], [("rtmp", q, 0)])
                tt("dve", V(rtmp, rt0 + 128, [(16, 8), (8, 2), (1, 8)]), V(ps[b], 8, [(64, 8), (-8, 2), (1, 8)]), Sq, ALU.mult,
                   [("ps", b), ("rope",)], [("rtmp", q, 1)])
                cp("act", V(qkbf, qb0 + 16, [(64, 8), (1, 48)]), V(ps[b], 16, [(64, 8), (1, 48)]), [("ps", b)], [("qkbf", q)])
                tt("dve", V(qkbf, qb0, [(64, 8), (1, 16)]), V(rtmp, rt0, [(16, 8), (1, 16)]), V(rtmp, rt0 + 128, [(16, 8), (1, 16)]),
                   ALU.add, [("rtmp", q, 0), ("rtmp", q, 1)], [("qkbf", q)])
            b2 = mmbank()
            for kc in range(8):
                mm(ps[b2][:, 0:256], nTc(kc, tb * 128, 128), wslab(sk, kc, 256, 0, 256), kc == 0, kc == 7,
                   [("wb", sk), ("nT", kc)], [("ps", b2)])
            tt("dve", V(rtmp, rt0 + 256, [(16, 2), (1, 16)]), V(ps[b2], 0, [(64, 2), (1, 16)]), Ck, ALU.mult,
               [("ps", b2), ("rope",)], [("rtmp", q, 2)])
            tt("dve", V(rtmp, rt0 + 288, [(16, 2), (8, 2), (1, 8)]), V(ps[b2], 8, [(64, 2), (-8, 2), (1, 8)]), Sk, ALU.mult,
               [("ps", b2), ("rope",)], [("rtmp", q, 3)])
            cp("act", V(qkbf, qb0 + 512 + 16, [(128, 2), (64, 2), (1, 48)]), V(ps[b2], 16, [(64, 2), (0, 2), (1, 48)]),
               [("ps", b2)], [("qkbf", q)])
            evac_vb(l, hc, tb, ps[b2], 128, ("ps", b2))
            if flagged:
                cp("act", ost[:, k * 512: k * 512 + 256], ps[b2][:, 0:256], [("ps", b2)], [("ost", k)])
            tt("dve", V(qkbf, qb0 + 512, [(128, 2), (64, 2), (1, 16)]), V(rtmp, rt0 + 256, [(16, 2), (0, 2), (1, 16)]),
               V(rtmp, rt0 + 288, [(16, 2), (0, 2), (1, 16)]), ALU.add, [("rtmp", q, 2), ("rtmp", q, 3)], [("qkbf", q)])
            if flagged:
                tt("dve", V(ost, k * 512, [(64, 2), (1, 16)]), V(rtmp, rt0 + 256, [(16, 2), (1, 16)]),
                   V(rtmp, rt0 + 288, [(16, 2), (1, 16)]), ALU.add, [("rtmp", q, 2), ("rtmp", q, 3), ("ost", k)], [("ost", k)])
                if issample:
                    for sm in range(2):
                        P.dma("act", nbks.ap()[l, sm, 64:128, :], ost[64 * sm:64 * sm + 64, k * 512: k * 512 + 128],
                              [("ost", k)], [("o_nbks", l, sm)])
                        P.dma("act", nbvs.ap()[l, sm, 64:128, :], ost[64 * sm:64 * sm + 64, k * 512 + 128: k * 512 + 256],
                              [("ost", k)], [("o_nbvs", l, sm)])
                else:
                    P.dma("act", nbkp.ap()[l], ost[:, k * 512: k * 512 + 128], [("ost", k)], [("o_nbkp", l)])
                    P.dma("act", nbvp.ap()[l], ost[:, k * 512 + 128: k * 512 + 256], [("ost", k)], [("o_nbvp", l)])

        def part_b(tb):
            q = tb % 2
            qb0 = q * 768
            j0 = 0 if do_q else 4
            for j in range(j0, 6):
                tr(ps7b[:, j * 128:(j + 1) * 128], qkbf[:, qb0 + j * 128: qb0 + (j + 1) * 128], idb[:], [("qkbf", q), ("idb",)], [("ps", 7)])
            if do_q:
                cp(evac_eng(), V(scr, 4 * 512 + tb * 128, [(512, 4), (1, 128)]), V(ps7b, 0, [(128, 4), (1, 128)]),
                   [("ps", 7)], [("scr", 4), ("scr", 5), ("scr", 6), ("scr", 7)])
            o = hc * 2 * 512 + tb * 128
            cp(evac_eng(), V(kTB[l], o, [(512, 2), (1, 128)]), V(ps7b, 512, [(128, 2), (1, 128)]), [("ps", 7)], [("kTB", l, hc)])

        for tb in range(NB):
            part_a(tb)
            if tb > 0:
                phase("ip_tr")
                part_b(tb - 1)
                phase("ip_qbkv")
        phase("ip_tr")
        part_b(NB - 1)

    def out_norm(l, N, grp):
        kind = "oa" if grp == 0 else "ob"
        k = rms_stats([(oT(grp * 4 + p, ALLR, 0, N), oTres(grp * 4 + p)) for p in range(4)], N, 512)
        for p in range(4):
            gi = gidx(kind, l, p)
            c = grp * 4 + p
            stt("dve", nTc(c, 0, N), oT(c, ALLR, 0, N), gcol[:, gi:gi + 1], rs[:, k * 512: k * 512 + N], ALU.mult, ALU.mult,
                oTres(c) + [("rs", k), ("gcol",)], [("nT", c)])

    def out_proj(l, N, do_norm=(0, 1)):
        for grp in do_norm:
            out_norm(l, N, grp)
        st = Stats(8, N)
        for half in range(2):
            s = load_slab(l, f"o{half}")
            if half == 0:
                bs = [mmbank() for _ in range(4)]
                for kc in range(8):
                    for mm_ in range(4):
                        mm(ps[bs[mm_]][:, 0:N], wslab(s, kc, 512, mm_ * 128, 128), nTc(kc, 0, N), kc == 0, kc == 7,
                           [("wb", s), ("nT", kc)], [("ps", bs[mm_])])
                for mm_ in range(4):
                    m = mm_
                    tt("dve", hTc(m, 0, N), ps[bs[mm_]][:, 0:N], hTc(m, 0, N), ALU.add, [("ps", bs[mm_]), ("hT", m)], [("hT", m)])
                    st.add(hTc(m, 0, N), [("hT", m)])
                continue
            for mm_ in range(4):
                m = half * 4 + mm_
                b = mmbank()
                for kc in range(8):
                    mm(ps[b][:, 0:N], wslab(s, kc, 512, mm_ * 128, 128), nTc(kc, 0, N), kc == 0, kc == 7,
                       [("wb", s), ("nT", kc)], [("ps", b)])
                tt("dve", hTc(m, 0, N), ps[b][:, 0:N], hTc(m, 0, N), ALU.add, [("ps", b), ("hT", m)], [("hT", m)])
                st.add(hTc(m, 0, N), [("hT", m)])
        return st

    def ffn(l, N, st=None):
        norm_to_nT(l, "ffn", N, st)
        for i in range(11):
            s = load_slab(l, f"gu{i}")
            if i == 0:
                bs = [mmbank() for _ in range(4)]
                coff = [0, 256, 128, 384]
                for kc in range(8):
                    for gi_ in range(4):
                        mm(ps[bs[gi_]][:, 0:N], wslab(s, kc, 512, coff[gi_], 128), nTc(kc, 0, N), kc == 0, kc == 7,
                           [("wb", s), ("nT", kc)], [("ps", bs[gi_])])
                for ff in range(2):
                    f = ff
                    bg, bu = bs[2 * ff], bs[2 * ff + 1]
                    k = f % 2
                    act(sgb[:, k * 512: k * 512 + N], ps[bg][:, 0:N], AF.Silu, [("ps", bg)], [("sgb", k)])
                    tt("dve", actT(f, N), sgb[:, k * 512: k * 512 + N], ps[bu][:, 0:N], ALU.mult, [("sgb", k), ("ps", bu)], [("scr", f)])
                continue
            for ff in range(2):
                f = 2 * i + ff
                bg = mmbank()
                for kc in range(8):
                    mm(ps[bg][:, 0:N], wslab(s, kc, 512, ff * 128, 128), nTc(kc, 0, N), kc == 0, kc == 7,
                       [("wb", s), ("nT", kc)], [("ps", bg)])
                bu = mmbank()
                for kc in range(8):
                    mm(ps[bu][:, 0:N], wslab(s, kc, 512, 256 + ff * 128, 128), nTc(kc, 0, N), kc == 0, kc == 7,
                       [("wb", s), ("nT", kc)], [("ps", bu)])
                k = f % 2
                act(sgb[:, k * 512: k * 512 + N], ps[bg][:, 0:N], AF.Silu, [("ps", bg)], [("sgb", k)])
                tt("dve", actT(f, N), sgb[:, k * 512: k * 512 + N], ps[bu][:, 0:N], ALU.mult, [("sgb", k), ("ps", bu)], [("scr", f)])
        for m in range(8):
            s = load_slab(l, f"dn{m}")
            b = mmbank()
            for kc in range(22):
                mm(ps[b][:, 0:N], wslab(s, kc, 128, 0, 128), actT(kc, N), kc == 0, kc == 21,
                   [("wb", s), ("scr", kc)], [("ps", b)])
            tt("dve", hTc(m, 0, N), ps[b][:, 0:N], hTc(m, 0, N), ALU.add, [("ps", b), ("hT", m)], [("hT", m)])
            cp("pool" if m % 2 else "act", nTc(m, 0, N), hTc(m, 0, N), [("hT", m)], [("nT", m)])

    plctr = [0]

    def load_p(tile, l):
        k = plctr[0] % 2
        plctr[0] += 1
        N = tile["N"]
        NB = N // 128
        r0 = tile["row0"]
        P.dma("sp", V(pst, k * 1024, [(256, NB), (1, 256)]),
              pin.ap()[l, r0:r0 + N, :].rearrange("(b p) c -> p b c", p=128), (), [("pst", k)])
        return k

    def ple(tile, l, pk):
        N = tile["N"]
        NB = N // 128
        for tb in range(NB):
            b = mmbank()
            for kc in range(2):
                o = pk * 1024 + tb * 256 + kc * 128
                tr(ps[b][:, kc * 128:(kc + 1) * 128], pst[:, o:o + 128], id32[:], [("pst", pk), ("id32",)], [("ps", b)])
            cp(evac_eng(), V(ppT, tb * 128, [(512, 2), (1, 128)]), V(ps[b], 0, [(128, 2), (1, 128)]), [("ps", b)], [("ppT",)])
        sp_ = load_slab(l, "pp")
        st = Stats(8, N)
        for half in range(2):
            s = load_slab(l, f"pg{half}")
            for mm_ in range(4):
                m = half * 4 + mm_
                bg = mmbank()
                for kc in range(8):
                    mm(ps[bg][:, 0:N], wslab(s, kc, 512, mm_ * 128, 128), nTc(kc, 0, N), kc == 0, kc == 7,
                       [("wb", s), ("nT", kc)], [("ps", bg)])
                bp = mmbank()
                for kc in range(2):
                    mm(ps[bp][:, 0:N], wslab(sp_, kc, 1024, m * 128, 128), ppT[:, kc * 512: kc * 512 + N], kc == 0, kc == 1,
                       [("wb", sp_), ("ppT",)], [("ps", bp)])
                k = m % 2
                act(sgb[:, k * 512: k * 512 + N], ps[bg][:, 0:N], AF.Sigmoid, [("ps", bg)], [("sgb", k)])
                tt("dve", rc[:, k * 512: k * 512 + N], sgb[:, k * 512: k * 512 + N], ps[bp][:, 0:N], ALU.mult,
                   [("sgb", k), ("ps", bp)], [("rc", k)])
                tt("dve", hTc(m, 0, N), hTc(m, 0, N), rc[:, k * 512: k * 512 + N], ALU.add, [("hT", m), ("rc", k)], [("hT", m)])
                st.add(hTc(m, 0, N), [("hT", m)])
        return st

    def load_x_block(tile, tb, pslot=None):
        r0 = tile["row0"] + tb * 128
        if tb < 2:
            buf, off, res = xs, tb * 1024, ("xs", tb)
        else:
            buf, off, res = pst, pslot * 1024, ("pst", pslot)
        tile.setdefault("xloc", {})[tb] = (buf, off, res)
        P.dma("sp", buf[:, off:off + 1024], xin.ap()[r0:r0 + 128, :], (), [res])

    def x_to_hT(tile, preloaded):
        N = tile["N"]
        NB = N // 128
        for tb in range(NB):
            if tb >= preloaded:
                ps_ = None
                if tb == 2:
                    ps_ = plctr[0] % 2
                elif tb == 3:
                    ps_ = (plctr[0] + 1) % 2
                load_x_block(tile, tb, ps_)
            buf, off, res = tile["xloc"][tb]
            for half in range(2):
                b = 4 + half
                for cc in range(4):
                    c = half * 4 + cc
                    tr(ps[b][:, cc * 128:(cc + 1) * 128], buf[:, off + c * 128: off + (c + 1) * 128], id32[:],
                       [res, ("id32",)], [("ps", b)])
                cp(evac_eng(), V(hT, half * 4 * 512 + tb * 128, [(512, 4), (1, 128)]), V(ps[b], 0, [(128, 4), (1, 128)]),
                   [("ps", b)], [("hT", half * 4 + cc) for cc in range(4)])

    def write_y(tile, st=None):
        N = tile["N"]
        NB = N // 128
        if st is None:
            k = rms_stats([(hTc(c, 0, N), [("hT", c)]) for c in range(8)], N, D)
        else:
            k = st.finish(D)
        for c in range(8):
            gi = gidx("fin", 0, c)
            stt("dve", hTc(c, 0, N), hTc(c, 0, N), gcol[:, gi:gi + 1], rs[:, k * 512: k * 512 + N], ALU.mult, ALU.mult,
                [("hT", c), ("rs", k), ("gcol",)], [("hT", c)])
        for tb in range(NB):
            for half in range(2):
                b = mmbank()
                for cc in range(4):
                    c = half * 4 + cc
                    tr(ps[b][:, cc * 128:(cc + 1) * 128], hTc(c, tb * 128, 128), id32[:], [("hT", c), ("id32",)], [("ps", b)])
                k2 = (tb * 2 + half) % 2
                cp("act", ost[:, k2 * 512:(k2 + 1) * 512], ps[b][:, 0:512], [("ps", b)], [("ost", k2)])
                r0 = tile["yrow0"] + tb * 128
                P.dma("act", y_d.ap()[r0:r0 + 128, half * 512:(half + 1) * 512], ost[:, k2 * 512:(k2 + 1) * 512],
                      [("ost", k2)], [("o_y", r0, half)])

    def load_E(l):
        P.dma("sp", Et[:], ebf.ap()[l], [("ebf", l)], [("Et",)])

    def load_sample_cache(l, s):
        for m in range(4):
            sl = m % 2
            P.dma("sp", xs[:, sl * 1024: sl * 1024 + 512], cak.ap()[l, s, m * 128:(m + 1) * 128, :], (), [("xs", sl)])
            P.dma("sp", xs[:, sl * 1024 + 512: sl * 1024 + 1024], cav.ap()[l, s, m * 128:(m + 1) * 128, :], (), [("xs", sl)])
            b = mmbank()
            for p in range(4):
                tr(ps[b][:, p * 128:(p + 1) * 128], xs[:, sl * 1024 + p * 128: sl * 1024 + (p + 1) * 128], id32[:],
                   [("xs", sl), ("id32",)], [("ps", b)])
            cp(evac_eng(), V(kTA[l], m * 128, [(512, 4), (1, 128)]), V(ps[b], 0, [(128, 4), (1, 128)]), [("ps", b)],
               [("kTA", l, 0, p) for p in range(4)])
            o = m * 768
            cp("pool", V(vA[l], o, [(192, 4), (128, 2), (1, 64)]), V(xs, sl * 1024 + 512, [(128, 4), (64, 2), (1, 64)]),
               [("xs", sl)], [("vA", l, 0, m)])
        P.dma("sp", xs[:, 0:128], cbk.ap()[l, s], (), [("xs", 0)])
        P.dma("sp", xs[:, 128:256], cbv.ap()[l, s], (), [("xs", 0)])
        cp("pool", V(qkbf, 512, [(128, 2), (64, 2), (1, 64)]), V(xs, 0, [(64, 2), (0, 2), (1, 64)]), [("xs", 0)], [("qkbf", 0)])
        for j in range(4, 6):
            tr(ps7b[:, j * 128:(j + 1) * 128], qkbf[:, j * 128:(j + 1) * 128], idb[:], [("qkbf", 0), ("idb",)], [("ps", 7)])
        cp(evac_eng(), V(kTB[l], 0, [(512, 2), (1, 128)]), V(ps7b, 512, [(128, 2), (1, 128)]), [("ps", 7)], [("kTB", l, 0)])
        evac_vb(l, 0, 0, xs, 128, ("xs", 0))

    tiles = []
    tiles.append(dict(kind="H0", row0=0, N=512, kvout=False))
    tiles.append(dict(kind="H1", row0=512, N=512, kvout=False))
    for i in range(8):
        tiles.append(dict(kind="own", row0=HALO + 512 * i, N=512, yrow0=512 * i, kvout=(i == 7)))
    tiles.append(dict(kind="S", row0=SROW, N=128, yrow0=OWN, kvout=True))

    import os
    DBG = int(os.environ.get("KDBG", "9999"))
    phctr = [0]

    class _Stop(Exception):
        pass

    def phase(name):
        P.label = name
        phctr[0] += 1
        if phctr[0] > DBG:
            print("KDBG stop before phase", phctr[0], name)
            raise _Stop()

    half_ctr = [0, 0]

    def main_schedule():
        for ti, tile in enumerate(tiles):
            kind = tile["kind"]
            N = tile["N"]
            pre = tile.get("preloaded", 0)
            phase("x_to_hT")
            x_to_hT(tile, pre)
            layers = [0] if kind == "H0" else [0, 1]
            nst = None
            for l in layers:
                kvonly = (kind == "H0") or (kind == "H1" and l == 1)
                if kind == "S":
                    hc = 1
                else:
                    hc = half_ctr[l] % 2
                    half_ctr[l] += 1
                if not kvonly:
                    pk = load_p(tile, l)
                    load_E(l)
                if l == 1 and kind != "S" and ti + 1 < len(tiles):
                    nt = tiles[ti + 1]
                    nb = min(3, nt["N"] // 128)
                    for tb in range(nb):
                        load_x_block(nt, tb, plctr[0] % 2)
                    nt["preloaded"] = nb
                phase("norm")
                norm_to_nT(l, "mix", N, nst)
                nst = None
                phase("in_proj")
                in_proj(tile, l, hc, not kvonly, phase)
                if kvonly:
                    continue
                phase("attn")
                if kind == "S":
                    for s in range(2):
                        load_sample_cache(l, s)
                        attn_sample(l, s)
                else:
                    halo_cur = kind in ("H0", "H1")
                    halo_prev = kind in ("H0", "H1") or (kind == "own" and tile["row0"] == HALO)
                    attn_prompt(l, hc, halo_prev, halo_cur)
                phase("out_proj")
                fst = out_proj(l, N, (0, 1) if kind == "S" else (1,))
                phase("ffn")
                ffn(l, N, fst)
                phase("ple")
                nst = ple(tile, l, pk)
            if kind in ("S", "own"):
                phase("write_y")
                write_y(tile, nst)
            if kind == "H0":
                P.label = "setup"
                build_E()

    try:
        main_schedule()
    except _Stop:
        pass

    sems = {}
    for e in Prog.ENG:
        sems[e] = es.enter_context(nc.semaphore(f"sem_{e}"))
    for q, n in P.ndma.items():
        for s in range(n):
            sems[("dma", q, s)] = es.enter_context(nc.semaphore(f"dma_{q}_{s}"))
    block = es.enter_context(nc.Block())
    P.emit(nc, block, sems)
    es.close()
    nc._prog = P
    return nc


_CACHE = {}


def _rope_tables():
    NBLK = NROW // 128
    inv = (500000.0 ** (-np.arange(0, 16, 2, dtype=np.float32) / np.float32(16))).astype(np.float32)
    tabs = []
    for c in range(NCORES):
        s0 = (c % 4) * OWN
        pos = np.concatenate([np.arange(s0 - HALO, s0 + OWN), PAST + np.arange(64), PAST + np.arange(64)]).astype(np.float32)
        ang = (pos[:, None] * inv[None, :]).astype(np.float32)
        co = np.cos(ang.astype(np.float64)).astype(np.float32)
        si = np.sin(ang.astype(np.float64)).astype(np.float32)
        t = np.concatenate([co, co, -si, si], axis=1).astype(np.float32)
        tabs.append(np.ascontiguousarray(t.reshape(NBLK, 128, 32).transpose(1, 0, 2).reshape(128, NBLK * 32)))
    return tabs


def kernel(x_prompt, x_sample, p_prompt, p_sample, cache_a_k, cache_a_v, cache_b_k, cache_b_v,
           g_mix_norm, w_in, rel_bias_a, sinks_b, g_out_a, g_out_b, w_out, g_ffn_norm,
           w_gate_up, w_down, w_ple_proj, w_ple_gate, g_final):
    if "nc" not in _CACHE:
        _CACHE["nc"] = build_program()
    nc = _CACHE["nc"]
    in_maps = prep_inputs(x_prompt, x_sample, p_prompt, p_sample, cache_a_k, cache_a_v, cache_b_k, cache_b_v,
                          g_mix_norm, w_in, rel_bias_a, sinks_b, g_out_a, g_out_b, w_out, g_ffn_norm,
                          w_gate_up, w_down, w_ple_proj, w_ple_gate, g_final)
    res = run_bass_kernel_spmd(nc, in_maps, core_ids=list(range(NCORES)))
    return gather_outputs(res.results)


def prep_inputs(x_prompt, x_sample, p_prompt, p_sample, cache_a_k, cache_a_v, cache_b_k, cache_b_v,
                g_mix_norm, w_in, rel_bias_a, sinks_b, g_out_a, g_out_b, w_out, g_ffn_norm,
                w_gate_up, w_down, w_ple_proj, w_ple_gate, g_final):
    f = lambda a: np.asarray(a, dtype=np.float32)
    x_prompt, x_sample, p_prompt, p_sample = f(x_prompt), f(x_sample), f(p_prompt), f(p_sample)
    cache_a_k, cache_a_v, cache_b_k, cache_b_v = f(cache_a_k), f(cache_a_v), f(cache_b_k), f(cache_b_v)
    wf = build_weights(f(w_in), f(w_out), f(w_gate_up), f(w_down), f(w_ple_proj), f(w_ple_gate))
    gcol = np.zeros((128, NG), np.float32)
    for l in range(NL):
        for c in range(8):
            gcol[:, gidx("mix", l, c)] = f(g_mix_norm)[l, c * 128:(c + 1) * 128]
            gcol[:, gidx("ffn", l, c)] = f(g_ffn_norm)[l, c * 128:(c + 1) * 128]
        for c in range(4):
            gcol[:, gidx("oa", l, c)] = f(g_out_a)[l, c * 128:(c + 1) * 128]
            gcol[:, gidx("ob", l, c)] = f(g_out_b)[l, c * 128:(c + 1) * 128]
    for c in range(8):
        gcol[:, gidx("fin", 0, c)] = f(g_final)[c * 128:(c + 1) * 128]
    sinkrow = np.ascontiguousarray(f(sinks_b).reshape(1, 16))
    j = np.arange(768)
    dist = np.where(j <= 640, j, j - 768)
    idx = np.clip(dist, -256, 256) + 256
    relc = np.ascontiguousarray(f(rel_bias_a)[:, :, idx].reshape(NL * 8, 768))
    ropes = _rope_tables()
    in_maps = []
    for c in range(NCORES):
        b = c // 4
        s0 = (c % 4) * OWN
        xin = np.zeros((NROW, D), np.float32)
        pin = np.zeros((NL, NROW, 256), np.float32)
        lo = s0 - HALO
        if lo >= 0:
            xin[0:HALO + OWN] = x_prompt[b, lo:s0 + OWN]
            pin[:, 0:HALO + OWN] = p_prompt[:, b, lo:s0 + OWN]
        else:
            xin[HALO:HALO + OWN] = x_prompt[b, s0:s0 + OWN]
            pin[:, HALO:HALO + OWN] = p_prompt[:, b, s0:s0 + OWN]
        xin[SROW:SROW + 64] = x_sample[2 * c]
        xin[SROW + 64:SROW + 128] = x_sample[2 * c + 1]
        pin[:, SROW:SROW + 64] = p_sample[:, 2 * c]
        pin[:, SROW + 64:SROW + 128] = p_sample[:, 2 * c + 1]
        km = np.zeros((128, 4), np.float32)
        if c % 4 == 0:
            km[:, 1] = MASKV
        km[0:64, 2] = MASKV
        in_maps.append(dict(
            xin=xin, pin=pin,
            cak=np.ascontiguousarray(cache_a_k[:, 2 * c:2 * c + 2].reshape(NL, 2, 512, 512)),
            cav=np.ascontiguousarray(cache_a_v[:, 2 * c:2 * c + 2].reshape(NL, 2, 512, 512)),
            cbk=np.ascontiguousarray(cache_b_k[:, 2 * c:2 * c + 2].reshape(NL, 2, 128, 128)),
            cbv=np.ascontiguousarray(cache_b_v[:, 2 * c:2 * c + 2].reshape(NL, 2, 128, 128)),
            wf=wf, gcol=gcol, sinkrow=sinkrow, relc=relc, rope=ropes[c], kmask=km))
    return in_maps


def gather_outputs(R):
    B, S = 2, 16384
    y_prompt = np.empty((B, S, D), np.float32)
    y_sample = np.empty((16, 64, D), np.float32)
    for c in range(NCORES):
        b = c // 4
        s0 = (c % 4) * OWN
        y_prompt[b, s0:s0 + OWN] = R[c]["y"][0:OWN]
        y_sample[2 * c] = R[c]["y"][OWN:OWN + 64]
        y_sample[2 * c + 1] = R[c]["y"][OWN + 64:OWN + 128]
    last = [3, 7]
    nak_p = np.stack([R[c]["nakp"] for c in last], axis=1).reshape(NL, B, 512, 8, 64)
    nav_p = np.stack([R[c]["navp"] for c in last], axis=1).reshape(NL, B, 512, 8, 64)
    nbk_p = np.stack([R[c]["nbkp"] for c in last], axis=1).reshape(NL, B, 128, 2, 64)
    nbv_p = np.stack([R[c]["nbvp"] for c in last], axis=1).reshape(NL, B, 128, 2, 64)
    nak_s = np.concatenate([R[c]["naks"] for c in range(NCORES)], axis=1).reshape(NL, 16, 512, 8, 64)
    nav_s = np.concatenate([R[c]["navs"] for c in range(NCORES)], axis=1).reshape(NL, 16, 512, 8, 64)
    nbk_s = np.concatenate([R[c]["nbks"] for c in range(NCORES)], axis=1).reshape(NL, 16, 128, 2, 64)
    nbv_s = np.concatenate([R[c]["nbvs"] for c in range(NCORES)], axis=1).reshape(NL, 16, 128, 2, 64)
    return (y_prompt, y_sample, nak_p, nav_p, nbk_p, nbv_p, nak_s, nav_s, nbk_s, nbv_s)
```
